# Optimizing a Trainium2 kernel written in Bass

```python
import math
import jax, jax.numpy as jnp
from jax import lax
import numpy as np

D_MODEL = 1024
BATCH = 2
SEQ = 16384
DEPTH = 2

CHUNK = 64
N_MIXERS = 2
EXPAND = 2
D_INNER = EXPAND * D_MODEL
SC_WIDTH = 3
SSD_HEAD_DIM = 64
SSD_HEADS = D_INNER // SSD_HEAD_DIM
SSD_GROUPS = 8
SSD_HPG = SSD_HEADS // SSD_GROUPS
SSD_STATE = 128
SSD_CONV = 4
SSD_XBC = D_INNER + 2 * SSD_GROUPS * SSD_STATE
SSD_IN = D_INNER + SSD_XBC + SSD_HEADS
N_SC_LAYERS = (DEPTH + N_MIXERS - 1) // N_MIXERS
N_SSD_LAYERS = DEPTH // N_MIXERS
EPS = 1e-6

kernel_name = "hybrid_shortconv_ssd_sandwich"


def rms_norm(x, g):
    xf = x.astype(jnp.float32)
    y = xf * lax.rsqrt(jnp.mean(xf * xf, axis=-1, keepdims=True) + EPS)
    return (y * g.astype(jnp.float32)).astype(x.dtype)


def causal_dwconv(u, w):
    width, ch = w.shape
    return lax.conv_general_dilated(
        u, w[:, None, :].astype(u.dtype), window_strides=(1,),
        padding=[(width - 1, 0)], dimension_numbers=("NWC", "WIO", "NWC"),
        feature_group_count=ch)


def short_conv_mixer(u, w_in, conv_w, w_out):
    z, bg, cg, v = jnp.split(u @ w_in.astype(u.dtype), 4, axis=-1)
    y = bg * causal_dwconv(cg * v, conv_w) * jax.nn.silu(z)
    return y @ w_out.astype(u.dtype)


def ssd_chunked(x, dt, a, bm, cm):
    bsz, seqlen = x.shape[0], x.shape[1]
    nc = seqlen // CHUNK
    x = x.astype(jnp.float32).reshape(bsz, nc, CHUNK, SSD_GROUPS, SSD_HPG, SSD_HEAD_DIM)
    dt = dt.reshape(bsz, nc, CHUNK, SSD_GROUPS, SSD_HPG)
    a = a.reshape(SSD_GROUPS, SSD_HPG)
    bm = bm.astype(jnp.float32).reshape(bsz, nc, CHUNK, SSD_GROUPS, SSD_STATE)
    cm = cm.astype(jnp.float32).reshape(bsz, nc, CHUNK, SSD_GROUPS, SSD_STATE)

    a_cum = jnp.cumsum(dt * a, axis=2)
    xdt = x * dt[..., None]

    ac = jnp.moveaxis(a_cum, 2, -1)
    seg = ac[..., :, None] - ac[..., None, :]
    mask = jnp.tril(jnp.ones((CHUNK, CHUNK), dtype=bool))
    decay = jnp.exp(jnp.where(mask, seg, -jnp.inf))
    cb = jnp.einsum("bclgn,bcsgn->bcgls", cm, bm)
    y_diag = jnp.einsum("bcgkls,bcsgkp->bclgkp", cb[:, :, :, None] * decay, xdt)

    decay_states = jnp.exp(a_cum[:, :, -1:] - a_cum)
    states = jnp.einsum("bclgn,bclgk,bclgkp->bcgkpn", bm, decay_states, xdt)
    chunk_decay = jnp.exp(a_cum[:, :, -1])

    def step(h, inp):
        s, d = inp
        return h * d[..., None, None] + s, h

    h0 = jnp.zeros((bsz, SSD_GROUPS, SSD_HPG, SSD_HEAD_DIM, SSD_STATE), jnp.float32)
    _, h_prev = lax.scan(step, h0, (jnp.moveaxis(states, 1, 0), jnp.moveaxis(chunk_decay, 1, 0)))
    h_prev = jnp.moveaxis(h_prev, 0, 1)

    y_off = jnp.einsum("bclgn,bcgkpn,bclgk->bclgkp", cm, h_prev, jnp.exp(a_cum))
    return (y_diag + y_off).reshape(bsz, seqlen, SSD_HEADS, SSD_HEAD_DIM)


def ssd_mixer(u, w_in, conv_w, conv_b, dt_bias, a_log, d_skip, norm_w, w_out):
    bsz, seqlen, _ = u.shape
    proj = u @ w_in.astype(u.dtype)
    z = proj[..., :D_INNER]
    xbc = proj[..., D_INNER:D_INNER + SSD_XBC]
    dt_raw = proj[..., D_INNER + SSD_XBC:]
    xbc = jax.nn.silu(causal_dwconv(xbc, conv_w) + conv_b.astype(u.dtype))
    gn = SSD_GROUPS * SSD_STATE
    xs = xbc[..., :D_INNER]
    bm = xbc[..., D_INNER:D_INNER + gn]
    cm = xbc[..., D_INNER + gn:]
    dt = jax.nn.softplus(dt_raw.astype(jnp.float32) + dt_bias.astype(jnp.float32))
    a = -jnp.exp(a_log.astype(jnp.float32))
    y = ssd_chunked(xs, dt, a, bm, cm)
    y = y + d_skip.astype(jnp.float32)[:, None] * xs.astype(jnp.float32).reshape(
        bsz, seqlen, SSD_HEADS, SSD_HEAD_DIM)
    yg = (y.reshape(bsz, seqlen, D_INNER) * jax.nn.silu(z.astype(jnp.float32)))
    yg = yg.reshape(bsz, seqlen, SSD_GROUPS, D_INNER // SSD_GROUPS)
    yg = yg * lax.rsqrt(jnp.mean(yg * yg, axis=-1, keepdims=True) + EPS)
    yg = yg.reshape(bsz, seqlen, D_INNER) * norm_w.astype(jnp.float32)
    return yg.astype(u.dtype) @ w_out.astype(u.dtype)


def setup_inputs(seed: int = 0) -> dict:
    key = jax.random.key(seed)
    ks = jax.random.split(key, 16)
    f32 = jnp.float32
    x = jax.random.normal(ks[0], (BATCH, SEQ, D_MODEL), f32)
    pre_norm = 1.0 + 0.02 * jax.random.normal(ks[1], (DEPTH, D_MODEL), f32)
    post_norm = 1.0 + 0.02 * jax.random.normal(ks[2], (DEPTH, D_MODEL), f32)
    sc_w_in = jax.random.normal(ks[3], (N_SC_LAYERS, D_MODEL, 4 * D_INNER), f32) * D_MODEL ** -0.5
    sc_conv_w = jax.random.normal(ks[4], (N_SC_LAYERS, SC_WIDTH, D_INNER), f32) * SC_WIDTH ** -0.5
    sc_w_out = jax.random.normal(ks[5], (N_SC_LAYERS, D_INNER, D_MODEL), f32) * D_INNER ** -0.5
    ssd_w_in = jax.random.normal(ks[6], (N_SSD_LAYERS, D_MODEL, SSD_IN), f32) * D_MODEL ** -0.5
    ssd_conv_w = jax.random.normal(ks[7], (N_SSD_LAYERS, SSD_CONV, SSD_XBC), f32) * SSD_CONV ** -0.5
    ssd_conv_b = 0.01 * jax.random.normal(ks[8], (N_SSD_LAYERS, SSD_XBC), f32)
    dt0 = jnp.exp(jax.random.uniform(ks[9], (N_SSD_LAYERS, SSD_HEADS), f32,
                                     math.log(1e-3), math.log(1e-1)))
    ssd_dt_bias = dt0 + jnp.log(-jnp.expm1(-dt0))
    ssd_a_log = jnp.log(jax.random.uniform(ks[10], (N_SSD_LAYERS, SSD_HEADS), f32, 1.0, 16.0))
    ssd_d_skip = 1.0 + 0.1 * jax.random.normal(ks[11], (N_SSD_LAYERS, SSD_HEADS), f32)
    ssd_norm = 1.0 + 0.02 * jax.random.normal(ks[12], (N_SSD_LAYERS, D_INNER), f32)
    ssd_w_out = jax.random.normal(ks[13], (N_SSD_LAYERS, D_INNER, D_MODEL), f32) * D_INNER ** -0.5
    return {"x": x, "pre_norm": pre_norm, "post_norm": post_norm,
            "sc_w_in": sc_w_in, "sc_conv_w": sc_conv_w, "sc_w_out": sc_w_out,
            "ssd_w_in": ssd_w_in, "ssd_conv_w": ssd_conv_w, "ssd_conv_b": ssd_conv_b,
            "ssd_dt_bias": ssd_dt_bias, "ssd_a_log": ssd_a_log, "ssd_d_skip": ssd_d_skip,
            "ssd_norm": ssd_norm, "ssd_w_out": ssd_w_out}


def reference(x, pre_norm, post_norm, sc_w_in, sc_conv_w, sc_w_out, ssd_w_in, ssd_conv_w,
              ssd_conv_b, ssd_dt_bias, ssd_a_log, ssd_d_skip, ssd_norm, ssd_w_out):
    h = x
    for i in range(DEPTH):
        u = rms_norm(h, pre_norm[i])
        j = i // N_MIXERS
        if i % N_MIXERS == 0:
            m = short_conv_mixer(u, sc_w_in[j], sc_conv_w[j], sc_w_out[j])
        else:
            m = ssd_mixer(u, ssd_w_in[j], ssd_conv_w[j], ssd_conv_b[j], ssd_dt_bias[j],
                          ssd_a_log[j], ssd_d_skip[j], ssd_norm[j], ssd_w_out[j])
        h = h + rms_norm(m, post_norm[i])
    return h
```

```python
from contextlib import ExitStack
import numpy as np
import concourse.bass as bass
import concourse.mybir as mybir
from concourse.bass_utils import run_bass_kernel_spmd

F32 = mybir.dt.float32
BF16 = mybir.dt.bfloat16
AF = mybir.ActivationFunctionType
ALU = mybir.AluOpType

D = 1024
KD = 8
DI = 2048
NCH = 16
NH = 32
NG = 8
HALO = 128
EPS = 1e-6
L1IN = 6176


class Sched:
    ENGS = ("pe", "act", "dve", "pool", "sp")

    def __init__(self):
        self.ops = {e: [] for e in self.ENGS}
        self.last_w = {}
        self.readers = {}
        self.dma_cnt = {}
        self.dma_inc = {}

    def barrier(self):
        toks = set()
        for e in self.ENGS:
            for i in range(len(self.ops[e]) - 1, -1, -1):
                if self.ops[e][i]["dma"] is None:
                    toks.add(("eng", e, i))
                    break
        for k, cnt in self.dma_cnt.items():
            toks.add(("dma", k, cnt))
        self.pending = {e: set(toks) for e in self.ENGS}

    def add(self, eng, fn, reads=(), writes=(), dma=None, inc=16):
        deps = set()
        if getattr(self, "pending", None) and self.pending.get(eng):
            deps |= self.pending[eng]
            self.pending[eng] = set()
        for r in reads:
            t = self.last_w.get(r)
            if t is not None:
                deps.add(t)
        for w in writes:
            t = self.last_w.get(w)
            if t is not None:
                deps.add(t)
            for t in self.readers.get(w, ()):
                deps.add(t)
        idx = len(self.ops[eng])
        if dma is not None:
            c = self.dma_cnt.get(dma, 0) + 1
            self.dma_cnt[dma] = c
            tok = ("dma", dma, c)
        else:
            tok = ("eng", eng, idx)
        if eng == "pe":
            deps = {d for d in deps if not (d[0] == "eng" and d[1] == "pe")}
        deps.discard(tok)
        if dma is not None:
            self.dma_inc[dma] = inc
        self.ops[eng].append(dict(fn=fn, deps=deps, tok=tok, dma=dma))
        for r in reads:
            lst = self.readers.setdefault(r, [])
            if tok[0] == "eng":
                lst[:] = [t for t in lst if not (t[0] == "eng" and t[1] == eng)]
            lst.append(tok)
        for w in writes:
            self.last_w[w] = tok
            self.readers[w] = []
        return tok

    def emit(self, nc, stack):
        sig = {e: [False] * len(self.ops[e]) for e in self.ENGS}
        for e in self.ENGS:
            for op in self.ops[e]:
                for d in op["deps"]:
                    if d[0] == "eng":
                        sig[d[1]][d[2]] = True
        cum = {}
        for e in self.ENGS:
            c = 0
            arr = []
            for s in sig[e]:
                if s:
                    c += 1
                arr.append(c)
            cum[e] = arr
        esem = {e: stack.enter_context(nc.semaphore("s_" + e)) for e in self.ENGS}
        dsem = {k: stack.enter_context(nc.semaphore("d_" + str(k))) for k in self.dma_cnt}
        block = stack.enter_context(nc.Block())

        def run(e, engobj):
            known = {}
            for i, op in enumerate(self.ops[e]):
                need = {}
                for d in op["deps"]:
                    if d[0] == "eng":
                        s, v = esem[d[1]], cum[d[1]][d[2]]
                    else:
                        s, v = dsem[d[1]], self.dma_inc[d[1]] * d[2]
                    if need.get(s.name, (None, 0))[1] < v:
                        need[s.name] = (s, v)
                for nm, (s, v) in need.items():
                    if known.get(nm, 0) >= v:
                        continue
                    engobj.wait_ge(s, v)
                    known[nm] = v
                ins = op["fn"](engobj)
                if op["dma"] is not None:
                    ins.then_inc(dsem[op["dma"]], self.dma_inc[op["dma"]])
                elif sig[e][i]:
                    ins.then_inc(esem[e], 1)

        block.tensor(lambda eng: run("pe", eng))
        block.scalar(lambda eng: run("act", eng))
        block.vector(lambda eng: run("dve", eng))
        block.gpsimd(lambda eng: run("pool", eng))
        block.sync(lambda eng: run("sp", eng))


class Ctx:
    def __init__(self, nc, st):
        self.nc = nc
        self.st = st
        self.S = Sched()

    def sb(self, name, shape, dt):
        return self.st.enter_context(self.nc.sbuf_tensor(name, shape, dt))

    def ps(self, name, shape, dt):
        return self.st.enter_context(self.nc.psum_tensor(name, shape, dt))

    def din(self, name, shape, dt=F32):
        return self.nc.dram_tensor(name, list(shape), dt, kind="ExternalInput").ap()

    def dout(self, name, shape, dt=F32):
        return self.nc.dram_tensor(name, list(shape), dt, kind="ExternalOutput").ap()


def make_ident(c, t, key):
    S = c.S
    S.add("pool", lambda e: e.memset(t[:], 0.0), writes=[key])
    S.add("pool", lambda e: e.affine_select(out=t[:], in_=t[:], pattern=[[-1, 128]],
                                            compare_op=ALU.not_equal, fill=1.0, base=0,
                                            channel_multiplier=1), reads=[key], writes=[key])


def make_tri(c, t, key):
    S = c.S
    S.add("pool", lambda e: e.memset(t[:], 1.0), writes=[key])
    S.add("pool", lambda e: e.affine_select(out=t[:], in_=t[:], pattern=[[1, 128]],
                                            compare_op=ALU.is_ge, fill=0.0, base=0,
                                            channel_multiplier=-1), reads=[key], writes=[key])


def emit_prenorm_stats(c, xres_j, xkey, gpre, bufs, uid):
    S = c.S
    stat, junk, utok, eps = bufs["stat"], bufs["junk"], bufs["utok"], bufs["eps"]
    sl = uid % 4
    st_ = stat[sl]
    sk = "stat%d" % sl
    ut = utok[uid % 2]
    uk = "utok%d" % (uid % 2)
    S.add("act", lambda e: e.activation(out=junk[:], in_=xres_j, func=AF.Square, accum_out=st_[:, 0:1]),
          reads=[xkey], writes=["junk", sk])
    S.add("act", lambda e: e.activation(out=st_[:, 1:2], in_=st_[:, 0:1], func=AF.Ln, scale=1.0 / D,
                                        bias=eps[:, 0:1]), reads=[sk, "eps"], writes=[sk])
    S.add("act", lambda e: e.activation(out=st_[:, 2:3], in_=st_[:, 1:2], func=AF.Exp, scale=-0.5),
          reads=[sk], writes=[sk])
    S.add("dve", lambda e: e.scalar_tensor_tensor(out=ut[:], in0=xres_j, scalar=st_[:, 2:3], in1=gpre[:],
                                                  op0=ALU.mult, op1=ALU.mult),
          reads=[xkey, sk, "gpre"], writes=[uk])


def emit_prenorm_T(c, uT, j, bufs, uid, ukp="uT"):
    S = c.S
    utok, tp, identb = bufs["utok"], bufs["tp"], bufs["identb"]
    ut = utok[uid % 2]
    uk = "utok%d" % (uid % 2)
    for kc in range(KD):
        S.add("pe", lambda e, kc=kc: e.transpose(tp[:, kc * 128:(kc + 1) * 128], ut[:, kc * 128:(kc + 1) * 128], identb[:]),
              reads=[uk, "identb"], writes=["tp"])
    S.add("act", lambda e: e.activation(out=uT[:, :, j * 128:(j + 1) * 128],
                                        in_=tp[:, 0:1024].rearrange("p (k t) -> p k t", k=KD), func=AF.Copy),
          reads=[uk], writes=["tp", "%s%d" % (ukp, j)])


def emit_prenorm(c, xres_j, xkey, gpre, uT, j, bufs, uid, ukp="uT"):
    emit_prenorm_stats(c, xres_j, xkey, gpre, bufs, uid)
    emit_prenorm_T(c, uT, j, bufs, uid, ukp)


def emit_outproj_post(c, lhs_buf, lhs_keys, wout, gpost, xres_j, xkey, j, bufs, uid, part=3):
    S = c.S
    stat, junk, eps, mres = bufs["stat"], bufs["junk"], bufs["eps"], bufs["mres"]
    po = bufs["po"][2 * (j % 2):2 * (j % 2) + 2] if len(bufs["po"]) >= 4 else bufs["po"]
    pok = bufs["pokeys"][2 * (j % 2):2 * (j % 2) + 2] if len(bufs["po"]) >= 4 else bufs["pokeys"]
    for hf in range(2 if (part & 1) else 0):
        for cc in range(NCH):
            S.add("pe", lambda e, hf=hf, cc=cc: e.matmul(po[hf][:], lhsT=lhs_buf[:, cc, j * 128:(j + 1) * 128],
                                                         rhs=wout[:, cc, hf * 512:(hf + 1) * 512],
                                                         start=(cc == 0), stop=(cc == NCH - 1)),
                  reads=[lhs_keys[cc], "wout"], writes=[pok[hf]])
    if not (part & 2):
        return
    sl = uid % 4
    st_ = stat[sl]
    sk = "stat%d" % sl
    for hf in range(2):
        S.add("act", lambda e, hf=hf: e.activation(out=junk[:, 0:512], in_=po[hf][:], func=AF.Square,
                                                   accum_out=st_[:, 3 + hf:4 + hf]),
              writes=[pok[hf], "junk", sk])
    S.add("dve", lambda e: e.tensor_tensor(out=st_[:, 5:6], in0=st_[:, 3:4], in1=st_[:, 4:5], op=ALU.add),
          reads=[sk], writes=[sk])
    S.add("act", lambda e: e.activation(out=st_[:, 6:7], in_=st_[:, 5:6], func=AF.Ln, scale=1.0 / D,
                                        bias=eps[:, 0:1]), reads=[sk, "eps"], writes=[sk])
    S.add("act", lambda e: e.activation(out=st_[:, 7:8], in_=st_[:, 6:7], func=AF.Exp, scale=-0.5),
          reads=[sk], writes=[sk])
    for hf in range(2):
        S.add("dve", lambda e, hf=hf: e.scalar_tensor_tensor(out=mres[:, hf * 512:(hf + 1) * 512], in0=po[hf][:],
                                                             scalar=st_[:, 7:8], in1=gpost[:, hf * 512:(hf + 1) * 512],
                                                             op0=ALU.mult, op1=ALU.mult),
              reads=[sk, "gpost"], writes=[pok[hf], "mres%d" % hf])
    S.add("pool", lambda e: e.tensor_tensor(out=xres_j, in0=xres_j, in1=mres[:], op=ALU.add),
          reads=["mres0", "mres1", xkey], writes=[xkey])


def common_bufs(c, po=None, pokeys=None):
    b = {}
    b["stat"] = [c.sb("stat%d" % i, [128, 8], F32) for i in range(4)]
    b["junk"] = c.sb("junk", [128, 1024], BF16)
    b["utok"] = [c.sb("utok%d" % i, [128, 1024], BF16) for i in range(2)]
    b["tp"] = c.ps("tp", [128, 1024], BF16)
    b["identb"] = c.sb("identb", [128, 128], BF16)
    b["eps"] = c.sb("eps", [128, 1], F32)
    b["mres"] = c.sb("mres", [128, 1024], F32)
    if po is None:
        po = [c.ps("po%d" % i, [128, 512], F32) for i in range(2)]
        pokeys = ["po0", "po1"]
    b["po"] = po
    b["pokeys"] = pokeys
    make_ident(c, b["identb"], "identb")
    c.S.add("dve", lambda e: e.memset(b["eps"][:], EPS), writes=["eps"])
    return b


def load_wout(c, wout_sb, w_out_dram):
    src = w_out_dram.rearrange("(c p) n -> p c n", p=128)
    for q in range(4):
        c.S.add("pool", lambda e, q=q: e.dma_start(out=wout_sb[:, q * 4:(q + 1) * 4, :], in_=src[:, q * 4:(q + 1) * 4, :]),
                writes=["wout"], dma="wout%d" % q)


def core_tokens(xflat_b, start, NTOK):
    out = np.zeros((HALO + NTOK, xflat_b.shape[1]), np.float32)
    lo = max(0, start - HALO)
    out[HALO - (start - lo):] = xflat_b[lo:start + NTOK]
    return out


def l1_param_maps(ssd_w_in, ssd_conv_w, ssd_conv_b, ssd_dt_bias, ssd_a_log):
    cw1h = np.ascontiguousarray(ssd_conv_w[0].reshape(4, 32, 128).transpose(2, 1, 0)).reshape(128, 128)
    cb1h = np.ascontiguousarray(ssd_conv_b[0].reshape(32, 128).T)
    dtbh = np.zeros((128, 1), np.float32)
    dtbh[:NH, 0] = ssd_dt_bias[0]
    alogh = np.zeros((128, 1), np.float32)
    alogh[:NH, 0] = ssd_a_log[0]
    return {"w_in": np.ascontiguousarray(ssd_w_in[0]), "cw1h": cw1h, "cb1h": cb1h, "dtbh": dtbh, "alogh": alogh}


def l1_full_extra(post_norm, ssd_d_skip, ssd_norm, ssd_w_out):
    dskh = np.ascontiguousarray(np.repeat(ssd_d_skip[0].reshape(NCH, 2), 64, axis=1).T).astype(np.float32)
    nwh = np.ascontiguousarray(ssd_norm[0].reshape(NCH, 128).T)
    return {"post": np.ascontiguousarray(post_norm[1]), "dskh": dskh, "nwh": nwh,
            "w_out": np.ascontiguousarray(ssd_w_out[0])}


def build_fused(NTOK, T):
    nc = bass.Bass("TRN2", target_bir_lowering=False)
    with ExitStack() as st:
        c = Ctx(nc, st)
        S = c.S
        NSUB = T // 128
        x = c.din("x", [HALO + NTOK, D])
        pre0 = c.din("pre0", [D])
        post0 = c.din("post0", [D])
        w0_in = c.din("w0_in", [D, 4 * DI])
        cw0h = c.din("cw0h", [128, NCH * 3])
        w0_out = c.din("w0_out", [DI, D])
        pre1 = c.din("pre1", [D])
        post1 = c.din("post1", [D])
        w1_in = c.din("w1_in", [D, L1IN])
        cw1h = c.din("cw1h", [128, 32 * 4])
        cb1h = c.din("cb1h", [128, 32])
        dtbh = c.din("dtbh", [128, 1])
        alogh = c.din("alogh", [128, 1])
        dskh = c.din("dskh", [128, NCH])
        nwh = c.din("nwh", [128, NCH])
        w1_out = c.din("w1_out", [DI, D])
        cmask = c.din("cmask", [128, 8])
        out = c.dout("out", [NTOK, D])
        w0s = nc.dram_tensor("w0s", [NCH, 128, KD * 4 * 128], BF16, kind="Internal").ap()
        w1s = nc.dram_tensor("w1s", [12, 128, KD * 512], BF16, kind="Internal").ap()
        h1s = nc.dram_tensor("h1s", [HALO + NTOK, D], F32, kind="Internal").ap()
        sbn = nc.dram_tensor("sbn", [128, DI], F32, kind="Internal").ap()
        sgt = nc.dram_tensor("sgt", [4 * 128, DI], F32, kind="Internal").ap()
        dbn = nc.dram_tensor("dbn", [128, 64], F32, kind="Internal").ap()
        dgt = nc.dram_tensor("dgt", [4 * 128, 64], F32, kind="Internal").ap()

        pin = [c.ps("pin%d" % i, [128, 512], F32) for i in range(4)]
        b = common_bufs(c)
        po = b["po"]
        identb, eps, tp = b["identb"], b["eps"], b["tp"]
        pC = c.ps("pC", [128, 512], F32)
        pS = pC[:, 384:512]
        pA = po[0]
        pAk = "po0"
        pA2 = po[1]
        pA2k = "po1"
        pY = [pin[2], pin[3]]
        pYk = ["pin2", "pin3"]
        bB = dict(b)
        bB["po"] = [pin[2], pin[3], po[0], po[1]]
        bB["pokeys"] = ["pin2", "pin3", "po0", "po1"]
        bA = dict(b)
        bA["po"] = [po[0], po[1], pin[0], pin[1]]
        bA["pokeys"] = ["po0", "po1", "pin0", "pin1"]

        gpre0 = c.sb("gpre0", [128, D], F32)
        gpre1 = c.sb("gpre1", [128, D], F32)
        gpost = c.sb("gpost", [128, D], F32)
        cw0 = c.sb("cw0", [128, NCH * 3], F32)
        carry0 = c.sb("carry0", [128, NCH, 2], F32)
        wout = c.sb("wout", [128, NCH, D], BF16)
        xres = [c.sb("xres%d" % i, [128, NSUB, D], F32) for i in range(2)]
        uTs = [c.sb("uT%d" % i, [128, KD, T], BF16) for i in range(2)]
        wbuf = [c.sb("wbuf%d" % i, [128, KD * 512], BF16) for i in range(3)]
        yT = c.sb("yT", [128, NCH, T], BF16)
        tmp = [c.sb("tmp%d" % i, [128, T], F32) for i in range(8)]
        cv = [c.sb("cv%d" % i, [128, T + 2], F32) for i in range(2)]
        cw1 = c.sb("cw1", [128, 32 * 4], F32)
        cb1 = c.sb("cb1", [128, 32], F32)
        dtb = c.sb("dtb", [128, 1], F32)
        acol = c.sb("acol", [128, 1], F32)
        onec = c.sb("onec", [128, 1], F32)
        identf = c.sb("identf", [128, 128], F32)
        trif = c.sb("trif", [128, 128], F32)
        onesf = c.sb("onesf", [128, 128], F32)
        wdt = c.sb("wdt", [128, KD, NH], BF16)
        carry1 = c.sb("carry1", [128, 32, 3], F32)
        xT = c.sb("xT", [128, NCH, T], BF16)
        BT = c.sb("BT", [128, NG, T], BF16)
        CT = c.sb("CT", [128, NG, T], BF16)
        xp = [c.sb("xp%d" % i, [128, T + 3], F32) for i in range(4)]
        acc = [c.sb("acc%d" % i, [128, T], F32) for i in range(4)]
        l1banks = [(pin[0], "pin0"), (pin[1], "pin1"), (po[0], "po0"), (po[1], "po1")]
        dtT = c.sb("dtT", [128, T], F32)
        adtT = c.sb("adtT", [128, T], F32)
        cs = [c.sb("cs%d" % i, [128, 8, NH], F32) for i in range(NSUB)]
        hl = [c.sb("hl%d" % i, [128, 4, NH], BF16) for i in range(NSUB)]
        xdt = [c.sb("xdt%d" % i, [128, 256], BF16) for i in range(2)]
        xB = [c.sb("xB%d" % i, [128, 384], BF16) for i in range(2)]
        xs = [c.sb("xs%d" % i, [128, 256], BF16) for i in range(2)]
        hT = c.sb("hT", [128, DI], F32)
        dcumt = c.sb("dcumt", [128, 64], F32)
        dcum = dcumt[:, 0:NH]
        dstage = c.sb("dstage", [128, 4, 64], F32)
        dsk = c.sb("dsk", [128, NCH], F32)
        nw = c.sb("nw", [128, NCH], F32)
        cm = c.sb("cm", [128, 8], F32)
        trib = c.sb("trib", [128, 128], BF16)
        onesb = c.sb("onesb", [128, 128], BF16)
        hTb = c.sb("hTb", [128, DI], BF16)
        dec4 = [c.sb("dec%d" % i, [128, 512], F32) for i in range(2)]
        ebc4 = [c.sb("ebc%d" % i, [128, 512], F32) for i in range(2)]
        m14 = [c.sb("m1%d" % i, [128, 512], F32) for i in range(2)]
        MT4 = [c.sb("MT%d" % i, [128, 512], BF16) for i in range(2)]
        CsT4 = [c.sb("CsT%d" % i, [128, 512], BF16) for i in range(2)]
        sstage = c.sb("sstage", [128, DI], F32)
        fac = c.sb("fac", [128, NH], F32)
        sq = [c.sb("sq%d" % i, [128, T], BF16) for i in range(2)]
        hkeys = ["hT%d" % g for g in range(NG)]

        w0src = w0_in.rearrange("(kc p) n -> p kc n", p=128)
        for ch in range(NCH):
            dst = w0s[ch].rearrange("p (kc w n) -> p kc w n", kc=KD, w=4)
            for which in range(4):
                col0 = which * DI + ch * 128
                S.add("pool", lambda e, dst=dst, which=which, col0=col0: e.dma_start(out=dst[:, :, which, :], in_=w0src[:, :, col0:col0 + 128]),
                      writes=["w0sraw%d_%d" % (ch, which)], dma="cast%d" % ((ch * 4 + which) % 4))
        w1src = w1_in.rearrange("(kc p) n -> p kc n", p=128)
        for t_, src_, k_ in ((gpre0, pre0, "gpre0"), (gpre1, pre1, "gpre1"), (gpost, post0, "gpost")):
            S.add("sp", lambda e, t_=t_, src_=src_: e.dma_start(out=t_[:], in_=src_.partition_broadcast(128)), writes=[k_], dma=k_)
        for t_, src_, k_ in ((cw0, cw0h, "cw0"), (cw1, cw1h, "cw1"), (cb1, cb1h, "cb1"), (dtb, dtbh, "dtb"), (acol, alogh, "acol"),
                             (dsk, dskh, "dsk"), (nw, nwh, "nw"), (cm, cmask, "cm")):
            S.add("sp", lambda e, t_=t_, src_=src_: e.dma_start(out=t_[:], in_=src_[:, :]), writes=[k_], dma=k_)
        S.add("pool", lambda e: e.dma_start(out=wdt[:], in_=w1src[:, :, L1IN - NH:L1IN]), writes=["wdt"], dma="wdt")
        load_wout(c, wout, w0_out)
        S.add("act", lambda e: e.activation(out=acol[:], in_=acol[:], func=AF.Exp), reads=["acol"], writes=["acol"])
        S.add("dve", lambda e: e.tensor_scalar(out=acol[:], in0=acol[:], scalar1=-1.0, scalar2=None, op0=ALU.mult),
              reads=["acol"], writes=["acol"])
        S.add("dve", lambda e: e.memset(onec[:], 1.0), writes=["onec"])
        S.add("pool", lambda e: e.memset(onesf[:], 1.0), writes=["onesf"])
        S.add("pool", lambda e: e.memset(onesb[:], 1.0), writes=["onesb"])
        S.add("pool", lambda e: e.memset(carry0[:], 0.0), writes=["carry0_%d" % i for i in range(NCH)])
        S.add("pool", lambda e: e.memset(carry1[:], 0.0), writes=["carry%d" % i for i in range(32)])
        make_ident(c, identf, "identf")
        make_tri(c, trif, "trif")
        make_tri(c, trib, "trib")
        S.add("dve", lambda e: e.memset(hT[:], 0.0), writes=hkeys)
        S.add("dve", lambda e: e.memset(dcumt[:], 1.0), writes=["dcum"])
        for gi in range(12):
            S.add("pool", lambda e, gi=gi: e.dma_start(out=w1s[gi].rearrange("p (kc n) -> p kc n", kc=KD), in_=w1src[:, :, gi * 512:(gi + 1) * 512]),
                  writes=["w1sraw%d" % gi], dma="cast%d" % (gi % 4))

        state = dict(uid=0, wcnt=0, ocnt=0, gj=0, w0ready=False, w1ready=False)

        def cast_ready(which_layer):
            if which_layer == 0 and not state["w0ready"]:
                state["w0ready"] = True
                S.add("sp", lambda e: e.nop(), reads=["w0sraw%d_%d" % (ch, w) for ch in range(NCH) for w in range(4)],
                      writes=["w0s%d" % ch for ch in range(NCH)])
            if which_layer == 1 and not state["w1ready"]:
                state["w1ready"] = True
                S.add("sp", lambda e: e.nop(), reads=["w1sraw%d" % gi for gi in range(12)] + ["w0sraw%d_%d" % (ch, w) for ch in range(NCH) for w in range(4)],
                      writes=["w1s%d" % gi for gi in range(12)])

        def pre_s(xr, xkeys, nsub, gp):
            uids = []
            for j in range(nsub):
                emit_prenorm_stats(c, xr[:, j, :], xkeys[j], gp, b, state["uid"])
                uids.append(state["uid"])
                state["uid"] += 1
            return uids

        def pre_t(nsub, ub, uids):
            for j in range(nsub):
                emit_prenorm_T(c, uTs[ub], j, b, uids[j], ukp="uT%d_" % ub)

        def l0_pre(xr, xkeys, nsub, ub):
            pre_t(nsub, ub, pre_s(xr, xkeys, nsub, gpre0))

        def l0_tile(xr, xkeys, nsub, ub):
            Tt = nsub * 128
            uT = uTs[ub]
            ukeys = ["uT%d_%d" % (ub, j) for j in range(nsub)]
            cast_ready(0)
            for ch in range(NCH):
                ws = state["wcnt"] % 3
                state["wcnt"] += 1
                wbv = wbuf[ws][:].rearrange("p (kc w n) -> p kc w n", kc=KD, w=4)
                S.add("sp", lambda e, ws=ws, ch=ch: e.dma_start(out=wbuf[ws][:], in_=w0s[ch]), reads=["w0s%d" % ch], writes=["wb%d" % ws], dma="wb%d" % ws)
                pr = ch % 2
                for which, bank in ((2, 0), (3, 1), (0, 2), (1, 3)):
                    for kc in range(KD):
                        S.add("pe", lambda e, wbv=wbv, which=which, bank=bank, kc=kc, Tt=Tt: e.matmul(
                            pin[bank][:, 0:Tt], lhsT=wbv[:, kc, which, :], rhs=uT[:, kc, 0:Tt], start=(kc == 0), stop=(kc == KD - 1)),
                            reads=["wb%d" % ws] + ukeys, writes=["pin%d" % bank])
                a, bq, cq, dq, cvb = tmp[pr], tmp[2 + pr], tmp[4 + pr], tmp[6 + pr], cv[pr]
                ka, kb, kc_, kd = "tmp%d" % pr, "tmp%d" % (2 + pr), "tmp%d" % (4 + pr), "tmp%d" % (6 + pr)
                ck = "carry0_%d" % ch
                S.add("act", lambda e, a=a, Tt=Tt: e.activation(out=a[:, 0:Tt], in_=pin[0][:, 0:Tt], func=AF.Copy), writes=["pin0", ka])
                S.add("pool", lambda e, cvb=cvb, ch=ch: e.tensor_copy(out=cvb[:, 0:2], in_=carry0[:, ch, :]), reads=[ck], writes=["cvh%d" % pr])
                S.add("dve", lambda e, cvb=cvb, a=a, Tt=Tt: e.tensor_tensor(out=cvb[:, 2:2 + Tt], in0=a[:, 0:Tt], in1=pin[1][:, 0:Tt], op=ALU.mult),
                      reads=[ka], writes=["pin1", "cvb%d" % pr])
                S.add("act", lambda e, bq=bq, Tt=Tt: e.activation(out=bq[:, 0:Tt], in_=pin[2][:, 0:Tt], func=AF.Silu), writes=["pin2", kb])
                S.add("dve", lambda e, cq=cq, bq=bq, Tt=Tt: e.tensor_tensor(out=cq[:, 0:Tt], in0=bq[:, 0:Tt], in1=pin[3][:, 0:Tt], op=ALU.mult),
                      reads=[kb], writes=["pin3", kc_])
                S.add("act", lambda e, dq=dq, cvb=cvb, ch=ch, Tt=Tt: e.activation(
                    out=dq[:, 0:Tt], in_=cvb[:, 0:Tt], func=AF.Copy, scale=cw0[:, ch * 3:ch * 3 + 1]),
                    reads=["cvh%d" % pr, "cvb%d" % pr, "cw0"], writes=[kd])
                for tap in (1, 2):
                    S.add("dve", lambda e, dq=dq, cvb=cvb, ch=ch, tap=tap, Tt=Tt: e.scalar_tensor_tensor(
                        out=dq[:, 0:Tt], in0=cvb[:, tap:tap + Tt], scalar=cw0[:, ch * 3 + tap:ch * 3 + tap + 1],
                        in1=dq[:, 0:Tt], op0=ALU.mult, op1=ALU.add),
                        reads=["cvh%d" % pr, "cvb%d" % pr, "cw0", kd], writes=[kd])
                S.add("pool", lambda e, cvb=cvb, ch=ch, Tt=Tt: e.tensor_copy(out=carry0[:, ch, :], in_=cvb[:, Tt:Tt + 2]),
                      reads=["cvb%d" % pr], writes=[ck])
                S.add("pool", lambda e, dq=dq, cq=cq, ch=ch, Tt=Tt: e.tensor_tensor(out=yT[:, ch, 0:Tt], in0=dq[:, 0:Tt], in1=cq[:, 0:Tt], op=ALU.mult),
                      reads=[kd, kc_], writes=["yT%d" % ch])
            ykeys = ["yT%d" % ch for ch in range(NCH)]
            ouids = []
            for j in range(nsub):
                emit_outproj_post(c, yT, ykeys, wout, gpost, xr[:, j, :], xkeys[j], j, bA, state["uid"], part=1)
                ouids.append(state["uid"])
                state["uid"] += 1
            for j in range(nsub):
                emit_outproj_post(c, yT, ykeys, wout, gpost, xr[:, j, :], xkeys[j], j, bA, ouids[j], part=2)

        def l1_pre(xr, xkeys, nsub, ub):
            pre_t(nsub, ub, pre_s(xr, xkeys, nsub, gpre1))

        def l1_tile(xr, xkeys, nsub, full, halo, bb, ub, pre_done, hook1, hook2):
            Tt = nsub * 128
            uT = uTs[ub]
            if not pre_done:
                l1_pre(xr, xkeys, nsub, ub)
            ukeys = ["uT%d_%d" % (ub, j) for j in range(nsub)]
            groups = list(range(12)) if (full and not halo) else ([4, 5, 6, 7, 8, 9, 10, 11] if full else [4, 5, 6, 7, 8, 9])
            def emit_dt_stats():
                pb, pk = l1banks[state["ocnt"] % 4]
                state["ocnt"] += 1
                for kc in range(KD):
                    S.add("pe", lambda e, kc=kc, pb=pb, Tt=Tt: e.matmul(pb[0:NH, 0:Tt], lhsT=wdt[:, kc, :], rhs=uT[:, kc, 0:Tt],
                                                                      start=(kc == 0), stop=(kc == KD - 1)), reads=["wdt"] + ukeys, writes=[pk])
                S.add("act", lambda e, pb=pb, Tt=Tt: e.activation(out=dtT[0:NH, 0:Tt], in_=pb[0:NH, 0:Tt], func=AF.Exp, bias=dtb[0:NH, 0:1], scale=1.0),
                      reads=["dtb"], writes=[pk, "dtT"])
                S.add("act", lambda e, Tt=Tt: e.activation(out=dtT[0:NH, 0:Tt], in_=dtT[0:NH, 0:Tt], func=AF.Ln, bias=onec[0:NH, 0:1], scale=1.0),
                      reads=["dtT", "onec"], writes=["dtT"])
                S.add("dve", lambda e, Tt=Tt: e.tensor_scalar(out=adtT[0:NH, 0:Tt], in0=dtT[0:NH, 0:Tt], scalar1=acol[0:NH, 0:1], scalar2=None, op0=ALU.mult),
                      reads=["dtT", "acol"], writes=["adtT"])

            def emit_stats():
                for j in range(nsub):
                    js = slice(j * 128, (j + 1) * 128)
                    csj, hlj = cs[j], hl[j]
                    ckey, hkey = "cs%d" % j, "hl%d" % j
                    S.add("pe", lambda e, js=js: e.transpose(pS[:, 0:NH], dtT[0:NH, js], identf[0:NH, 0:NH]), reads=["dtT", "identf"], writes=["pC"])
                    S.add("pe", lambda e, js=js: e.transpose(pS[:, NH:2 * NH], adtT[0:NH, js], identf[0:NH, 0:NH]), reads=["adtT", "identf"], writes=["pC"])
                    S.add("act", lambda e, csj=csj: e.activation(out=csj[:, 0:2, :], in_=pS[:, 0:2 * NH].rearrange("p (a h) -> p a h", a=2), func=AF.Copy),
                          writes=["pC", ckey])
                    S.add("pe", lambda e, csj=csj: e.matmul(pS[:, 2 * NH:3 * NH], lhsT=trif[:], rhs=csj[:, 1, :], start=True, stop=True),
                          reads=[ckey, "trif"], writes=["pC"])
                    S.add("pe", lambda e, csj=csj: e.matmul(pS[:, 3 * NH:4 * NH], lhsT=onesf[:], rhs=csj[:, 1, :], start=True, stop=True),
                          reads=[ckey, "onesf"], writes=["pC"])
                    S.add("act", lambda e, csj=csj: e.activation(out=csj[:, 2, :], in_=pS[:, 2 * NH:3 * NH], func=AF.Copy), writes=["pC", ckey])
                    S.add("act", lambda e, csj=csj: e.activation(out=csj[:, 3, :], in_=pS[:, 2 * NH:3 * NH], func=AF.Copy, scale=-1.0), writes=["pC", ckey])
                    S.add("dve", lambda e, csj=csj: e.tensor_tensor(out=csj[:, 7, :], in0=pS[:, 3 * NH:4 * NH], in1=csj[:, 2, :], op=ALU.subtract),
                          writes=["pC", ckey])
                    S.add("act", lambda e, csj=csj: e.activation(out=csj[:, 6, :], in_=pS[:, 3 * NH:4 * NH], func=AF.Exp), writes=["pC", ckey])
                    S.add("act", lambda e, csj=csj: e.activation(out=csj[:, 7, :], in_=csj[:, 7, :], func=AF.Exp), reads=[ckey], writes=[ckey])
                    S.add("dve", lambda e, csj=csj: e.tensor_tensor(out=csj[:, 5, :], in0=csj[:, 7, :], in1=csj[:, 0, :], op=ALU.mult), reads=[ckey], writes=[ckey])
                    if full:
                        S.add("dve", lambda e, csj=csj, hlj=hlj: e.tensor_copy(out=hlj[:, 0, :], in_=csj[:, 1, :]), reads=[ckey], writes=[hkey])
                        S.add("dve", lambda e, csj=csj, hlj=hlj: e.tensor_tensor(out=hlj[:, 1, :], in0=csj[:, 1, :], in1=hlj[:, 0, :], op=ALU.subtract),
                              reads=[ckey, hkey], writes=[hkey])
                        S.add("dve", lambda e, csj=csj, hlj=hlj: e.tensor_copy(out=hlj[:, 2, :], in_=csj[:, 3, :]), reads=[ckey, hkey], writes=[hkey])
                        S.add("dve", lambda e, csj=csj, hlj=hlj: e.tensor_tensor(out=hlj[:, 3, :], in0=csj[:, 3, :], in1=hlj[:, 2, :], op=ALU.subtract),
                              reads=[ckey, hkey], writes=[hkey])
                    else:
                        S.add("dve", lambda e, csj=csj: e.tensor_tensor(out=dcum, in0=dcum, in1=csj[:, 6, :], op=ALU.mult), reads=[ckey, "dcum"], writes=["dcum"])

            state["ocnt"] = 0
            if not halo:
                emit_dt_stats()
            if not full:
                hook1()
            stats_after = None if halo else (3 if full else 4)
            hook1_after = 7 if full else None
            cast_ready(1)
            pend = []
            for gi in groups:
                ws = state["wcnt"] % 3
                state["wcnt"] += 1
                wbv = wbuf[ws][:].rearrange("p (kc n) -> p kc n", kc=KD)
                S.add("sp", lambda e, ws=ws, gi=gi: e.dma_start(out=wbuf[ws][:], in_=w1s[gi]), reads=["w1s%d" % gi], writes=["wb%d" % ws], dma="wb%d" % ws)
                for q in range(4):
                    o = gi * 4 + q
                    pb, pk = l1banks[state["ocnt"] % 4]
                    state["ocnt"] += 1
                    for kc in range(KD):
                        S.add("pe", lambda e, wbv=wbv, q=q, kc=kc, pb=pb, Tt=Tt: e.matmul(
                            pb[:, 0:Tt], lhsT=wbv[:, kc, q * 128:(q + 1) * 128], rhs=uT[:, kc, 0:Tt],
                            start=(kc == 0), stop=(kc == KD - 1)), reads=["wb%d" % ws] + ukeys, writes=[pk])
                    if o < 16:
                        S.add("act", lambda e, o=o, pb=pb, Tt=Tt: e.activation(out=yT[:, o, 0:Tt], in_=pb[:, 0:Tt], func=AF.Silu),
                              writes=[pk, "yT%d" % o])
                        continue
                    ci = o - 16
                    ck = "carry%d" % ci
                    if halo:
                        S.add("dve", lambda e, ci=ci, pb=pb, Tt=Tt: e.tensor_copy(out=carry1[:, ci, :], in_=pb[:, Tt - 3:Tt]), writes=[pk, ck])
                        continue
                    sl = ci % 4
                    xpb, ab = xp[sl], acc[sl]
                    S.add("act", lambda e, xpb=xpb, pb=pb, Tt=Tt: e.activation(out=xpb[:, 3:3 + Tt], in_=pb[:, 0:Tt], func=AF.Copy),
                          writes=[pk, "xpb%d" % sl])
                    S.add("pool", lambda e, xpb=xpb, ci=ci: e.tensor_copy(out=xpb[:, 0:3], in_=carry1[:, ci, :]), reads=[ck], writes=["xph%d" % sl])
                    S.add("act", lambda e, pb=pb, ab=ab, ci=ci, Tt=Tt: e.activation(
                        out=ab[:, 0:Tt], in_=pb[:, 0:Tt], func=AF.Copy, scale=cw1[:, ci * 4 + 3:ci * 4 + 4]),
                        reads=["cw1"], writes=[pk, "acc%d" % sl])
                    for tap in (0, 1, 2):
                        S.add("dve", lambda e, xpb=xpb, ab=ab, ci=ci, tap=tap, Tt=Tt: e.scalar_tensor_tensor(
                            out=ab[:, 0:Tt], in0=xpb[:, tap:tap + Tt], scalar=cw1[:, ci * 4 + tap:ci * 4 + tap + 1],
                            in1=ab[:, 0:Tt], op0=ALU.mult, op1=ALU.add),
                            reads=["xph%d" % sl, "xpb%d" % sl, "cw1", "acc%d" % sl], writes=["acc%d" % sl])
                    S.add("pool", lambda e, xpb=xpb, ci=ci, Tt=Tt: e.tensor_copy(out=carry1[:, ci, :], in_=xpb[:, Tt:Tt + 3]),
                          reads=["xpb%d" % sl], writes=[ck])
                    if ci < 16:
                        dst, dk = xT[:, ci, 0:Tt], "xT%d" % ci
                    elif ci < 24:
                        dst, dk = BT[:, ci - 16, 0:Tt], "BT%d" % (ci - 16)
                    else:
                        dst, dk = CT[:, ci - 24, 0:Tt], "CT%d" % (ci - 24)
                    pend.append(lambda dst=dst, ab=ab, ci=ci, Tt=Tt, sl=sl, dk=dk: S.add(
                        "act", lambda e: e.activation(out=dst, in_=ab[:, 0:Tt], func=AF.Silu, bias=cb1[:, ci:ci + 1], scale=1.0),
                        reads=["acc%d" % sl, "cb1"], writes=[dk]))
                    if len(pend) > 2:
                        pend.pop(0)()
                if gi == stats_after:
                    emit_stats()
                if gi == hook1_after:
                    hook1()
            while pend:
                pend.pop(0)()
            if halo:
                hook2()
                return
            assert 2 * T <= 512
            its = [(g, j) for gp in range(0, NG, 2) for j in range(nsub) for g in (gp, gp + 1)]

            def stage_a(n):
                g, j = its[n]
                js = slice(j * 128, (j + 1) * 128)
                csj, hlj = cs[j], hl[j]
                ckey, hkey = "cs%d" % j, "hl%d" % j
                sl = state["gj"] % 2
                state["gj"] += 1
                xBb, xsb, xdb = xB[sl], xs[sl], xdt[sl]
                for q in range(2):
                    S.add("pe", lambda e, q=q, g=g, js=js: e.transpose(tp[:, q * 128:(q + 1) * 128], xT[:, 2 * g + q, js], identb[:]),
                          reads=["xT%d" % (2 * g + q), "identb"], writes=["tp"])
                S.add("pe", lambda e, g=g, js=js: e.transpose(tp[:, 256:384], BT[:, g, js], identb[:]), reads=["BT%d" % g, "identb"], writes=["tp"])
                S.add("act", lambda e, xBb=xBb: e.activation(out=xBb[:], in_=tp[:, 0:384], func=AF.Copy), writes=["tp", "xB%d" % sl])
                S.add("dve", lambda e, xBb=xBb, xsb=xsb, csj=csj, g=g: e.tensor_tensor(
                    out=xsb[:].rearrange("p (h q) -> p h q", h=4), in0=xBb[:, 0:256].rearrange("p (h q) -> p h q", h=4),
                    in1=csj[:, 5, 4 * g:4 * g + 4].unsqueeze(2).to_broadcast([128, 4, 64]), op=ALU.mult),
                    reads=["xB%d" % sl, ckey], writes=["xs%d" % sl])
                if not full:
                    return sl
                S.add("dve", lambda e, xBb=xBb, xdb=xdb, csj=csj, g=g: e.tensor_tensor(
                    out=xdb[:].rearrange("p (h q) -> p h q", h=4), in0=xBb[:, 0:256].rearrange("p (h q) -> p h q", h=4),
                    in1=csj[:, 0, 4 * g:4 * g + 4].unsqueeze(2).to_broadcast([128, 4, 64]), op=ALU.mult),
                    reads=["xB%d" % sl, ckey], writes=["xdt%d" % sl])
                S.add("pe", lambda e, g=g, js=js: e.matmul(pC[:, 0:128], lhsT=BT[:, g, js], rhs=CT[:, g, js], start=True, stop=True),
                      reads=["BT%d" % g, "CT%d" % g], writes=["pC"])
                for i in range(4):
                    h = 4 * g + i
                    reg = slice(i * 128, (i + 1) * 128)
                    S.add("pe", lambda e, hlj=hlj, h=h, reg=reg: e.matmul(pA[:, reg], lhsT=hlj[:, 0, h:h + 1].to_broadcast([128, 128]), rhs=trib[:],
                                                                        start=True, stop=False), reads=[hkey, "trib"], writes=[pAk])
                    S.add("pe", lambda e, hlj=hlj, h=h, reg=reg: e.matmul(pA[:, reg], lhsT=hlj[:, 1, h:h + 1].to_broadcast([128, 128]), rhs=trib[:],
                                                                        start=False, stop=True), reads=[hkey, "trib"], writes=[pAk])
                S.add("act", lambda e, sl=sl: e.activation(out=ebc4[sl][:], in_=pA[:, 0:512], func=AF.Exp), writes=[pAk, "ebc%d" % sl])
                for i in range(4):
                    h = 4 * g + i
                    reg = slice(i * 128, (i + 1) * 128)
                    S.add("act", lambda e, sl=sl, csj=csj, h=h, reg=reg: e.activation(out=dec4[sl][:, reg], in_=pA[:, reg], func=AF.Exp,
                                                                                     bias=csj[:, 3, h:h + 1], scale=1.0),
                          reads=[ckey], writes=[pAk, "dec%d" % sl])
                S.add("dve", lambda e, sl=sl: e.tensor_tensor(out=m14[sl][:].rearrange("p (h l) -> p h l", h=4),
                                                              in0=dec4[sl][:].rearrange("p (h l) -> p h l", h=4),
                                                              in1=pC[:, 0:128].unsqueeze(1).to_broadcast([128, 4, 128]), op=ALU.mult),
                      reads=["dec%d" % sl], writes=["pC", "m1%d" % sl])
                S.add("pool", lambda e, sl=sl: e.affine_select(out=MT4[sl][:].rearrange("p (h l) -> p h l", h=4),
                                                               in_=m14[sl][:].rearrange("p (h l) -> p h l", h=4),
                                                               pattern=[[0, 4], [1, 128]], compare_op=ALU.is_ge, fill=0.0, base=0, channel_multiplier=-1),
                      reads=["m1%d" % sl], writes=["MT%d" % sl])
                S.add("pool", lambda e, sl=sl, g=g, js=js: e.tensor_tensor(out=CsT4[sl][:].rearrange("p (h l) -> p h l", h=4),
                                                                          in0=ebc4[sl][:].rearrange("p (h l) -> p h l", h=4),
                                                                          in1=CT[:, g, js].unsqueeze(1).to_broadcast([128, 4, 128]), op=ALU.mult),
                      reads=["ebc%d" % sl, "CT%d" % g], writes=["CsT%d" % sl])
                return sl

            def stage_b(n, sl):
                g, j = its[n]
                js = slice(j * 128, (j + 1) * 128)
                csj = cs[j]
                ckey = "cs%d" % j
                hk, hbk = "hT%d" % g, "hTb%d" % g
                xBb, xsb, xdb = xB[sl], xs[sl], xdt[sl]
                if full:
                    for i in range(4):
                        h = 4 * g + i
                        cc = i // 2
                        reg = slice(i * 128, (i + 1) * 128)
                        yo = pY[g % 2][(i % 2) * 64:(i % 2 + 1) * 64, cc * T + j * 128:cc * T + (j + 1) * 128]
                        yk = pYk[g % 2]
                        S.add("pe", lambda e, yo=yo, xdb=xdb, sl=sl, i=i, reg=reg: e.matmul(yo, lhsT=xdb[:, i * 64:(i + 1) * 64], rhs=MT4[sl][:, reg], start=True, stop=False),
                              reads=["xdt%d" % sl, "MT%d" % sl], writes=[yk])
                        S.add("pe", lambda e, yo=yo, h=h, sl=sl, reg=reg: e.matmul(yo, lhsT=hTb[:, h * 64:(h + 1) * 64], rhs=CsT4[sl][:, reg], start=False, stop=True),
                              reads=[hbk, "CsT%d" % sl], writes=[yk])
                S.add("pe", lambda e, xBb=xBb, xsb=xsb: e.matmul(pC[:, 128:384], lhsT=xBb[:, 256:384], rhs=xsb[:], start=True, stop=True),
                      reads=["xB%d" % sl, "xs%d" % sl], writes=["pC"])
                hg = hT[:, g * 256:(g + 1) * 256]
                ueng = "dve" if full else "pool"
                S.add(ueng, lambda e, hg=hg, csj=csj, g=g: e.tensor_tensor(
                    out=hg.rearrange("p (h q) -> p h q", h=4), in0=hg.rearrange("p (h q) -> p h q", h=4),
                    in1=csj[:, 6, 4 * g:4 * g + 4].unsqueeze(2).to_broadcast([128, 4, 64]), op=ALU.mult), reads=[ckey], writes=[hk])
                S.add("dve", lambda e, hg=hg: e.tensor_tensor(out=hg, in0=hg, in1=pC[:, 128:384], op=ALU.add), writes=["pC", hk])
                if full:
                    S.add("act", lambda e, hg=hg, g=g: e.activation(out=hTb[:, g * 256:(g + 1) * 256], in_=hg, func=AF.Copy), reads=[hk], writes=[hbk])
                if full and j == nsub - 1:
                    for q in range(2):
                        cc = 2 * g + q
                        ygb = tmp[(2 * g + q) % 4]
                        ygk = "tmp%d" % ((2 * g + q) % 4)
                        sqb = sq[q]
                        S.add("dve", lambda e, ygb=ygb, cc=cc, q=q, g=g, Tt=Tt: e.scalar_tensor_tensor(
                            out=ygb[:, 0:Tt], in0=xT[:, cc, 0:Tt], scalar=dsk[:, cc:cc + 1], in1=pY[g % 2][:, q * T:q * T + Tt], op0=ALU.mult, op1=ALU.add),
                              reads=["xT%d" % cc, "dsk"], writes=[pYk[g % 2], ygk])
                        S.add("dve", lambda e, ygb=ygb, cc=cc, Tt=Tt: e.tensor_tensor(out=ygb[:, 0:Tt], in0=ygb[:, 0:Tt], in1=yT[:, cc, 0:Tt], op=ALU.mult),
                              reads=["yT%d" % cc, ygk], writes=[ygk])
                        S.add("act", lambda e, ygb=ygb, sqb=sqb, Tt=Tt: e.activation(out=sqb[:, 0:Tt], in_=ygb[:, 0:Tt], func=AF.Square),
                              reads=[ygk], writes=["sq%d" % q])
                        S.add("pe", lambda e, sqb=sqb, q=q, Tt=Tt: e.matmul(pin[0][:, 0:Tt], lhsT=onesb[:], rhs=sqb[:, 0:Tt], start=(q == 0), stop=(q == 1)),
                              reads=["sq%d" % q, "onesb"], writes=["pin0"])
                    gl, gr = tmp[4], tmp[5]
                    S.add("act", lambda e, Tt=Tt: e.activation(out=gl[:, 0:Tt], in_=pin[0][:, 0:Tt], func=AF.Ln, scale=1.0 / 256, bias=eps[:, 0:1]),
                          reads=["eps"], writes=["pin0", "tmp4"])
                    S.add("act", lambda e, Tt=Tt: e.activation(out=gr[:, 0:Tt], in_=gl[:, 0:Tt], func=AF.Exp, scale=-0.5), reads=["tmp4"], writes=["tmp5"])
                    for q in range(2):
                        cc = 2 * g + q
                        ygb = tmp[(2 * g + q) % 4]
                        ygk = "tmp%d" % ((2 * g + q) % 4)
                        S.add("dve", lambda e, ygb=ygb, cc=cc, Tt=Tt: e.scalar_tensor_tensor(
                            out=yT[:, cc, 0:Tt], in0=ygb[:, 0:Tt], scalar=nw[:, cc:cc + 1], in1=gr[:, 0:Tt], op0=ALU.mult, op1=ALU.mult),
                            reads=[ygk, "nw", "tmp5"], writes=["yT%d" % cc])

            slots = {0: stage_a(0)}
            for n in range(len(its)):
                if n + 1 < len(its):
                    slots[n + 1] = stage_a(n + 1)
                stage_b(n, slots[n])
            if full:
                zkeys = ["yT%d" % cc for cc in range(NCH)]
                ouids = []
                for j in range(nsub):
                    emit_outproj_post(c, yT, zkeys, wout, gpost, xr[:, j, :], xkeys[j], j, bb, state["uid"], part=1)
                    ouids.append(state["uid"])
                    state["uid"] += 1
                hook2()
                for j in range(nsub):
                    emit_outproj_post(c, yT, zkeys, wout, gpost, xr[:, j, :], xkeys[j], j, bb, ouids[j], part=2)
            else:
                hook2()

        tiles = [(0, 1)] + [(HALO + i * T, NSUB) for i in range(NTOK // T)]
        def load_x(ti):
            tok0, nsub = tiles[ti]
            Tt = nsub * 128
            xr = xres[ti % 2]
            xkeys = ["xres%d_%d" % (ti % 2, j) for j in range(NSUB)]
            S.add("sp", lambda e: e.dma_start(out=xr[:, 0:nsub, :], in_=x[tok0:tok0 + Tt, :].rearrange("(j p) d -> p j d", p=128)),
                  writes=xkeys[0:nsub], dma="xld%d" % (ti % 2))
            return xr, xkeys, nsub

        nxt = load_x(0)
        l0_pre(nxt[0], nxt[1], nxt[2], 0)
        for ti, (tok0, nsub) in enumerate(tiles):
            Tt = nsub * 128
            xr, xkeys = nxt[0], nxt[1]
            l0_tile(xr, xkeys, nsub, 0)
            S.add("pool", lambda e, xr=xr, tok0=tok0, nsub=nsub, Tt=Tt: e.dma_start(
                out=h1s[tok0:tok0 + Tt, :].rearrange("(j p) d -> p j d", p=128), in_=xr[:, 0:nsub, :]),
                reads=xkeys[0:nsub], writes=["h1s%d" % ti], dma="xst%d" % (ti % 2))

            nxt_box = [None]

            def hook_a1(ti=ti, nxt_box=nxt_box):
                if ti + 1 < len(tiles):
                    r = load_x(ti + 1)
                    nxt_box[0] = r + (pre_s(r[0], r[1], r[2], gpre0),)

            def hook_a2(ti=ti, nxt_box=nxt_box):
                if ti + 1 < len(tiles):
                    r = nxt_box[0]
                    pre_t(r[2], 0, r[3])
            l1_tile(xr, xkeys, nsub, False, ti == 0, b, 1, False, hook_a1, hook_a2)
            nxt = nxt_box[0]
        S.add("sp", lambda e: e.dma_start(out=sbn[:, :], in_=hT[:]), reads=hkeys, writes=["sbn"], dma="sbn")
        S.add("sp", lambda e: e.dma_start(out=dbn[:, :], in_=dcumt[:]), reads=["dcum"], writes=["dbn"], dma="dbn")
        RG = [[0, 1, 2, 3], [4, 5, 6, 7]]
        S.add("pool", lambda e: e.collective_compute("AllGather", ALU.bypass, replica_groups=RG, ins=[sbn[:, :]], outs=[sgt[:, :]]),
              reads=["sbn"], writes=["sgt"], dma="cc", inc=1)
        S.add("pool", lambda e: e.collective_compute("AllGather", ALU.bypass, replica_groups=RG, ins=[dbn[:, :]], outs=[dgt[:, :]]),
              reads=["dbn"], writes=["dgt"], dma="cc2", inc=1)
        S.barrier()
        S.add("sp", lambda e: e.dma_start(out=gpost[:], in_=post1.partition_broadcast(128)), writes=["gpost"], dma="gpost")
        load_wout(c, wout, w1_out)
        S.add("pool", lambda e: e.memset(carry1[:], 0.0), writes=["carry%d" % i for i in range(32)])
        S.add("dve", lambda e: e.memset(hT[:], 0.0), writes=hkeys)
        S.add("sp", lambda e: e.dma_start(out=dstage[:], in_=dgt.rearrange("(k p) h -> p k h", p=128)), reads=["dgt"], writes=["dstage"], dma="dstage")
        for k in range(3):
            S.add("sp", lambda e, k=k: e.dma_start(out=sstage[:], in_=sgt[k * 128:(k + 1) * 128, :]), reads=["sgt"], writes=["sstage"], dma="sstage")
            S.add("dve", lambda e, k=k: e.tensor_scalar(out=fac[:], in0=dstage[:, k, 0:NH], scalar1=cm[:, k:k + 1],
                                                        scalar2=cm[:, 4 + k:5 + k], op0=ALU.mult, op1=ALU.add),
                  reads=["dstage", "cm"], writes=["fac"])
            S.add("dve", lambda e: e.tensor_tensor(out=hT[:].rearrange("p (h q) -> p h q", h=NH),
                                                   in0=hT[:].rearrange("p (h q) -> p h q", h=NH),
                                                   in1=fac[:].unsqueeze(2).to_broadcast([128, NH, 64]), op=ALU.mult),
                  reads=["fac"], writes=hkeys)
            S.add("dve", lambda e, k=k: e.scalar_tensor_tensor(out=hT[:], in0=sstage[:, 0:DI], scalar=cm[:, k:k + 1], in1=hT[:],
                                                               op0=ALU.mult, op1=ALU.add),
                  reads=["sstage", "cm"], writes=hkeys)
        S.add("act", lambda e: e.activation(out=hTb[:], in_=hT[:], func=AF.Copy), reads=hkeys, writes=["hTb%d" % g for g in range(NG)])
        def load_h(ti):
            tok0, nsub = tiles[ti]
            Tt = nsub * 128
            xr = xres[ti % 2]
            xkeys = ["xres%d_%d" % (ti % 2, j) for j in range(NSUB)]
            S.add("sp", lambda e: e.dma_start(out=xr[:, 0:nsub, :], in_=h1s[tok0:tok0 + Tt, :].rearrange("(j p) d -> p j d", p=128)),
                  reads=["h1s%d" % ti], writes=xkeys[0:nsub], dma="xld%d" % (ti % 2))
            return xr, xkeys, nsub

        nxt = load_h(0)
        l1_pre(nxt[0], nxt[1], nxt[2], 0)
        for ti, (tok0, nsub) in enumerate(tiles):
            Tt = nsub * 128
            xr, xkeys = nxt[0], nxt[1]
            nxt_box = [None]

            def hook_b1(ti=ti, nxt_box=nxt_box):
                if ti + 1 < len(tiles):
                    r = load_h(ti + 1)
                    nxt_box[0] = r + (pre_s(r[0], r[1], r[2], gpre1),)

            def hook_b2(ti=ti, nxt_box=nxt_box):
                if ti + 1 < len(tiles):
                    r = nxt_box[0]
                    pre_t(r[2], (ti + 1) % 2, r[3])
            l1_tile(xr, xkeys, nsub, True, ti == 0, bB, ti % 2, True, hook_b1, hook_b2)
            nxt = nxt_box[0]
            if ti > 0:
                S.add("pool", lambda e, xr=xr, tok0=tok0, nsub=nsub, Tt=Tt: e.dma_start(
                    out=out[tok0 - HALO:tok0 - HALO + Tt, :].rearrange("(j p) d -> p j d", p=128), in_=xr[:, 0:nsub, :]),
                    reads=xkeys[0:nsub], writes=["outd"], dma="xst%d" % (ti % 2))
        S.add("sp", lambda e: e.nop(), reads=["outd"])
        S.emit(nc, st)
    return nc


TF = 256
_PROGS = {}


def fused_maps(x, pre_norm, post_norm, sc_w_in, sc_conv_w, sc_w_out, ssd_w_in, ssd_conv_w, ssd_conv_b,
               ssd_dt_bias, ssd_a_log, ssd_d_skip, ssd_norm, ssd_w_out, NTOK):
    B, L, _ = x.shape
    cpb = L // NTOK
    cwh = np.ascontiguousarray(sc_conv_w[0].reshape(3, NCH, 128).transpose(2, 1, 0)).reshape(128, NCH * 3)
    pm = l1_param_maps(ssd_w_in, ssd_conv_w, ssd_conv_b, ssd_dt_bias, ssd_a_log)
    ex = l1_full_extra(post_norm, ssd_d_skip, ssd_norm, ssd_w_out)
    shared = {"pre0": np.ascontiguousarray(pre_norm[0]), "post0": np.ascontiguousarray(post_norm[0]),
              "w0_in": np.ascontiguousarray(sc_w_in[0]), "cw0h": cwh, "w0_out": np.ascontiguousarray(sc_w_out[0]),
              "pre1": np.ascontiguousarray(pre_norm[1]), "post1": ex["post"], "w1_in": pm["w_in"], "cw1h": pm["cw1h"],
              "cb1h": pm["cb1h"], "dtbh": pm["dtbh"], "alogh": pm["alogh"], "dskh": ex["dskh"], "nwh": ex["nwh"],
              "w1_out": ex["w_out"]}
    maps = []
    for core in range(B * cpb):
        bi, ci = divmod(core, cpb)
        cm = np.zeros((128, 8), np.float32)
        for k in range(4):
            cm[:, k] = 1.0 if k < ci else 0.0
        cm[:, 4:8] = 1.0 - cm[:, 0:4]
        m = dict(shared)
        m["x"] = core_tokens(x[bi], ci * NTOK, NTOK)
        m["cmask"] = cm
        maps.append(m)
    return maps


def kernel(x, pre_norm, post_norm, sc_w_in, sc_conv_w, sc_w_out, ssd_w_in, ssd_conv_w, ssd_conv_b,
           ssd_dt_bias, ssd_a_log, ssd_d_skip, ssd_norm, ssd_w_out):
    f = lambda a: np.ascontiguousarray(np.asarray(a), dtype=np.float32)
    args = [f(a) for a in (x, pre_norm, post_norm, sc_w_in, sc_conv_w, sc_w_out, ssd_w_in, ssd_conv_w, ssd_conv_b,
                           ssd_dt_bias, ssd_a_log, ssd_d_skip, ssd_norm, ssd_w_out)]
    B, L, _ = args[0].shape
    ncores = 8
    NTOK = B * L // ncores
    if ("fused", NTOK) not in _PROGS:
        _PROGS[("fused", NTOK)] = build_fused(NTOK, TF)
    nc = _PROGS[("fused", NTOK)]
    res = run_bass_kernel_spmd(nc, fused_maps(*args, NTOK), core_ids=list(range(ncores)))
    out = np.concatenate([r["out"] for r in res.results], 0).reshape(B, L, D)
    return out.astype(np.float32)
```

```python
from contextlib import ExitStack
import numpy as np
import concourse.bass as bass
import concourse.mybir as mybir
from concourse.bass_utils import run_bass_kernel_spmd

F32 = mybir.dt.float32
BF16 = mybir.dt.bfloat16
AF = mybir.ActivationFunctionType
ALU = mybir.AluOpType

D = 1024
KD = 8
DI = 2048
NCH = 16
NH = 32
NG = 8
HALO = 128
EPS = 1e-6
L1IN = 6176


class Sched:
    ENGS = ("pe", "act", "dve", "pool", "sp")

    def __init__(self):
        self.ops = {e: [] for e in self.ENGS}
        self.last_w = {}
        self.readers = {}
        self.dma_cnt = {}
        self.dma_inc = {}

    def barrier(self):
        toks = set()
        for e in self.ENGS:
            for i in range(len(self.ops[e]) - 1, -1, -1):
                if self.ops[e][i]["dma"] is None:
                    toks.add(("eng", e, i))
                    break
        for k, cnt in self.dma_cnt.items():
            toks.add(("dma", k, cnt))
        self.pending = {e: set(toks) for e in self.ENGS}

    def add(self, eng, fn, reads=(), writes=(), dma=None, inc=16):
        deps = set()
        if getattr(self, "pending", None) and self.pending.get(eng):
            deps |= self.pending[eng]
            self.pending[eng] = set()
        for r in reads:
            t = self.last_w.get(r)
            if t is not None:
                deps.add(t)
        for w in writes:
            t = self.last_w.get(w)
            if t is not None:
                deps.add(t)
            for t in self.readers.get(w, ()):
                deps.add(t)
        idx = len(self.ops[eng])
        if dma is not None:
            c = self.dma_cnt.get(dma, 0) + 1
            self.dma_cnt[dma] = c
            tok = ("dma", dma, c)
        else:
            tok = ("eng", eng, idx)
        if eng == "pe":
            deps = {d for d in deps if not (d[0] == "eng" and d[1] == "pe")}
        deps.discard(tok)
        if dma is not None:
            self.dma_inc[dma] = inc
        self.ops[eng].append(dict(fn=fn, deps=deps, tok=tok, dma=dma))
        for r in reads:
            lst = self.readers.setdefault(r, [])
            if tok[0] == "eng":
                lst[:] = [t for t in lst if not (t[0] == "eng" and t[1] == eng)]
            lst.append(tok)
        for w in writes:
            self.last_w[w] = tok
            self.readers[w] = []
        return tok

    def emit(self, nc, stack):
        sig = {e: [False] * len(self.ops[e]) for e in self.ENGS}
        for e in self.ENGS:
            for op in self.ops[e]:
                for d in op["deps"]:
                    if d[0] == "eng":
                        sig[d[1]][d[2]] = True
        cum = {}
        for e in self.ENGS:
            c = 0
            arr = []
            for s in sig[e]:
                if s:
                    c += 1
                arr.append(c)
            cum[e] = arr
        esem = {e: stack.enter_context(nc.semaphore("s_" + e)) for e in self.ENGS}
        dsem = {k: stack.enter_context(nc.semaphore("d_" + str(k))) for k in self.dma_cnt}
        block = stack.enter_context(nc.Block())

        def run(e, engobj):
            known = {}
            for i, op in enumerate(self.ops[e]):
                need = {}
                for d in op["deps"]:
                    if d[0] == "eng":
                        s, v = esem[d[1]], cum[d[1]][d[2]]
                    else:
                        s, v = dsem[d[1]], self.dma_inc[d[1]] * d[2]
                    if need.get(s.name, (None, 0))[1] < v:
                        need[s.name] = (s, v)
                for nm, (s, v) in need.items():
                    if known.get(nm, 0) >= v:
                        continue
                    engobj.wait_ge(s, v)
                    known[nm] = v
                ins = op["fn"](engobj)
                if op["dma"] is not None:
                    ins.then_inc(dsem[op["dma"]], self.dma_inc[op["dma"]])
                elif sig[e][i]:
                    ins.then_inc(esem[e], 1)

        block.tensor(lambda eng: run("pe", eng))
        block.scalar(lambda eng: run("act", eng))
        block.vector(lambda eng: run("dve", eng))
        block.gpsimd(lambda eng: run("pool", eng))
        block.sync(lambda eng: run("sp", eng))


class Ctx:
    def __init__(self, nc, st):
        self.nc = nc
        self.st = st
        self.S = Sched()

    def sb(self, name, shape, dt):
        return self.st.enter_context(self.nc.sbuf_tensor(name, shape, dt))

    def ps(self, name, shape, dt):
        return self.st.enter_context(self.nc.psum_tensor(name, shape, dt))

    def din(self, name, shape, dt=F32):
        return self.nc.dram_tensor(name, list(shape), dt, kind="ExternalInput").ap()

    def dout(self, name, shape, dt=F32):
        return self.nc.dram_tensor(name, list(shape), dt, kind="ExternalOutput").ap()


def make_ident(c, t, key):
    S = c.S
    S.add("pool", lambda e: e.memset(t[:], 0.0), writes=[key])
    S.add("pool", lambda e: e.affine_select(out=t[:], in_=t[:], pattern=[[-1, 128]],
                                            compare_op=ALU.not_equal, fill=1.0, base=0,
                                            channel_multiplier=1), reads=[key], writes=[key])


def make_tri(c, t, key):
    S = c.S
    S.add("pool", lambda e: e.memset(t[:], 1.0), writes=[key])
    S.add("pool", lambda e: e.affine_select(out=t[:], in_=t[:], pattern=[[1, 128]],
                                            compare_op=ALU.is_ge, fill=0.0, base=0,
                                            channel_multiplier=-1), reads=[key], writes=[key])


def emit_prenorm_stats(c, xres_j, xkey, gpre, bufs, uid):
    S = c.S
    stat, junk, utok, eps = bufs["stat"], bufs["junk"], bufs["utok"], bufs["eps"]
    sl = uid % 4
    st_ = stat[sl]
    sk = "stat%d" % sl
    ut = utok[uid % 2]
    uk = "utok%d" % (uid % 2)
    S.add("act", lambda e: e.activation(out=junk[:], in_=xres_j, func=AF.Square, accum_out=st_[:, 0:1]),
          reads=[xkey], writes=["junk", sk])
    S.add("act", lambda e: e.activation(out=st_[:, 1:2], in_=st_[:, 0:1], func=AF.Ln, scale=1.0 / D,
                                        bias=eps[:, 0:1]), reads=[sk, "eps"], writes=[sk])
    S.add("act", lambda e: e.activation(out=st_[:, 2:3], in_=st_[:, 1:2], func=AF.Exp, scale=-0.5),
          reads=[sk], writes=[sk])
    S.add("dve", lambda e: e.scalar_tensor_tensor(out=ut[:], in0=xres_j, scalar=st_[:, 2:3], in1=gpre[:],
                                                  op0=ALU.mult, op1=ALU.mult),
          reads=[xkey, sk, "gpre"], writes=[uk])


def emit_prenorm_T(c, uT, j, bufs, uid, ukp="uT"):
    S = c.S
    utok, tp, identb = bufs["utok"], bufs["tp"], bufs["identb"]
    ut = utok[uid % 2]
    uk = "utok%d" % (uid % 2)
    for kc in range(KD):
        S.add("pe", lambda e, kc=kc: e.transpose(tp[:, kc * 128:(kc + 1) * 128], ut[:, kc * 128:(kc + 1) * 128], identb[:]),
              reads=[uk, "identb"], writes=["tp"])
    S.add("act", lambda e: e.activation(out=uT[:, :, j * 128:(j + 1) * 128],
                                        in_=tp[:, 0:1024].rearrange("p (k t) -> p k t", k=KD), func=AF.Copy),
          reads=[uk], writes=["tp", "%s%d" % (ukp, j)])


def emit_prenorm(c, xres_j, xkey, gpre, uT, j, bufs, uid, ukp="uT"):
    emit_prenorm_stats(c, xres_j, xkey, gpre, bufs, uid)
    emit_prenorm_T(c, uT, j, bufs, uid, ukp)


def emit_outproj_post(c, lhs_buf, lhs_keys, wout, gpost, xres_j, xkey, j, bufs, uid, part=3):
    S = c.S
    stat, junk, eps, mres = bufs["stat"], bufs["junk"], bufs["eps"], bufs["mres"]
    po = bufs["po"][2 * (j % 2):2 * (j % 2) + 2] if len(bufs["po"]) >= 4 else bufs["po"]
    pok = bufs["pokeys"][2 * (j % 2):2 * (j % 2) + 2] if len(bufs["po"]) >= 4 else bufs["pokeys"]
    for hf in range(2 if (part & 1) else 0):
        for cc in range(NCH):
            S.add("pe", lambda e, hf=hf, cc=cc: e.matmul(po[hf][:], lhsT=lhs_buf[:, cc, j * 128:(j + 1) * 128],
                                                         rhs=wout[:, cc, hf * 512:(hf + 1) * 512],
                                                         start=(cc == 0), stop=(cc == NCH - 1)),
                  reads=[lhs_keys[cc], "wout"], writes=[pok[hf]])
    if not (part & 2):
        return
    sl = uid % 4
    st_ = stat[sl]
    sk = "stat%d" % sl
    for hf in range(2):
        S.add("act", lambda e, hf=hf: e.activation(out=junk[:, 0:512], in_=po[hf][:], func=AF.Square,
                                                   accum_out=st_[:, 3 + hf:4 + hf]),
              writes=[pok[hf], "junk", sk])
    S.add("dve", lambda e: e.tensor_tensor(out=st_[:, 5:6], in0=st_[:, 3:4], in1=st_[:, 4:5], op=ALU.add),
          reads=[sk], writes=[sk])
    S.add("act", lambda e: e.activation(out=st_[:, 6:7], in_=st_[:, 5:6], func=AF.Ln, scale=1.0 / D,
                                        bias=eps[:, 0:1]), reads=[sk, "eps"], writes=[sk])
    S.add("act", lambda e: e.activation(out=st_[:, 7:8], in_=st_[:, 6:7], func=AF.Exp, scale=-0.5),
          reads=[sk], writes=[sk])
    for hf in range(2):
        S.add("dve", lambda e, hf=hf: e.scalar_tensor_tensor(out=mres[:, hf * 512:(hf + 1) * 512], in0=po[hf][:],
                                                             scalar=st_[:, 7:8], in1=gpost[:, hf * 512:(hf + 1) * 512],
                                                             op0=ALU.mult, op1=ALU.mult),
              reads=[sk, "gpost"], writes=[pok[hf], "mres%d" % hf])
    S.add("pool", lambda e: e.tensor_tensor(out=xres_j, in0=xres_j, in1=mres[:], op=ALU.add),
          reads=["mres0", "mres1", xkey], writes=[xkey])


def common_bufs(c, po=None, pokeys=None):
    b = {}
    b["stat"] = [c.sb("stat%d" % i, [128, 8], F32) for i in range(4)]
    b["junk"] = c.sb("junk", [128, 1024], BF16)
    b["utok"] = [c.sb("utok%d" % i, [128, 1024], BF16) for i in range(2)]
    b["tp"] = c.ps("tp", [128, 1024], BF16)
    b["identb"] = c.sb("identb", [128, 128], BF16)
    b["eps"] = c.sb("eps", [128, 1], F32)
    b["mres"] = c.sb("mres", [128, 1024], F32)
    if po is None:
        po = [c.ps("po%d" % i, [128, 512], F32) for i in range(2)]
        pokeys = ["po0", "po1"]
    b["po"] = po
    b["pokeys"] = pokeys
    make_ident(c, b["identb"], "identb")
    c.S.add("dve", lambda e: e.memset(b["eps"][:], EPS), writes=["eps"])
    return b


def load_wout(c, wout_sb, w_out_dram):
    src = w_out_dram.rearrange("(c p) n -> p c n", p=128)
    for q in range(4):
        c.S.add("pool", lambda e, q=q: e.dma_start(out=wout_sb[:, q * 4:(q + 1) * 4, :], in_=src[:, q * 4:(q + 1) * 4, :]),
                writes=["wout"], dma="wout%d" % q)


def core_tokens(xflat_b, start, NTOK):
    out = np.zeros((HALO + NTOK, xflat_b.shape[1]), np.float32)
    lo = max(0, start - HALO)
    out[HALO - (start - lo):] = xflat_b[lo:start + NTOK]
    return out


def l1_param_maps(ssd_w_in, ssd_conv_w, ssd_conv_b, ssd_dt_bias, ssd_a_log):
    cw1h = np.ascontiguousarray(ssd_conv_w[0].reshape(4, 32, 128).transpose(2, 1, 0)).reshape(128, 128)
    cb1h = np.ascontiguousarray(ssd_conv_b[0].reshape(32, 128).T)
    dtbh = np.zeros((128, 1), np.float32)
    dtbh[:NH, 0] = ssd_dt_bias[0]
    alogh = np.zeros((128, 1), np.float32)
    alogh[:NH, 0] = ssd_a_log[0]
    return {"w_in": np.ascontiguousarray(ssd_w_in[0]), "cw1h": cw1h, "cb1h": cb1h, "dtbh": dtbh, "alogh": alogh}


def l1_full_extra(post_norm, ssd_d_skip, ssd_norm, ssd_w_out):
    dskh = np.ascontiguousarray(np.repeat(ssd_d_skip[0].reshape(NCH, 2), 64, axis=1).T).astype(np.float32)
    nwh = np.ascontiguousarray(ssd_norm[0].reshape(NCH, 128).T)
    return {"post": np.ascontiguousarray(post_norm[1]), "dskh": dskh, "nwh": nwh,
            "w_out": np.ascontiguousarray(ssd_w_out[0])}


def build_fused(NTOK, T):
    nc = bass.Bass("TRN2", target_bir_lowering=False)
    with ExitStack() as st:
        c = Ctx(nc, st)
        S = c.S
        NSUB = T // 128
        x = c.din("x", [HALO + NTOK, D])
        pre0 = c.din("pre0", [D])
        post0 = c.din("post0", [D])
        w0_in = c.din("w0_in", [D, 4 * DI])
        cw0h = c.din("cw0h", [128, NCH * 3])
        w0_out = c.din("w0_out", [DI, D])
        pre1 = c.din("pre1", [D])
        post1 = c.din("post1", [D])
        w1_in = c.din("w1_in", [D, L1IN])
        cw1h = c.din("cw1h", [128, 32 * 4])
        cb1h = c.din("cb1h", [128, 32])
        dtbh = c.din("dtbh", [128, 1])
        alogh = c.din("alogh", [128, 1])
        dskh = c.din("dskh", [128, NCH])
        nwh = c.din("nwh", [128, NCH])
        w1_out = c.din("w1_out", [DI, D])
        cmask = c.din("cmask", [128, 8])
        out = c.dout("out", [NTOK, D])
        w0s = nc.dram_tensor("w0s", [NCH, 128, KD * 4 * 128], BF16, kind="Internal").ap()
        w1s = nc.dram_tensor("w1s", [12, 128, KD * 512], BF16, kind="Internal").ap()
        h1s = nc.dram_tensor("h1s", [HALO + NTOK, D], F32, kind="Internal").ap()
        sbn = nc.dram_tensor("sbn", [128, DI], F32, kind="Internal").ap()
        sgt = nc.dram_tensor("sgt", [4 * 128, DI], F32, kind="Internal").ap()
        dbn = nc.dram_tensor("dbn", [128, 64], F32, kind="Internal").ap()
        dgt = nc.dram_tensor("dgt", [4 * 128, 64], F32, kind="Internal").ap()

        pin = [c.ps("pin%d" % i, [128, 512], F32) for i in range(4)]
        b = common_bufs(c)
        po = b["po"]
        identb, eps, tp = b["identb"], b["eps"], b["tp"]
        pC = c.ps("pC", [128, 512], F32)
        pS = pC[:, 384:512]
        pA = po[0]
        pAk = "po0"
        pA2 = po[1]
        pA2k = "po1"
        pY = [pin[2], pin[3]]
        pYk = ["pin2", "pin3"]
        bB = dict(b)
        bB["po"] = [pin[2], pin[3], po[0], po[1]]
        bB["pokeys"] = ["pin2", "pin3", "po0", "po1"]
        bA = dict(b)
        bA["po"] = [po[0], po[1], pin[0], pin[1]]
        bA["pokeys"] = ["po0", "po1", "pin0", "pin1"]

        gpre0 = c.sb("gpre0", [128, D], F32)
        gpre1 = c.sb("gpre1", [128, D], F32)
        gpost = c.sb("gpost", [128, D], F32)
        cw0 = c.sb("cw0", [128, NCH * 3], F32)
        carry0 = c.sb("carry0", [128, NCH, 2], F32)
        wout = c.sb("wout", [128, NCH, D], BF16)
        xres = [c.sb("xres%d" % i, [128, NSUB, D], F32) for i in range(2)]
        uTs = [c.sb("uT%d" % i, [128, KD, T], BF16) for i in range(2)]
        wbuf = [c.sb("wbuf%d" % i, [128, KD * 512], BF16) for i in range(3)]
        yT = c.sb("yT", [128, NCH, T], BF16)
        tmp = [c.sb("tmp%d" % i, [128, T], F32) for i in range(8)]
        cv = [c.sb("cv%d" % i, [128, T + 2], F32) for i in range(2)]
        cw1 = c.sb("cw1", [128, 32 * 4], F32)
        cb1 = c.sb("cb1", [128, 32], F32)
        dtb = c.sb("dtb", [128, 1], F32)
        acol = c.sb("acol", [128, 1], F32)
        onec = c.sb("onec", [128, 1], F32)
        identf = c.sb("identf", [128, 128], F32)
        trif = c.sb("trif", [128, 128], F32)
        onesf = c.sb("onesf", [128, 128], F32)
        wdt = c.sb("wdt", [128, KD, NH], BF16)
        carry1 = c.sb("carry1", [128, 32, 3], F32)
        xT = c.sb("xT", [128, NCH, T], BF16)
        BT = c.sb("BT", [128, NG, T], BF16)
        CT = c.sb("CT", [128, NG, T], BF16)
        xp = [c.sb("xp%d" % i, [128, T + 3], F32) for i in range(4)]
        acc = [c.sb("acc%d" % i, [128, T], F32) for i in range(4)]
        l1banks = [(pin[0], "pin0"), (pin[1], "pin1"), (po[0], "po0"), (po[1], "po1")]
        dtT = c.sb("dtT", [128, T], F32)
        adtT = c.sb("adtT", [128, T], F32)
        cs = [c.sb("cs%d" % i, [128, 8, NH], F32) for i in range(NSUB)]
        hl = [c.sb("hl%d" % i, [128, 4, NH], BF16) for i in range(NSUB)]
        xdt = [c.sb("xdt%d" % i, [128, 256], BF16) for i in range(2)]
        xB = [c.sb("xB%d" % i, [128, 384], BF16) for i in range(2)]
        xs = [c.sb("xs%d" % i, [128, 256], BF16) for i in range(2)]
        hT = c.sb("hT", [128, DI], F32)
        dcumt = c.sb("dcumt", [128, 64], F32)
        dcum = dcumt[:, 0:NH]
        dstage = c.sb("dstage", [128, 4, 64], F32)
        dsk = c.sb("dsk", [128, NCH], F32)
        nw = c.sb("nw", [128, NCH], F32)
        cm = c.sb("cm", [128, 8], F32)
        trib = c.sb("trib", [128, 128], BF16)
        onesb = c.sb("onesb", [128, 128], BF16)
        hTb = c.sb("hTb", [128, DI], BF16)
        dec4 = [c.sb("dec%d" % i, [128, 512], F32) for i in range(2)]
        ebc4 = [c.sb("ebc%d" % i, [128, 512], F32) for i in range(2)]
        m14 = [c.sb("m1%d" % i, [128, 512], F32) for i in range(2)]
        MT4 = [c.sb("MT%d" % i, [128, 512], BF16) for i in range(2)]
        CsT4 = [c.sb("CsT%d" % i, [128, 512], BF16) for i in range(2)]
        sstage = c.sb("sstage", [128, DI], F32)
        fac = c.sb("fac", [128, NH], F32)
        sq = [c.sb("sq%d" % i, [128, T], BF16) for i in range(2)]
        hkeys = ["hT%d" % g for g in range(NG)]

        w0src = w0_in.rearrange("(kc p) n -> p kc n", p=128)
        for ch in range(NCH):
            dst = w0s[ch].rearrange("p (kc w n) -> p kc w n", kc=KD, w=4)
            for which in range(4):
                col0 = which * DI + ch * 128
                S.add("pool", lambda e, dst=dst, which=which, col0=col0: e.dma_start(out=dst[:, :, which, :], in_=w0src[:, :, col0:col0 + 128]),
                      writes=["w0sraw%d_%d" % (ch, which)], dma="cast%d" % ((ch * 4 + which) % 4))
        w1src = w1_in.rearrange("(kc p) n -> p kc n", p=128)
        for t_, src_, k_ in ((gpre0, pre0, "gpre0"), (gpre1, pre1, "gpre1"), (gpost, post0, "gpost")):
            S.add("sp", lambda e, t_=t_, src_=src_: e.dma_start(out=t_[:], in_=src_.partition_broadcast(128)), writes=[k_], dma=k_)
        for t_, src_, k_ in ((cw0, cw0h, "cw0"), (cw1, cw1h, "cw1"), (cb1, cb1h, "cb1"), (dtb, dtbh, "dtb"), (acol, alogh, "acol"),
                             (dsk, dskh, "dsk"), (nw, nwh, "nw"), (cm, cmask, "cm")):
            S.add("sp", lambda e, t_=t_, src_=src_: e.dma_start(out=t_[:], in_=src_[:, :]), writes=[k_], dma=k_)
        S.add("pool", lambda e: e.dma_start(out=wdt[:], in_=w1src[:, :, L1IN - NH:L1IN]), writes=["wdt"], dma="wdt")
        load_wout(c, wout, w0_out)
        S.add("act", lambda e: e.activation(out=acol[:], in_=acol[:], func=AF.Exp), reads=["acol"], writes=["acol"])
        S.add("dve", lambda e: e.tensor_scalar(out=acol[:], in0=acol[:], scalar1=-1.0, scalar2=None, op0=ALU.mult),
              reads=["acol"], writes=["acol"])
        S.add("dve", lambda e: e.memset(onec[:], 1.0), writes=["onec"])
        S.add("pool", lambda e: e.memset(onesf[:], 1.0), writes=["onesf"])
        S.add("pool", lambda e: e.memset(onesb[:], 1.0), writes=["onesb"])
        S.add("pool", lambda e: e.memset(carry0[:], 0.0), writes=["carry0_%d" % i for i in range(NCH)])
        S.add("pool", lambda e: e.memset(carry1[:], 0.0), writes=["carry%d" % i for i in range(32)])
        make_ident(c, identf, "identf")
        make_tri(c, trif, "trif")
        make_tri(c, trib, "trib")
        S.add("dve", lambda e: e.memset(hT[:], 0.0), writes=hkeys)
        S.add("dve", lambda e: e.memset(dcumt[:], 1.0), writes=["dcum"])
        for gi in range(12):
            S.add("pool", lambda e, gi=gi: e.dma_start(out=w1s[gi].rearrange("p (kc n) -> p kc n", kc=KD), in_=w1src[:, :, gi * 512:(gi + 1) * 512]),
                  writes=["w1sraw%d" % gi], dma="cast%d" % (gi % 4))

        state = dict(uid=0, wcnt=0, ocnt=0, gj=0, w0ready=False, w1ready=False)

        def cast_ready(which_layer):
            if which_layer == 0 and not state["w0ready"]:
                state["w0ready"] = True
                S.add("sp", lambda e: e.nop(), reads=["w0sraw%d_%d" % (ch, w) for ch in range(NCH) for w in range(4)],
                      writes=["w0s%d" % ch for ch in range(NCH)])
            if which_layer == 1 and not state["w1ready"]:
                state["w1ready"] = True
                S.add("sp", lambda e: e.nop(), reads=["w1sraw%d" % gi for gi in range(12)] + ["w0sraw%d_%d" % (ch, w) for ch in range(NCH) for w in range(4)],
                      writes=["w1s%d" % gi for gi in range(12)])

        def pre_s(xr, xkeys, nsub, gp):
            uids = []
            for j in range(nsub):
                emit_prenorm_stats(c, xr[:, j, :], xkeys[j], gp, b, state["uid"])
                uids.append(state["uid"])
                state["uid"] += 1
            return uids

        def pre_t(nsub, ub, uids):
            for j in range(nsub):
                emit_prenorm_T(c, uTs[ub], j, b, uids[j], ukp="uT%d_" % ub)

        def l0_pre(xr, xkeys, nsub, ub):
            pre_t(nsub, ub, pre_s(xr, xkeys, nsub, gpre0))

        def l0_tile(xr, xkeys, nsub, ub):
            Tt = nsub * 128
            uT = uTs[ub]
            ukeys = ["uT%d_%d" % (ub, j) for j in range(nsub)]
            cast_ready(0)
            for ch in range(NCH):
                ws = state["wcnt"] % 3
                state["wcnt"] += 1
                wbv = wbuf[ws][:].rearrange("p (kc w n) -> p kc w n", kc=KD, w=4)
                S.add("sp", lambda e, ws=ws, ch=ch: e.dma_start(out=wbuf[ws][:], in_=w0s[ch]), reads=["w0s%d" % ch], writes=["wb%d" % ws], dma="wb%d" % ws)
                pr = ch % 2
                for which, bank in ((2, 0), (3, 1), (0, 2), (1, 3)):
                    for kc in range(KD):
                        S.add("pe", lambda e, wbv=wbv, which=which, bank=bank, kc=kc, Tt=Tt: e.matmul(
                            pin[bank][:, 0:Tt], lhsT=wbv[:, kc, which, :], rhs=uT[:, kc, 0:Tt], start=(kc == 0), stop=(kc == KD - 1)),
                            reads=["wb%d" % ws] + ukeys, writes=["pin%d" % bank])
                a, bq, cq, dq, cvb = tmp[pr], tmp[2 + pr], tmp[4 + pr], tmp[6 + pr], cv[pr]
                ka, kb, kc_, kd = "tmp%d" % pr, "tmp%d" % (2 + pr), "tmp%d" % (4 + pr), "tmp%d" % (6 + pr)
                ck = "carry0_%d" % ch
                S.add("act", lambda e, a=a, Tt=Tt: e.activation(out=a[:, 0:Tt], in_=pin[0][:, 0:Tt], func=AF.Copy), writes=["pin0", ka])
                S.add("pool", lambda e, cvb=cvb, ch=ch: e.tensor_copy(out=cvb[:, 0:2], in_=carry0[:, ch, :]), reads=[ck], writes=["cvh%d" % pr])
                S.add("dve", lambda e, cvb=cvb, a=a, Tt=Tt: e.tensor_tensor(out=cvb[:, 2:2 + Tt], in0=a[:, 0:Tt], in1=pin[1][:, 0:Tt], op=ALU.mult),
                      reads=[ka], writes=["pin1", "cvb%d" % pr])
                S.add("act", lambda e, bq=bq, Tt=Tt: e.activation(out=bq[:, 0:Tt], in_=pin[2][:, 0:Tt], func=AF.Silu), writes=["pin2", kb])
                S.add("dve", lambda e, cq=cq, bq=bq, Tt=Tt: e.tensor_tensor(out=cq[:, 0:Tt], in0=bq[:, 0:Tt], in1=pin[3][:, 0:Tt], op=ALU.mult),
                      reads=[kb], writes=["pin3", kc_])
                S.add("act", lambda e, dq=dq, cvb=cvb, ch=ch, Tt=Tt: e.activation(
                    out=dq[:, 0:Tt], in_=cvb[:, 0:Tt], func=AF.Copy, scale=cw0[:, ch * 3:ch * 3 + 1]),
                    reads=["cvh%d" % pr, "cvb%d" % pr, "cw0"], writes=[kd])
                for tap in (1, 2):
                    S.add("dve", lambda e, dq=dq, cvb=cvb, ch=ch, tap=tap, Tt=Tt: e.scalar_tensor_tensor(
                        out=dq[:, 0:Tt], in0=cvb[:, tap:tap + Tt], scalar=cw0[:, ch * 3 + tap:ch * 3 + tap + 1],
                        in1=dq[:, 0:Tt], op0=ALU.mult, op1=ALU.add),
                        reads=["cvh%d" % pr, "cvb%d" % pr, "cw0", kd], writes=[kd])
                S.add("pool", lambda e, cvb=cvb, ch=ch, Tt=Tt: e.tensor_copy(out=carry0[:, ch, :], in_=cvb[:, Tt:Tt + 2]),
                      reads=["cvb%d" % pr], writes=[ck])
                S.add("pool", lambda e, dq=dq, cq=cq, ch=ch, Tt=Tt: e.tensor_tensor(out=yT[:, ch, 0:Tt], in0=dq[:, 0:Tt], in1=cq[:, 0:Tt], op=ALU.mult),
                      reads=[kd, kc_], writes=["yT%d" % ch])
            ykeys = ["yT%d" % ch for ch in range(NCH)]
            ouids = []
            for j in range(nsub):
                emit_outproj_post(c, yT, ykeys, wout, gpost, xr[:, j, :], xkeys[j], j, bA, state["uid"], part=1)
                ouids.append(state["uid"])
                state["uid"] += 1
            for j in range(nsub):
                emit_outproj_post(c, yT, ykeys, wout, gpost, xr[:, j, :], xkeys[j], j, bA, ouids[j], part=2)

        def l1_pre(xr, xkeys, nsub, ub):
            pre_t(nsub, ub, pre_s(xr, xkeys, nsub, gpre1))

        def l1_tile(xr, xkeys, nsub, full, halo, bb, ub, pre_done, hook1, hook2):
            Tt = nsub * 128
            uT = uTs[ub]
            if not pre_done:
                l1_pre(xr, xkeys, nsub, ub)
            ukeys = ["uT%d_%d" % (ub, j) for j in range(nsub)]
            groups = list(range(12)) if (full and not halo) else ([4, 5, 6, 7, 8, 9, 10, 11] if full else [4, 5, 6, 7, 8, 9])
            def emit_dt_stats():
                pb, pk = l1banks[state["ocnt"] % 4]
                state["ocnt"] += 1
                for kc in range(KD):
                    S.add("pe", lambda e, kc=kc, pb=pb, Tt=Tt: e.matmul(pb[0:NH, 0:Tt], lhsT=wdt[:, kc, :], rhs=uT[:, kc, 0:Tt],
                                                                      start=(kc == 0), stop=(kc == KD - 1)), reads=["wdt"] + ukeys, writes=[pk])
                S.add("act", lambda e, pb=pb, Tt=Tt: e.activation(out=dtT[0:NH, 0:Tt], in_=pb[0:NH, 0:Tt], func=AF.Exp, bias=dtb[0:NH, 0:1], scale=1.0),
                      reads=["dtb"], writes=[pk, "dtT"])
                S.add("act", lambda e, Tt=Tt: e.activation(out=dtT[0:NH, 0:Tt], in_=dtT[0:NH, 0:Tt], func=AF.Ln, bias=onec[0:NH, 0:1], scale=1.0),
                      reads=["dtT", "onec"], writes=["dtT"])
                S.add("dve", lambda e, Tt=Tt: e.tensor_scalar(out=adtT[0:NH, 0:Tt], in0=dtT[0:NH, 0:Tt], scalar1=acol[0:NH, 0:1], scalar2=None, op0=ALU.mult),
                      reads=["dtT", "acol"], writes=["adtT"])

            def emit_stats():
                for j in range(nsub):
                    js = slice(j * 128, (j + 1) * 128)
                    csj, hlj = cs[j], hl[j]
                    ckey, hkey = "cs%d" % j, "hl%d" % j
                    S.add("pe", lambda e, js=js: e.transpose(pS[:, 0:NH], dtT[0:NH, js], identf[0:NH, 0:NH]), reads=["dtT", "identf"], writes=["pC"])
                    S.add("pe", lambda e, js=js: e.transpose(pS[:, NH:2 * NH], adtT[0:NH, js], identf[0:NH, 0:NH]), reads=["adtT", "identf"], writes=["pC"])
                    S.add("act", lambda e, csj=csj: e.activation(out=csj[:, 0:2, :], in_=pS[:, 0:2 * NH].rearrange("p (a h) -> p a h", a=2), func=AF.Copy),
                          writes=["pC", ckey])
                    S.add("pe", lambda e, csj=csj: e.matmul(pS[:, 2 * NH:3 * NH], lhsT=trif[:], rhs=csj[:, 1, :], start=True, stop=True),
                          reads=[ckey, "trif"], writes=["pC"])
                    S.add("pe", lambda e, csj=csj: e.matmul(pS[:, 3 * NH:4 * NH], lhsT=onesf[:], rhs=csj[:, 1, :], start=True, stop=True),
                          reads=[ckey, "onesf"], writes=["pC"])
                    S.add("act", lambda e, csj=csj: e.activation(out=csj[:, 2, :], in_=pS[:, 2 * NH:3 * NH], func=AF.Copy), writes=["pC", ckey])
                    S.add("act", lambda e, csj=csj: e.activation(out=csj[:, 3, :], in_=pS[:, 2 * NH:3 * NH], func=AF.Copy, scale=-1.0), writes=["pC", ckey])
                    S.add("dve", lambda e, csj=csj: e.tensor_tensor(out=csj[:, 7, :], in0=pS[:, 3 * NH:4 * NH], in1=csj[:, 2, :], op=ALU.subtract),
                          writes=["pC", ckey])
                    S.add("act", lambda e, csj=csj: e.activation(out=csj[:, 6, :], in_=pS[:, 3 * NH:4 * NH], func=AF.Exp), writes=["pC", ckey])
                    S.add("act", lambda e, csj=csj: e.activation(out=csj[:, 7, :], in_=csj[:, 7, :], func=AF.Exp), reads=[ckey], writes=[ckey])
                    S.add("dve", lambda e, csj=csj: e.tensor_tensor(out=csj[:, 5, :], in0=csj[:, 7, :], in1=csj[:, 0, :], op=ALU.mult), reads=[ckey], writes=[ckey])
                    if full:
                        S.add("dve", lambda e, csj=csj, hlj=hlj: e.tensor_copy(out=hlj[:, 0, :], in_=csj[:, 1, :]), reads=[ckey], writes=[hkey])
                        S.add("dve", lambda e, csj=csj, hlj=hlj: e.tensor_tensor(out=hlj[:, 1, :], in0=csj[:, 1, :], in1=hlj[:, 0, :], op=ALU.subtract),
                              reads=[ckey, hkey], writes=[hkey])
                        S.add("dve", lambda e, csj=csj, hlj=hlj: e.tensor_copy(out=hlj[:, 2, :], in_=csj[:, 3, :]), reads=[ckey, hkey], writes=[hkey])
                        S.add("dve", lambda e, csj=csj, hlj=hlj: e.tensor_tensor(out=hlj[:, 3, :], in0=csj[:, 3, :], in1=hlj[:, 2, :], op=ALU.subtract),
                              reads=[ckey, hkey], writes=[hkey])
                    else:
                        S.add("dve", lambda e, csj=csj: e.tensor_tensor(out=dcum, in0=dcum, in1=csj[:, 6, :], op=ALU.mult), reads=[ckey, "dcum"], writes=["dcum"])

            state["ocnt"] = 0
            if not halo:
                emit_dt_stats()
            if not full:
                hook1()
            stats_after = None if halo else (3 if full else 4)
            hook1_after = 7 if full else None
            cast_ready(1)
            pend = []
            for gi in groups:
                ws = state["wcnt"] % 3
                state["wcnt"] += 1
                wbv = wbuf[ws][:].rearrange("p (kc n) -> p kc n", kc=KD)
                S.add("sp", lambda e, ws=ws, gi=gi: e.dma_start(out=wbuf[ws][:], in_=w1s[gi]), reads=["w1s%d" % gi], writes=["wb%d" % ws], dma="wb%d" % ws)
                for q in range(4):
                    o = gi * 4 + q
                    pb, pk = l1banks[state["ocnt"] % 4]
                    state["ocnt"] += 1
                    for kc in range(KD):
                        S.add("pe", lambda e, wbv=wbv, q=q, kc=kc, pb=pb, Tt=Tt: e.matmul(
                            pb[:, 0:Tt], lhsT=wbv[:, kc, q * 128:(q + 1) * 128], rhs=uT[:, kc, 0:Tt],
                            start=(kc == 0), stop=(kc == KD - 1)), reads=["wb%d" % ws] + ukeys, writes=[pk])
                    if o < 16:
                        S.add("act", lambda e, o=o, pb=pb, Tt=Tt: e.activation(out=yT[:, o, 0:Tt], in_=pb[:, 0:Tt], func=AF.Silu),
                              writes=[pk, "yT%d" % o])
                        continue
                    ci = o - 16
                    ck = "carry%d" % ci
                    if halo:
                        S.add("dve", lambda e, ci=ci, pb=pb, Tt=Tt: e.tensor_copy(out=carry1[:, ci, :], in_=pb[:, Tt - 3:Tt]), writes=[pk, ck])
                        continue
                    sl = ci % 4
                    xpb, ab = xp[sl], acc[sl]
                    S.add("act", lambda e, xpb=xpb, pb=pb, Tt=Tt: e.activation(out=xpb[:, 3:3 + Tt], in_=pb[:, 0:Tt], func=AF.Copy),
                          writes=[pk, "xpb%d" % sl])
                    S.add("pool", lambda e, xpb=xpb, ci=ci: e.tensor_copy(out=xpb[:, 0:3], in_=carry1[:, ci, :]), reads=[ck], writes=["xph%d" % sl])
                    S.add("act", lambda e, pb=pb, ab=ab, ci=ci, Tt=Tt: e.activation(
                        out=ab[:, 0:Tt], in_=pb[:, 0:Tt], func=AF.Copy, scale=cw1[:, ci * 4 + 3:ci * 4 + 4]),
                        reads=["cw1"], writes=[pk, "acc%d" % sl])
                    for tap in (0, 1, 2):
                        S.add("dve", lambda e, xpb=xpb, ab=ab, ci=ci, tap=tap, Tt=Tt: e.scalar_tensor_tensor(
                            out=ab[:, 0:Tt], in0=xpb[:, tap:tap + Tt], scalar=cw1[:, ci * 4 + tap:ci * 4 + tap + 1],
                            in1=ab[:, 0:Tt], op0=ALU.mult, op1=ALU.add),
                            reads=["xph%d" % sl, "xpb%d" % sl, "cw1", "acc%d" % sl], writes=["acc%d" % sl])
                    S.add("pool", lambda e, xpb=xpb, ci=ci, Tt=Tt: e.tensor_copy(out=carry1[:, ci, :], in_=xpb[:, Tt:Tt + 3]),
                          reads=["xpb%d" % sl], writes=[ck])
                    if ci < 16:
                        dst, dk = xT[:, ci, 0:Tt], "xT%d" % ci
                    elif ci < 24:
                        dst, dk = BT[:, ci - 16, 0:Tt], "BT%d" % (ci - 16)
                    else:
                        dst, dk = CT[:, ci - 24, 0:Tt], "CT%d" % (ci - 24)
                    pend.append(lambda dst=dst, ab=ab, ci=ci, Tt=Tt, sl=sl, dk=dk: S.add(
                        "act", lambda e: e.activation(out=dst, in_=ab[:, 0:Tt], func=AF.Silu, bias=cb1[:, ci:ci + 1], scale=1.0),
                        reads=["acc%d" % sl, "cb1"], writes=[dk]))
                    if len(pend) > 2:
                        pend.pop(0)()
                if gi == stats_after:
                    emit_stats()
                if gi == hook1_after:
                    hook1()
            while pend:
                pend.pop(0)()
            if halo:
                hook2()
                return
            assert 2 * T <= 512
            its = [(g, j) for gp in range(0, NG, 2) for j in range(nsub) for g in (gp, gp + 1)]

            def stage_a(n):
                g, j = its[n]
                js = slice(j * 128, (j + 1) * 128)
                csj, hlj = cs[j], hl[j]
                ckey, hkey = "cs%d" % j, "hl%d" % j
                sl = state["gj"] % 2
                state["gj"] += 1
                xBb, xsb, xdb = xB[sl], xs[sl], xdt[sl]
                for q in range(2):
                    S.add("pe", lambda e, q=q, g=g, js=js: e.transpose(tp[:, q * 128:(q + 1) * 128], xT[:, 2 * g + q, js], identb[:]),
                          reads=["xT%d" % (2 * g + q), "identb"], writes=["tp"])
                S.add("pe", lambda e, g=g, js=js: e.transpose(tp[:, 256:384], BT[:, g, js], identb[:]), reads=["BT%d" % g, "identb"], writes=["tp"])
                S.add("act", lambda e, xBb=xBb: e.activation(out=xBb[:], in_=tp[:, 0:384], func=AF.Copy), writes=["tp", "xB%d" % sl])
                S.add("dve", lambda e, xBb=xBb, xsb=xsb, csj=csj, g=g: e.tensor_tensor(
                    out=xsb[:].rearrange("p (h q) -> p h q", h=4), in0=xBb[:, 0:256].rearrange("p (h q) -> p h q", h=4),
                    in1=csj[:, 5, 4 * g:4 * g + 4].unsqueeze(2).to_broadcast([128, 4, 64]), op=ALU.mult),
                    reads=["xB%d" % sl, ckey], writes=["xs%d" % sl])
                if not full:
                    return sl
                S.add("dve", lambda e, xBb=xBb, xdb=xdb, csj=csj, g=g: e.tensor_tensor(
                    out=xdb[:].rearrange("p (h q) -> p h q", h=4), in0=xBb[:, 0:256].rearrange("p (h q) -> p h q", h=4),
                    in1=csj[:, 0, 4 * g:4 * g + 4].unsqueeze(2).to_broadcast([128, 4, 64]), op=ALU.mult),
                    reads=["xB%d" % sl, ckey], writes=["xdt%d" % sl])
                S.add("pe", lambda e, g=g, js=js: e.matmul(pC[:, 0:128], lhsT=BT[:, g, js], rhs=CT[:, g, js], start=True, stop=True),
                      reads=["BT%d" % g, "CT%d" % g], writes=["pC"])
                for i in range(4):
                    h = 4 * g + i
                    reg = slice(i * 128, (i + 1) * 128)
                    S.add("pe", lambda e, hlj=hlj, h=h, reg=reg: e.matmul(pA[:, reg], lhsT=hlj[:, 0, h:h + 1].to_broadcast([128, 128]), rhs=trib[:],
                                                                        start=True, stop=False), reads=[hkey, "trib"], writes=[pAk])
                    S.add("pe", lambda e, hlj=hlj, h=h, reg=reg: e.matmul(pA[:, reg], lhsT=hlj[:, 1, h:h + 1].to_broadcast([128, 128]), rhs=trib[:],
                                                                        start=False, stop=False), reads=[hkey, "trib"], writes=[pAk])
                    S.add("pe", lambda e, hlj=hlj, h=h, reg=reg: e.matmul(pA[:, reg], lhsT=identb[:], rhs=hlj[:, 2, h:h + 1].to_broadcast([128, 128]),
                                                                        start=False, stop=False), reads=[hkey, "identb"], writes=[pAk])
                    S.add("pe", lambda e, hlj=hlj, h=h, reg=reg: e.matmul(pA[:, reg], lhsT=identb[:], rhs=hlj[:, 3, h:h + 1].to_broadcast([128, 128]),
                                                                        start=False, stop=True), reads=[hkey, "identb"], writes=[pAk])
                    S.add("pe", lambda e, hlj=hlj, h=h, reg=reg: e.matmul(pA2[:, reg], lhsT=hlj[:, 0, h:h + 1].to_broadcast([128, 128]), rhs=trib[:],
                                                                        start=True, stop=False), reads=[hkey, "trib"], writes=[pA2k])
                    S.add("pe", lambda e, hlj=hlj, h=h, reg=reg: e.matmul(pA2[:, reg], lhsT=hlj[:, 1, h:h + 1].to_broadcast([128, 128]), rhs=trib[:],
                                                                        start=False, stop=True), reads=[hkey, "trib"], writes=[pA2k])
                S.add("act", lambda e, sl=sl: e.activation(out=dec4[sl][:], in_=pA[:, 0:512], func=AF.Exp), writes=[pAk, "dec%d" % sl])
                S.add("act", lambda e, sl=sl: e.activation(out=ebc4[sl][:], in_=pA2[:, 0:512], func=AF.Exp), writes=[pA2k, "ebc%d" % sl])
                S.add("dve", lambda e, sl=sl: e.tensor_tensor(out=m14[sl][:].rearrange("p (h l) -> p h l", h=4),
                                                              in0=dec4[sl][:].rearrange("p (h l) -> p h l", h=4),
                                                              in1=pC[:, 0:128].unsqueeze(1).to_broadcast([128, 4, 128]), op=ALU.mult),
                      reads=["dec%d" % sl], writes=["pC", "m1%d" % sl])
                S.add("pool", lambda e, sl=sl: e.affine_select(out=MT4[sl][:].rearrange("p (h l) -> p h l", h=4),
                                                               in_=m14[sl][:].rearrange("p (h l) -> p h l", h=4),
                                                               pattern=[[0, 4], [1, 128]], compare_op=ALU.is_ge, fill=0.0, base=0, channel_multiplier=-1),
                      reads=["m1%d" % sl], writes=["MT%d" % sl])
                S.add("pool", lambda e, sl=sl, g=g, js=js: e.tensor_tensor(out=CsT4[sl][:].rearrange("p (h l) -> p h l", h=4),
                                                                          in0=ebc4[sl][:].rearrange("p (h l) -> p h l", h=4),
                                                                          in1=CT[:, g, js].unsqueeze(1).to_broadcast([128, 4, 128]), op=ALU.mult),
                      reads=["ebc%d" % sl, "CT%d" % g], writes=["CsT%d" % sl])
                return sl

            def stage_b(n, sl):
                g, j = its[n]
                js = slice(j * 128, (j + 1) * 128)
                csj = cs[j]
                ckey = "cs%d" % j
                hk, hbk = "hT%d" % g, "hTb%d" % g
                xBb, xsb, xdb = xB[sl], xs[sl], xdt[sl]
                if full:
                    for i in range(4):
                        h = 4 * g + i
                        cc = i // 2
                        reg = slice(i * 128, (i + 1) * 128)
                        yo = pY[g % 2][(i % 2) * 64:(i % 2 + 1) * 64, cc * T + j * 128:cc * T + (j + 1) * 128]
                        yk = pYk[g % 2]
                        S.add("pe", lambda e, yo=yo, xdb=xdb, sl=sl, i=i, reg=reg: e.matmul(yo, lhsT=xdb[:, i * 64:(i + 1) * 64], rhs=MT4[sl][:, reg], start=True, stop=False),
                              reads=["xdt%d" % sl, "MT%d" % sl], writes=[yk])
                        S.add("pe", lambda e, yo=yo, h=h, sl=sl, reg=reg: e.matmul(yo, lhsT=hTb[:, h * 64:(h + 1) * 64], rhs=CsT4[sl][:, reg], start=False, stop=True),
                              reads=[hbk, "CsT%d" % sl], writes=[yk])
                S.add("pe", lambda e, xBb=xBb, xsb=xsb: e.matmul(pC[:, 128:384], lhsT=xBb[:, 256:384], rhs=xsb[:], start=True, stop=True),
                      reads=["xB%d" % sl, "xs%d" % sl], writes=["pC"])
                hg = hT[:, g * 256:(g + 1) * 256]
                ueng = "dve" if full else "pool"
                S.add(ueng, lambda e, hg=hg, csj=csj, g=g: e.tensor_tensor(
                    out=hg.rearrange("p (h q) -> p h q", h=4), in0=hg.rearrange("p (h q) -> p h q", h=4),
                    in1=csj[:, 6, 4 * g:4 * g + 4].unsqueeze(2).to_broadcast([128, 4, 64]), op=ALU.mult), reads=[ckey], writes=[hk])
                S.add("dve", lambda e, hg=hg: e.tensor_tensor(out=hg, in0=hg, in1=pC[:, 128:384], op=ALU.add), writes=["pC", hk])
                if full:
                    S.add("act", lambda e, hg=hg, g=g: e.activation(out=hTb[:, g * 256:(g + 1) * 256], in_=hg, func=AF.Copy), reads=[hk], writes=[hbk])
                if full and j == nsub - 1:
                    for q in range(2):
                        cc = 2 * g + q
                        ygb = tmp[(2 * g + q) % 4]
                        ygk = "tmp%d" % ((2 * g + q) % 4)
                        sqb = sq[q]
                        S.add("dve", lambda e, ygb=ygb, cc=cc, q=q, g=g, Tt=Tt: e.scalar_tensor_tensor(
                            out=ygb[:, 0:Tt], in0=xT[:, cc, 0:Tt], scalar=dsk[:, cc:cc + 1], in1=pY[g % 2][:, q * T:q * T + Tt], op0=ALU.mult, op1=ALU.add),
                              reads=["xT%d" % cc, "dsk"], writes=[pYk[g % 2], ygk])
                        S.add("dve", lambda e, ygb=ygb, cc=cc, Tt=Tt: e.tensor_tensor(out=ygb[:, 0:Tt], in0=ygb[:, 0:Tt], in1=yT[:, cc, 0:Tt], op=ALU.mult),
                              reads=["yT%d" % cc, ygk], writes=[ygk])
                        S.add("act", lambda e, ygb=ygb, sqb=sqb, Tt=Tt: e.activation(out=sqb[:, 0:Tt], in_=ygb[:, 0:Tt], func=AF.Square),
                              reads=[ygk], writes=["sq%d" % q])
                        S.add("pe", lambda e, sqb=sqb, q=q, Tt=Tt: e.matmul(pin[0][:, 0:Tt], lhsT=onesb[:], rhs=sqb[:, 0:Tt], start=(q == 0), stop=(q == 1)),
                              reads=["sq%d" % q, "onesb"], writes=["pin0"])
                    gl, gr = tmp[4], tmp[5]
                    S.add("act", lambda e, Tt=Tt: e.activation(out=gl[:, 0:Tt], in_=pin[0][:, 0:Tt], func=AF.Ln, scale=1.0 / 256, bias=eps[:, 0:1]),
                          reads=["eps"], writes=["pin0", "tmp4"])
                    S.add("act", lambda e, Tt=Tt: e.activation(out=gr[:, 0:Tt], in_=gl[:, 0:Tt], func=AF.Exp, scale=-0.5), reads=["tmp4"], writes=["tmp5"])
                    for q in range(2):
                        cc = 2 * g + q
                        ygb = tmp[(2 * g + q) % 4]
                        ygk = "tmp%d" % ((2 * g + q) % 4)
                        S.add("dve", lambda e, ygb=ygb, cc=cc, Tt=Tt: e.scalar_tensor_tensor(
                            out=yT[:, cc, 0:Tt], in0=ygb[:, 0:Tt], scalar=nw[:, cc:cc + 1], in1=gr[:, 0:Tt], op0=ALU.mult, op1=ALU.mult),
                            reads=[ygk, "nw", "tmp5"], writes=["yT%d" % cc])

            slots = {0: stage_a(0)}
            for n in range(len(its)):
                if n + 1 < len(its):
                    slots[n + 1] = stage_a(n + 1)
                stage_b(n, slots[n])
            if full:
                zkeys = ["yT%d" % cc for cc in range(NCH)]
                ouids = []
                for j in range(nsub):
                    emit_outproj_post(c, yT, zkeys, wout, gpost, xr[:, j, :], xkeys[j], j, bb, state["uid"], part=1)
                    ouids.append(state["uid"])
                    state["uid"] += 1
                hook2()
                for j in range(nsub):
                    emit_outproj_post(c, yT, zkeys, wout, gpost, xr[:, j, :], xkeys[j], j, bb, ouids[j], part=2)
            else:
                hook2()

        tiles = [(0, 1)] + [(HALO + i * T, NSUB) for i in range(NTOK // T)]
        def load_x(ti):
            tok0, nsub = tiles[ti]
            Tt = nsub * 128
            xr = xres[ti % 2]
            xkeys = ["xres%d_%d" % (ti % 2, j) for j in range(NSUB)]
            S.add("sp", lambda e: e.dma_start(out=xr[:, 0:nsub, :], in_=x[tok0:tok0 + Tt, :].rearrange("(j p) d -> p j d", p=128)),
                  writes=xkeys[0:nsub], dma="xld%d" % (ti % 2))
            return xr, xkeys, nsub

        nxt = load_x(0)
        l0_pre(nxt[0], nxt[1], nxt[2], 0)
        for ti, (tok0, nsub) in enumerate(tiles):
            Tt = nsub * 128
            xr, xkeys = nxt[0], nxt[1]
            l0_tile(xr, xkeys, nsub, 0)
            S.add("pool", lambda e, xr=xr, tok0=tok0, nsub=nsub, Tt=Tt: e.dma_start(
                out=h1s[tok0:tok0 + Tt, :].rearrange("(j p) d -> p j d", p=128), in_=xr[:, 0:nsub, :]),
                reads=xkeys[0:nsub], writes=["h1s%d" % ti], dma="xst%d" % (ti % 2))

            nxt_box = [None]

            def hook_a1(ti=ti, nxt_box=nxt_box):
                if ti + 1 < len(tiles):
                    r = load_x(ti + 1)
                    nxt_box[0] = r + (pre_s(r[0], r[1], r[2], gpre0),)

            def hook_a2(ti=ti, nxt_box=nxt_box):
                if ti + 1 < len(tiles):
                    r = nxt_box[0]
                    pre_t(r[2], 0, r[3])
            l1_tile(xr, xkeys, nsub, False, ti == 0, b, 1, False, hook_a1, hook_a2)
            nxt = nxt_box[0]
        S.add("sp", lambda e: e.dma_start(out=sbn[:, :], in_=hT[:]), reads=hkeys, writes=["sbn"], dma="sbn")
        S.add("sp", lambda e: e.dma_start(out=dbn[:, :], in_=dcumt[:]), reads=["dcum"], writes=["dbn"], dma="dbn")
        RG = [[0, 1, 2, 3], [4, 5, 6, 7]]
        S.add("pool", lambda e: e.collective_compute("AllGather", ALU.bypass, replica_groups=RG, ins=[sbn[:, :]], outs=[sgt[:, :]]),
              reads=["sbn"], writes=["sgt"], dma="cc", inc=1)
        S.add("pool", lambda e: e.collective_compute("AllGather", ALU.bypass, replica_groups=RG, ins=[dbn[:, :]], outs=[dgt[:, :]]),
              reads=["dbn"], writes=["dgt"], dma="cc2", inc=1)
        S.barrier()
        S.add("sp", lambda e: e.dma_start(out=gpost[:], in_=post1.partition_broadcast(128)), writes=["gpost"], dma="gpost")
        load_wout(c, wout, w1_out)
        S.add("pool", lambda e: e.memset(carry1[:], 0.0), writes=["carry%d" % i for i in range(32)])
        S.add("dve", lambda e: e.memset(hT[:], 0.0), writes=hkeys)
        S.add("sp", lambda e: e.dma_start(out=dstage[:], in_=dgt.rearrange("(k p) h -> p k h", p=128)), reads=["dgt"], writes=["dstage"], dma="dstage")
        for k in range(3):
            S.add("sp", lambda e, k=k: e.dma_start(out=sstage[:], in_=sgt[k * 128:(k + 1) * 128, :]), reads=["sgt"], writes=["sstage"], dma="sstage")
            S.add("dve", lambda e, k=k: e.tensor_scalar(out=fac[:], in0=dstage[:, k, 0:NH], scalar1=cm[:, k:k + 1],
                                                        scalar2=cm[:, 4 + k:5 + k], op0=ALU.mult, op1=ALU.add),
                  reads=["dstage", "cm"], writes=["fac"])
            S.add("dve", lambda e: e.tensor_tensor(out=hT[:].rearrange("p (h q) -> p h q", h=NH),
                                                   in0=hT[:].rearrange("p (h q) -> p h q", h=NH),
                                                   in1=fac[:].unsqueeze(2).to_broadcast([128, NH, 64]), op=ALU.mult),
                  reads=["fac"], writes=hkeys)
            S.add("dve", lambda e, k=k: e.scalar_tensor_tensor(out=hT[:], in0=sstage[:, 0:DI], scalar=cm[:, k:k + 1], in1=hT[:],
                                                               op0=ALU.mult, op1=ALU.add),
                  reads=["sstage", "cm"], writes=hkeys)
        S.add("act", lambda e: e.activation(out=hTb[:], in_=hT[:], func=AF.Copy), reads=hkeys, writes=["hTb%d" % g for g in range(NG)])
        def load_h(ti):
            tok0, nsub = tiles[ti]
            Tt = nsub * 128
            xr = xres[ti % 2]
            xkeys = ["xres%d_%d" % (ti % 2, j) for j in range(NSUB)]
            S.add("sp", lambda e: e.dma_start(out=xr[:, 0:nsub, :], in_=h1s[tok0:tok0 + Tt, :].rearrange("(j p) d -> p j d", p=128)),
                  reads=["h1s%d" % ti], writes=xkeys[0:nsub], dma="xld%d" % (ti % 2))
            return xr, xkeys, nsub

        nxt = load_h(0)
        l1_pre(nxt[0], nxt[1], nxt[2], 0)
        for ti, (tok0, nsub) in enumerate(tiles):
            Tt = nsub * 128
            xr, xkeys = nxt[0], nxt[1]
            nxt_box = [None]

            def hook_b1(ti=ti, nxt_box=nxt_box):
                if ti + 1 < len(tiles):
                    r = load_h(ti + 1)
                    nxt_box[0] = r + (pre_s(r[0], r[1], r[2], gpre1),)

            def hook_b2(ti=ti, nxt_box=nxt_box):
                if ti + 1 < len(tiles):
                    r = nxt_box[0]
                    pre_t(r[2], (ti + 1) % 2, r[3])
            l1_tile(xr, xkeys, nsub, True, ti == 0, bB, ti % 2, True, hook_b1, hook_b2)
            nxt = nxt_box[0]
            if ti > 0:
                S.add("pool", lambda e, xr=xr, tok0=tok0, nsub=nsub, Tt=Tt: e.dma_start(
                    out=out[tok0 - HALO:tok0 - HALO + Tt, :].rearrange("(j p) d -> p j d", p=128), in_=xr[:, 0:nsub, :]),
                    reads=xkeys[0:nsub], writes=["outd"], dma="xst%d" % (ti % 2))
        S.add("sp", lambda e: e.nop(), reads=["outd"])
        S.emit(nc, st)
    return nc


TF = 256
_PROGS = {}


def fused_maps(x, pre_norm, post_norm, sc_w_in, sc_conv_w, sc_w_out, ssd_w_in, ssd_conv_w, ssd_conv_b,
               ssd_dt_bias, ssd_a_log, ssd_d_skip, ssd_norm, ssd_w_out, NTOK):
    B, L, _ = x.shape
    cpb = L // NTOK
    cwh = np.ascontiguousarray(sc_conv_w[0].reshape(3, NCH, 128).transpose(2, 1, 0)).reshape(128, NCH * 3)
    pm = l1_param_maps(ssd_w_in, ssd_conv_w, ssd_conv_b, ssd_dt_bias, ssd_a_log)
    ex = l1_full_extra(post_norm, ssd_d_skip, ssd_norm, ssd_w_out)
    shared = {"pre0": np.ascontiguousarray(pre_norm[0]), "post0": np.ascontiguousarray(post_norm[0]),
              "w0_in": np.ascontiguousarray(sc_w_in[0]), "cw0h": cwh, "w0_out": np.ascontiguousarray(sc_w_out[0]),
              "pre1": np.ascontiguousarray(pre_norm[1]), "post1": ex["post"], "w1_in": pm["w_in"], "cw1h": pm["cw1h"],
              "cb1h": pm["cb1h"], "dtbh": pm["dtbh"], "alogh": pm["alogh"], "dskh": ex["dskh"], "nwh": ex["nwh"],
              "w1_out": ex["w_out"]}
    maps = []
    for core in range(B * cpb):
        bi, ci = divmod(core, cpb)
        cm = np.zeros((128, 8), np.float32)
        for k in range(4):
            cm[:, k] = 1.0 if k < ci else 0.0
        cm[:, 4:8] = 1.0 - cm[:, 0:4]
        m = dict(shared)
        m["x"] = core_tokens(x[bi], ci * NTOK, NTOK)
        m["cmask"] = cm
        maps.append(m)
    return maps


def kernel(x, pre_norm, post_norm, sc_w_in, sc_conv_w, sc_w_out, ssd_w_in, ssd_conv_w, ssd_conv_b,
           ssd_dt_bias, ssd_a_log, ssd_d_skip, ssd_norm, ssd_w_out):
    f = lambda a: np.ascontiguousarray(np.asarray(a), dtype=np.float32)
    args = [f(a) for a in (x, pre_norm, post_norm, sc_w_in, sc_conv_w, sc_w_out, ssd_w_in, ssd_conv_w, ssd_conv_b,
                           ssd_dt_bias, ssd_a_log, ssd_d_skip, ssd_norm, ssd_w_out)]
    B, L, _ = args[0].shape
    ncores = 8
    NTOK = B * L // ncores
    if ("fused", NTOK) not in _PROGS:
        _PROGS[("fused", NTOK)] = build_fused(NTOK, TF)
    nc = _PROGS[("fused", NTOK)]
    res = run_bass_kernel_spmd(nc, fused_maps(*args, NTOK), core_ids=list(range(ncores)))
    out = np.concatenate([r["out"] for r in res.results], 0).reshape(B, L, D)
    return out.astype(np.float32)
```

```python
from contextlib import ExitStack
import numpy as np
import concourse.bass as bass
import concourse.mybir as mybir
from concourse.bass_utils import run_bass_kernel_spmd

F32 = mybir.dt.float32
BF16 = mybir.dt.bfloat16
AF = mybir.ActivationFunctionType
ALU = mybir.AluOpType

D = 1024
KD = 8
DI = 2048
NCH = 16
NH = 32
NG = 8
HALO = 128
EPS = 1e-6
L1IN = 6176


class Sched:
    ENGS = ("pe", "act", "dve", "pool", "sp")

    def __init__(self):
        self.ops = {e: [] for e in self.ENGS}
        self.last_w = {}
        self.readers = {}
        self.dma_cnt = {}
        self.dma_inc = {}

    def barrier(self):
        toks = set()
        for e in self.ENGS:
            for i in range(len(self.ops[e]) - 1, -1, -1):
                if self.ops[e][i]["dma"] is None:
                    toks.add(("eng", e, i))
                    break
        for k, cnt in self.dma_cnt.items():
            toks.add(("dma", k, cnt))
        self.pending = {e: set(toks) for e in self.ENGS}

    def add(self, eng, fn, reads=(), writes=(), dma=None, inc=16):
        deps = set()
        if getattr(self, "pending", None) and self.pending.get(eng):
            deps |= self.pending[eng]
            self.pending[eng] = set()
        for r in reads:
            t = self.last_w.get(r)
            if t is not None:
                deps.add(t)
        for w in writes:
            t = self.last_w.get(w)
            if t is not None:
                deps.add(t)
            for t in self.readers.get(w, ()):
                deps.add(t)
        idx = len(self.ops[eng])
        if dma is not None:
            c = self.dma_cnt.get(dma, 0) + 1
            self.dma_cnt[dma] = c
            tok = ("dma", dma, c)
        else:
            tok = ("eng", eng, idx)
        if eng == "pe":
            deps = {d for d in deps if not (d[0] == "eng" and d[1] == "pe")}
        deps.discard(tok)
        if dma is not None:
            self.dma_inc[dma] = inc
        self.ops[eng].append(dict(fn=fn, deps=deps, tok=tok, dma=dma))
        for r in reads:
            lst = self.readers.setdefault(r, [])
            if tok[0] == "eng":
                lst[:] = [t for t in lst if not (t[0] == "eng" and t[1] == eng)]
            lst.append(tok)
        for w in writes:
            self.last_w[w] = tok
            self.readers[w] = []
        return tok

    def emit(self, nc, stack):
        sig = {e: [False] * len(self.ops[e]) for e in self.ENGS}
        for e in self.ENGS:
            for op in self.ops[e]:
                for d in op["deps"]:
                    if d[0] == "eng":
                        sig[d[1]][d[2]] = True
        cum = {}
        for e in self.ENGS:
            c = 0
            arr = []
            for s in sig[e]:
                if s:
                    c += 1
                arr.append(c)
            cum[e] = arr
        esem = {e: stack.enter_context(nc.semaphore("s_" + e)) for e in self.ENGS}
        dsem = {k: stack.enter_context(nc.semaphore("d_" + str(k))) for k in self.dma_cnt}
        block = stack.enter_context(nc.Block())

        def run(e, engobj):
            known = {}
            for i, op in enumerate(self.ops[e]):
                need = {}
                for d in op["deps"]:
                    if d[0] == "eng":
                        s, v = esem[d[1]], cum[d[1]][d[2]]
                    else:
                        s, v = dsem[d[1]], self.dma_inc[d[1]] * d[2]
                    if need.get(s.name, (None, 0))[1] < v:
                        need[s.name] = (s, v)
                for nm, (s, v) in need.items():
                    if known.get(nm, 0) >= v:
                        continue
                    engobj.wait_ge(s, v)
                    known[nm] = v
                ins = op["fn"](engobj)
                if op["dma"] is not None:
                    ins.then_inc(dsem[op["dma"]], self.dma_inc[op["dma"]])
                elif sig[e][i]:
                    ins.then_inc(esem[e], 1)

        block.tensor(lambda eng: run("pe", eng))
        block.scalar(lambda eng: run("act", eng))
        block.vector(lambda eng: run("dve", eng))
        block.gpsimd(lambda eng: run("pool", eng))
        block.sync(lambda eng: run("sp", eng))


class Ctx:
    def __init__(self, nc, st):
        self.nc = nc
        self.st = st
        self.S = Sched()

    def sb(self, name, shape, dt):
        return self.st.enter_context(self.nc.sbuf_tensor(name, shape, dt))

    def ps(self, name, shape, dt):
        return self.st.enter_context(self.nc.psum_tensor(name, shape, dt))

    def din(self, name, shape, dt=F32):
        return self.nc.dram_tensor(name, list(shape), dt, kind="ExternalInput").ap()

    def dout(self, name, shape, dt=F32):
        return self.nc.dram_tensor(name, list(shape), dt, kind="ExternalOutput").ap()


def make_ident(c, t, key):
    S = c.S
    S.add("pool", lambda e: e.memset(t[:], 0.0), writes=[key])
    S.add("pool", lambda e: e.affine_select(out=t[:], in_=t[:], pattern=[[-1, 128]],
                                            compare_op=ALU.not_equal, fill=1.0, base=0,
                                            channel_multiplier=1), reads=[key], writes=[key])


def make_tri(c, t, key):
    S = c.S
    S.add("pool", lambda e: e.memset(t[:], 1.0), writes=[key])
    S.add("pool", lambda e: e.affine_select(out=t[:], in_=t[:], pattern=[[1, 128]],
                                            compare_op=ALU.is_ge, fill=0.0, base=0,
                                            channel_multiplier=-1), reads=[key], writes=[key])


def emit_prenorm_stats(c, xres_j, xkey, gpre, bufs, uid):
    S = c.S
    stat, junk, utok, eps = bufs["stat"], bufs["junk"], bufs["utok"], bufs["eps"]
    sl = uid % 4
    st_ = stat[sl]
    sk = "stat%d" % sl
    ut = utok[uid % 2]
    uk = "utok%d" % (uid % 2)
    S.add("act", lambda e: e.activation(out=junk[:], in_=xres_j, func=AF.Square, accum_out=st_[:, 0:1]),
          reads=[xkey], writes=["junk", sk])
    S.add("act", lambda e: e.activation(out=st_[:, 1:2], in_=st_[:, 0:1], func=AF.Ln, scale=1.0 / D,
                                        bias=eps[:, 0:1]), reads=[sk, "eps"], writes=[sk])
    S.add("act", lambda e: e.activation(out=st_[:, 2:3], in_=st_[:, 1:2], func=AF.Exp, scale=-0.5),
          reads=[sk], writes=[sk])
    S.add("dve", lambda e: e.scalar_tensor_tensor(out=ut[:], in0=xres_j, scalar=st_[:, 2:3], in1=gpre[:],
                                                  op0=ALU.mult, op1=ALU.mult),
          reads=[xkey, sk, "gpre"], writes=[uk])


def emit_prenorm_T(c, uT, j, bufs, uid, ukp="uT"):
    S = c.S
    utok, tp, identb = bufs["utok"], bufs["tp"], bufs["identb"]
    ut = utok[uid % 2]
    uk = "utok%d" % (uid % 2)
    for kc in range(KD):
        S.add("pe", lambda e, kc=kc: e.transpose(tp[:, kc * 128:(kc + 1) * 128], ut[:, kc * 128:(kc + 1) * 128], identb[:]),
              reads=[uk, "identb"], writes=["tp"])
    S.add("act", lambda e: e.activation(out=uT[:, :, j * 128:(j + 1) * 128],
                                        in_=tp[:, 0:1024].rearrange("p (k t) -> p k t", k=KD), func=AF.Copy),
          reads=[uk], writes=["tp", "%s%d" % (ukp, j)])


def emit_prenorm(c, xres_j, xkey, gpre, uT, j, bufs, uid, ukp="uT"):
    emit_prenorm_stats(c, xres_j, xkey, gpre, bufs, uid)
    emit_prenorm_T(c, uT, j, bufs, uid, ukp)


def emit_outproj_post(c, lhs_buf, lhs_keys, wout, gpost, xres_j, xkey, j, bufs, uid, part=3):
    S = c.S
    stat, junk, eps, mres = bufs["stat"], bufs["junk"], bufs["eps"], bufs["mres"]
    po = bufs["po"][2 * (j % 2):2 * (j % 2) + 2] if len(bufs["po"]) >= 4 else bufs["po"]
    pok = bufs["pokeys"][2 * (j % 2):2 * (j % 2) + 2] if len(bufs["po"]) >= 4 else bufs["pokeys"]
    for hf in range(2 if (part & 1) else 0):
        for cc in range(NCH):
            S.add("pe", lambda e, hf=hf, cc=cc: e.matmul(po[hf][:], lhsT=lhs_buf[:, cc, j * 128:(j + 1) * 128],
                                                         rhs=wout[:, cc, hf * 512:(hf + 1) * 512],
                                                         start=(cc == 0), stop=(cc == NCH - 1)),
                  reads=[lhs_keys[cc], "wout"], writes=[pok[hf]])
    if not (part & 2):
        return
    sl = uid % 4
    st_ = stat[sl]
    sk = "stat%d" % sl
    for hf in range(2):
        S.add("act", lambda e, hf=hf: e.activation(out=junk[:, 0:512], in_=po[hf][:], func=AF.Square,
                                                   accum_out=st_[:, 3 + hf:4 + hf]),
              writes=[pok[hf], "junk", sk])
    S.add("dve", lambda e: e.tensor_tensor(out=st_[:, 5:6], in0=st_[:, 3:4], in1=st_[:, 4:5], op=ALU.add),
          reads=[sk], writes=[sk])
    S.add("act", lambda e: e.activation(out=st_[:, 6:7], in_=st_[:, 5:6], func=AF.Ln, scale=1.0 / D,
                                        bias=eps[:, 0:1]), reads=[sk, "eps"], writes=[sk])
    S.add("act", lambda e: e.activation(out=st_[:, 7:8], in_=st_[:, 6:7], func=AF.Exp, scale=-0.5),
          reads=[sk], writes=[sk])
    for hf in range(2):
        S.add("dve", lambda e, hf=hf: e.scalar_tensor_tensor(out=mres[:, hf * 512:(hf + 1) * 512], in0=po[hf][:],
                                                             scalar=st_[:, 7:8], in1=gpost[:, hf * 512:(hf + 1) * 512],
                                                             op0=ALU.mult, op1=ALU.mult),
              reads=[sk, "gpost"], writes=[pok[hf], "mres%d" % hf])
    S.add("pool", lambda e: e.tensor_tensor(out=xres_j, in0=xres_j, in1=mres[:], op=ALU.add),
          reads=["mres0", "mres1", xkey], writes=[xkey])


def common_bufs(c, po=None, pokeys=None):
    b = {}
    b["stat"] = [c.sb("stat%d" % i, [128, 8], F32) for i in range(4)]
    b["junk"] = c.sb("junk", [128, 1024], BF16)
    b["utok"] = [c.sb("utok%d" % i, [128, 1024], BF16) for i in range(2)]
    b["tp"] = c.ps("tp", [128, 1024], BF16)
    b["identb"] = c.sb("identb", [128, 128], BF16)
    b["eps"] = c.sb("eps", [128, 1], F32)
    b["mres"] = c.sb("mres", [128, 1024], F32)
    if po is None:
        po = [c.ps("po%d" % i, [128, 512], F32) for i in range(2)]
        pokeys = ["po0", "po1"]
    b["po"] = po
    b["pokeys"] = pokeys
    make_ident(c, b["identb"], "identb")
    c.S.add("dve", lambda e: e.memset(b["eps"][:], EPS), writes=["eps"])
    return b


def load_wout(c, wout_sb, w_out_dram):
    src = w_out_dram.rearrange("(c p) n -> p c n", p=128)
    for q in range(4):
        c.S.add("pool", lambda e, q=q: e.dma_start(out=wout_sb[:, q * 4:(q + 1) * 4, :], in_=src[:, q * 4:(q + 1) * 4, :]),
                writes=["wout"], dma="wout%d" % q)


def core_tokens(xflat_b, start, NTOK):
    out = np.zeros((HALO + NTOK, xflat_b.shape[1]), np.float32)
    lo = max(0, start - HALO)
    out[HALO - (start - lo):] = xflat_b[lo:start + NTOK]
    return out


def l1_param_maps(ssd_w_in, ssd_conv_w, ssd_conv_b, ssd_dt_bias, ssd_a_log):
    cw1h = np.ascontiguousarray(ssd_conv_w[0].reshape(4, 32, 128).transpose(2, 1, 0)).reshape(128, 128)
    cb1h = np.ascontiguousarray(ssd_conv_b[0].reshape(32, 128).T)
    dtbh = np.zeros((128, 1), np.float32)
    dtbh[:NH, 0] = ssd_dt_bias[0]
    alogh = np.zeros((128, 1), np.float32)
    alogh[:NH, 0] = ssd_a_log[0]
    return {"w_in": np.ascontiguousarray(ssd_w_in[0]), "cw1h": cw1h, "cb1h": cb1h, "dtbh": dtbh, "alogh": alogh}


def l1_full_extra(post_norm, ssd_d_skip, ssd_norm, ssd_w_out):
    dskh = np.ascontiguousarray(np.broadcast_to(ssd_d_skip[0][None, :], (128, NH))).astype(np.float32)
    nwh = np.ascontiguousarray(ssd_norm[0].reshape(NCH, 128).T)
    return {"post": np.ascontiguousarray(post_norm[1]), "dskh": dskh, "nwh": nwh,
            "w_out": np.ascontiguousarray(ssd_w_out[0])}


def build_fused(NTOK, T):
    nc = bass.Bass("TRN2", target_bir_lowering=False)
    with ExitStack() as st:
        c = Ctx(nc, st)
        S = c.S
        NSUB = T // 128
        x = c.din("x", [HALO + NTOK, D])
        pre0 = c.din("pre0", [D])
        post0 = c.din("post0", [D])
        w0_in = c.din("w0_in", [D, 4 * DI])
        cw0h = c.din("cw0h", [128, NCH * 3])
        w0_out = c.din("w0_out", [DI, D])
        pre1 = c.din("pre1", [D])
        post1 = c.din("post1", [D])
        w1_in = c.din("w1_in", [D, L1IN])
        cw1h = c.din("cw1h", [128, 32 * 4])
        cb1h = c.din("cb1h", [128, 32])
        dtbh = c.din("dtbh", [128, 1])
        alogh = c.din("alogh", [128, 1])
        dskh = c.din("dskh", [128, NH])
        nwh = c.din("nwh", [128, NCH])
        w1_out = c.din("w1_out", [DI, D])
        cmask = c.din("cmask", [128, 8])
        out = c.dout("out", [NTOK, D])
        w0s = nc.dram_tensor("w0s", [NCH, 128, KD * 4 * 128], BF16, kind="Internal").ap()
        w1s = nc.dram_tensor("w1s", [12, 128, KD * 512], BF16, kind="Internal").ap()
        h1s = nc.dram_tensor("h1s", [HALO + NTOK, D], F32, kind="Internal").ap()
        sbn = nc.dram_tensor("sbn", [128, DI], F32, kind="Internal").ap()
        sgt = nc.dram_tensor("sgt", [4 * 128, DI], F32, kind="Internal").ap()
        dbn = nc.dram_tensor("dbn", [128, 64], F32, kind="Internal").ap()
        dgt = nc.dram_tensor("dgt", [4 * 128, 64], F32, kind="Internal").ap()

        pin = [c.ps("pin%d" % i, [128, 512], F32) for i in range(4)]
        b = common_bufs(c)
        po = b["po"]
        identb, eps, tp = b["identb"], b["eps"], b["tp"]
        pC = c.ps("pC", [128, 512], F32)
        pS = pC[:, 384:512]
        pA = po[0]
        pAk = "po0"
        pA2 = po[1]
        pA2k = "po1"
        pY = [pin[2], pin[3]]
        pYk = ["pin2", "pin3"]
        bB = dict(b)
        bB["po"] = [pin[2], pin[3], po[0], po[1]]
        bB["pokeys"] = ["pin2", "pin3", "po0", "po1"]
        bA = dict(b)
        bA["po"] = [po[0], po[1], pin[0], pin[1]]
        bA["pokeys"] = ["po0", "po1", "pin0", "pin1"]

        gpre0 = c.sb("gpre0", [128, D], F32)
        gpre1 = c.sb("gpre1", [128, D], F32)
        gpost = c.sb("gpost", [128, D], F32)
        cw0 = c.sb("cw0", [128, NCH * 3], F32)
        carry0 = c.sb("carry0", [128, NCH, 2], F32)
        wout = c.sb("wout", [128, NCH, D], BF16)
        xres = [c.sb("xres%d" % i, [128, NSUB, D], F32) for i in range(2)]
        uTs = [c.sb("uT%d" % i, [128, KD, T], BF16) for i in range(2)]
        wbuf = [c.sb("wbuf%d" % i, [128, KD * 512], BF16) for i in range(3)]
        yT = c.sb("yT", [128, NCH, T], BF16)
        tmp = [c.sb("tmp%d" % i, [128, T], F32) for i in range(8)]
        cv = [c.sb("cv%d" % i, [128, T + 2], F32) for i in range(2)]
        cw1 = c.sb("cw1", [128, 32 * 4], F32)
        cb1 = c.sb("cb1", [128, 32], F32)
        dtb = c.sb("dtb", [128, 1], F32)
        acol = c.sb("acol", [128, 1], F32)
        onec = c.sb("onec", [128, 1], F32)
        identf = c.sb("identf", [128, 128], F32)
        trif = c.sb("trif", [128, 128], F32)
        onesf = c.sb("onesf", [128, 128], F32)
        wdt = c.sb("wdt", [128, KD, NH], BF16)
        carry1 = c.sb("carry1", [128, 32, 3], F32)
        xT = c.sb("xT", [128, NCH, T], BF16)
        BT = c.sb("BT", [128, NG, T], BF16)
        CT = c.sb("CT", [128, NG, T], BF16)
        xp = [c.sb("xp%d" % i, [128, T + 3], F32) for i in range(4)]
        acc = [c.sb("acc%d" % i, [128, T], F32) for i in range(4)]
        l1banks = [(pin[0], "pin0"), (pin[1], "pin1"), (po[0], "po0"), (po[1], "po1")]
        dtT = c.sb("dtT", [128, T], F32)
        adtT = c.sb("adtT", [128, T], F32)
        cs = [c.sb("cs%d" % i, [128, 8, NH], F32) for i in range(NSUB)]
        hl = [c.sb("hl%d" % i, [128, 4, NH], BF16) for i in range(NSUB)]
        xdt = [c.sb("xdt%d" % i, [128, 256], BF16) for i in range(2)]
        xB = [c.sb("xB%d" % i, [128, 384], BF16) for i in range(4)]
        xs = [c.sb("xs%d" % i, [128, 256], BF16) for i in range(4)]
        hT = c.sb("hT", [128, DI], F32)
        dcumt = c.sb("dcumt", [128, 64], F32)
        dcum = dcumt[:, 0:NH]
        dstage = c.sb("dstage", [128, 4, 64], F32)
        dsk = c.sb("dsk", [128, NH], F32)
        nw = c.sb("nw", [128, NCH], F32)
        cm = c.sb("cm", [128, 8], F32)
        trib = c.sb("trib", [128, 128], BF16)
        onesb = c.sb("onesb", [128, 128], BF16)
        DIm = c.sb("DIm", [128, NH, 128], BF16)
        hTb = c.sb("hTb", [128, DI], BF16)
        dec4 = [c.sb("dec%d" % i, [128, 512], F32) for i in range(2)]
        ebc4 = [c.sb("ebc%d" % i, [128, 512], F32) for i in range(2)]
        m14 = [c.sb("m1%d" % i, [128, 512], F32) for i in range(2)]
        MT4 = [c.sb("MT%d" % i, [128, 512], BF16) for i in range(2)]
        CsT4 = [c.sb("CsT%d" % i, [128, 512], BF16) for i in range(2)]
        sstage = c.sb("sstage", [128, DI], F32)
        fac = c.sb("fac", [128, NH], F32)
        sq = [c.sb("sq%d" % i, [128, T], BF16) for i in range(2)]
        hkeys = ["hT%d" % g for g in range(NG)]

        w0src = w0_in.rearrange("(kc p) n -> p kc n", p=128)
        for ch in range(NCH):
            dst = w0s[ch].rearrange("p (kc w n) -> p kc w n", kc=KD, w=4)
            for which in range(4):
                col0 = which * DI + ch * 128
                S.add("pool", lambda e, dst=dst, which=which, col0=col0: e.dma_start(out=dst[:, :, which, :], in_=w0src[:, :, col0:col0 + 128]),
                      writes=["w0sraw%d_%d" % (ch, which)], dma="cast%d" % ((ch * 4 + which) % 4))
        w1src = w1_in.rearrange("(kc p) n -> p kc n", p=128)
        for t_, src_, k_ in ((gpre0, pre0, "gpre0"), (gpre1, pre1, "gpre1"), (gpost, post0, "gpost")):
            S.add("sp", lambda e, t_=t_, src_=src_: e.dma_start(out=t_[:], in_=src_.partition_broadcast(128)), writes=[k_], dma=k_)
        for t_, src_, k_ in ((cw0, cw0h, "cw0"), (cw1, cw1h, "cw1"), (cb1, cb1h, "cb1"), (dtb, dtbh, "dtb"), (acol, alogh, "acol"),
                             (dsk, dskh, "dsk"), (nw, nwh, "nw"), (cm, cmask, "cm")):
            S.add("sp", lambda e, t_=t_, src_=src_: e.dma_start(out=t_[:], in_=src_[:, :]), writes=[k_], dma=k_)
        S.add("pool", lambda e: e.dma_start(out=wdt[:], in_=w1src[:, :, L1IN - NH:L1IN]), writes=["wdt"], dma="wdt")
        load_wout(c, wout, w0_out)
        S.add("act", lambda e: e.activation(out=acol[:], in_=acol[:], func=AF.Exp), reads=["acol"], writes=["acol"])
        S.add("dve", lambda e: e.tensor_scalar(out=acol[:], in0=acol[:], scalar1=-1.0, scalar2=None, op0=ALU.mult),
              reads=["acol"], writes=["acol"])
        S.add("dve", lambda e: e.memset(onec[:], 1.0), writes=["onec"])
        S.add("pool", lambda e: e.memset(onesf[:], 1.0), writes=["onesf"])
        S.add("pool", lambda e: e.memset(onesb[:], 1.0), writes=["onesb"])
        S.add("pool", lambda e: e.memset(carry0[:], 0.0), writes=["carry0_%d" % i for i in range(NCH)])
        S.add("pool", lambda e: e.memset(carry1[:], 0.0), writes=["carry%d" % i for i in range(32)])
        make_ident(c, identf, "identf")
        make_tri(c, trif, "trif")
        make_tri(c, trib, "trib")
        for h in range(NH):
            S.add("dve", lambda e, h=h: e.tensor_scalar(out=DIm[:, h, :], in0=identf[:], scalar1=dsk[:, h:h + 1], scalar2=None, op0=ALU.mult),
                  reads=["identf", "dsk"], writes=["DIm"])
        S.add("dve", lambda e: e.memset(hT[:], 0.0), writes=hkeys)
        S.add("dve", lambda e: e.memset(dcumt[:], 1.0), writes=["dcum"])
        for gi in range(12):
            S.add("pool", lambda e, gi=gi: e.dma_start(out=w1s[gi].rearrange("p (kc n) -> p kc n", kc=KD), in_=w1src[:, :, gi * 512:(gi + 1) * 512]),
                  writes=["w1sraw%d" % gi], dma="cast%d" % (gi % 4))

        state = dict(uid=0, wcnt=0, ocnt=0, gj=0, w0ready=False, w1ready=False)

        def cast_ready(which_layer):
            if which_layer == 0 and not state["w0ready"]:
                state["w0ready"] = True
                S.add("sp", lambda e: e.nop(), reads=["w0sraw%d_%d" % (ch, w) for ch in range(NCH) for w in range(4)],
                      writes=["w0s%d" % ch for ch in range(NCH)])
            if which_layer == 1 and not state["w1ready"]:
                state["w1ready"] = True
                S.add("sp", lambda e: e.nop(), reads=["w1sraw%d" % gi for gi in range(12)] + ["w0sraw%d_%d" % (ch, w) for ch in range(NCH) for w in range(4)],
                      writes=["w1s%d" % gi for gi in range(12)])

        def pre_s(xr, xkeys, nsub, gp):
            uids = []
            for j in range(nsub):
                emit_prenorm_stats(c, xr[:, j, :], xkeys[j], gp, b, state["uid"])
                uids.append(state["uid"])
                state["uid"] += 1
            return uids

        def pre_t(nsub, ub, uids):
            for j in range(nsub):
                emit_prenorm_T(c, uTs[ub], j, b, uids[j], ukp="uT%d_" % ub)

        def l0_pre(xr, xkeys, nsub, ub):
            pre_t(nsub, ub, pre_s(xr, xkeys, nsub, gpre0))

        def l0_tile(xr, xkeys, nsub, ub):
            Tt = nsub * 128
            uT = uTs[ub]
            ukeys = ["uT%d_%d" % (ub, j) for j in range(nsub)]
            cast_ready(0)
            for ch in range(NCH):
                ws = state["wcnt"] % 3
                state["wcnt"] += 1
                wbv = wbuf[ws][:].rearrange("p (kc w n) -> p kc w n", kc=KD, w=4)
                S.add("sp", lambda e, ws=ws, ch=ch: e.dma_start(out=wbuf[ws][:], in_=w0s[ch]), reads=["w0s%d" % ch], writes=["wb%d" % ws], dma="wb%d" % ws)
                pr = ch % 2
                for which, bank in ((2, 0), (3, 1), (0, 2), (1, 3)):
                    for kc in range(KD):
                        S.add("pe", lambda e, wbv=wbv, which=which, bank=bank, kc=kc, Tt=Tt: e.matmul(
                            pin[bank][:, 0:Tt], lhsT=wbv[:, kc, which, :], rhs=uT[:, kc, 0:Tt], start=(kc == 0), stop=(kc == KD - 1)),
                            reads=["wb%d" % ws] + ukeys, writes=["pin%d" % bank])
                a, bq, cq, dq, cvb = tmp[pr], tmp[2 + pr], tmp[4 + pr], tmp[6 + pr], cv[pr]
                ka, kb, kc_, kd = "tmp%d" % pr, "tmp%d" % (2 + pr), "tmp%d" % (4 + pr), "tmp%d" % (6 + pr)
                ck = "carry0_%d" % ch
                S.add("act", lambda e, a=a, Tt=Tt: e.activation(out=a[:, 0:Tt], in_=pin[0][:, 0:Tt], func=AF.Copy), writes=["pin0", ka])
                S.add("pool", lambda e, cvb=cvb, ch=ch: e.tensor_copy(out=cvb[:, 0:2], in_=carry0[:, ch, :]), reads=[ck], writes=["cvh%d" % pr])
                S.add("dve", lambda e, cvb=cvb, a=a, Tt=Tt: e.tensor_tensor(out=cvb[:, 2:2 + Tt], in0=a[:, 0:Tt], in1=pin[1][:, 0:Tt], op=ALU.mult),
                      reads=[ka], writes=["pin1", "cvb%d" % pr])
                S.add("act", lambda e, bq=bq, Tt=Tt: e.activation(out=bq[:, 0:Tt], in_=pin[2][:, 0:Tt], func=AF.Silu), writes=["pin2", kb])
                S.add("dve", lambda e, cq=cq, bq=bq, Tt=Tt: e.tensor_tensor(out=cq[:, 0:Tt], in0=bq[:, 0:Tt], in1=pin[3][:, 0:Tt], op=ALU.mult),
                      reads=[kb], writes=["pin3", kc_])
                S.add("act", lambda e, dq=dq, cvb=cvb, ch=ch, Tt=Tt: e.activation(
                    out=dq[:, 0:Tt], in_=cvb[:, 0:Tt], func=AF.Copy, scale=cw0[:, ch * 3:ch * 3 + 1]),
                    reads=["cvh%d" % pr, "cvb%d" % pr, "cw0"], writes=[kd])
                for tap in (1, 2):
                    S.add("dve", lambda e, dq=dq, cvb=cvb, ch=ch, tap=tap, Tt=Tt: e.scalar_tensor_tensor(
                        out=dq[:, 0:Tt], in0=cvb[:, tap:tap + Tt], scalar=cw0[:, ch * 3 + tap:ch * 3 + tap + 1],
                        in1=dq[:, 0:Tt], op0=ALU.mult, op1=ALU.add),
                        reads=["cvh%d" % pr, "cvb%d" % pr, "cw0", kd], writes=[kd])
                S.add("pool", lambda e, cvb=cvb, ch=ch, Tt=Tt: e.tensor_copy(out=carry0[:, ch, :], in_=cvb[:, Tt:Tt + 2]),
                      reads=["cvb%d" % pr], writes=[ck])
                S.add("pool", lambda e, dq=dq, cq=cq, ch=ch, Tt=Tt: e.tensor_tensor(out=yT[:, ch, 0:Tt], in0=dq[:, 0:Tt], in1=cq[:, 0:Tt], op=ALU.mult),
                      reads=[kd, kc_], writes=["yT%d" % ch])
            ykeys = ["yT%d" % ch for ch in range(NCH)]
            ouids = []
            for j in range(nsub):
                emit_outproj_post(c, yT, ykeys, wout, gpost, xr[:, j, :], xkeys[j], j, bA, state["uid"], part=1)
                ouids.append(state["uid"])
                state["uid"] += 1
            for j in range(nsub):
                emit_outproj_post(c, yT, ykeys, wout, gpost, xr[:, j, :], xkeys[j], j, bA, ouids[j], part=2)

        def l1_pre(xr, xkeys, nsub, ub):
            pre_t(nsub, ub, pre_s(xr, xkeys, nsub, gpre1))

        def l1_tile(xr, xkeys, nsub, full, halo, bb, ub, pre_done, hook1, hook2):
            Tt = nsub * 128
            uT = uTs[ub]
            if not pre_done:
                l1_pre(xr, xkeys, nsub, ub)
            ukeys = ["uT%d_%d" % (ub, j) for j in range(nsub)]
            groups = list(range(12)) if (full and not halo) else ([4, 5, 6, 7, 8, 9, 10, 11] if full else [4, 5, 6, 7, 8, 9])
            def emit_dt_stats():
                pb, pk = l1banks[state["ocnt"] % 4]
                state["ocnt"] += 1
                for kc in range(KD):
                    S.add("pe", lambda e, kc=kc, pb=pb, Tt=Tt: e.matmul(pb[0:NH, 0:Tt], lhsT=wdt[:, kc, :], rhs=uT[:, kc, 0:Tt],
                                                                      start=(kc == 0), stop=(kc == KD - 1)), reads=["wdt"] + ukeys, writes=[pk])
                S.add("act", lambda e, pb=pb, Tt=Tt: e.activation(out=dtT[0:NH, 0:Tt], in_=pb[0:NH, 0:Tt], func=AF.Exp, bias=dtb[0:NH, 0:1], scale=1.0),
                      reads=["dtb"], writes=[pk, "dtT"])
                S.add("act", lambda e, Tt=Tt: e.activation(out=dtT[0:NH, 0:Tt], in_=dtT[0:NH, 0:Tt], func=AF.Ln, bias=onec[0:NH, 0:1], scale=1.0),
                      reads=["dtT", "onec"], writes=["dtT"])
                S.add("dve", lambda e, Tt=Tt: e.tensor_scalar(out=adtT[0:NH, 0:Tt], in0=dtT[0:NH, 0:Tt], scalar1=acol[0:NH, 0:1], scalar2=None, op0=ALU.mult),
                      reads=["dtT", "acol"], writes=["adtT"])

            def emit_stats():
                for j in range(nsub):
                    js = slice(j * 128, (j + 1) * 128)
                    csj, hlj = cs[j], hl[j]
                    ckey, hkey = "cs%d" % j, "hl%d" % j
                    S.add("pe", lambda e, js=js: e.transpose(pS[:, 0:NH], dtT[0:NH, js], identf[0:NH, 0:NH]), reads=["dtT", "identf"], writes=["pC"])
                    S.add("pe", lambda e, js=js: e.transpose(pS[:, NH:2 * NH], adtT[0:NH, js], identf[0:NH, 0:NH]), reads=["adtT", "identf"], writes=["pC"])
                    S.add("act", lambda e, csj=csj: e.activation(out=csj[:, 0:2, :], in_=pS[:, 0:2 * NH].rearrange("p (a h) -> p a h", a=2), func=AF.Copy),
                          writes=["pC", ckey])
                    S.add("pe", lambda e, csj=csj: e.matmul(pS[:, 2 * NH:3 * NH], lhsT=trif[:], rhs=csj[:, 1, :], start=True, stop=True),
                          reads=[ckey, "trif"], writes=["pC"])
                    S.add("pe", lambda e, csj=csj: e.matmul(pS[:, 3 * NH:4 * NH], lhsT=onesf[:], rhs=csj[:, 1, :], start=True, stop=True),
                          reads=[ckey, "onesf"], writes=["pC"])
                    S.add("act", lambda e, csj=csj: e.activation(out=csj[:, 2, :], in_=pS[:, 2 * NH:3 * NH], func=AF.Copy), writes=["pC", ckey])
                    S.add("act", lambda e, csj=csj: e.activation(out=csj[:, 3, :], in_=pS[:, 2 * NH:3 * NH], func=AF.Copy, scale=-1.0), writes=["pC", ckey])
                    S.add("dve", lambda e, csj=csj: e.tensor_tensor(out=csj[:, 7, :], in0=pS[:, 3 * NH:4 * NH], in1=csj[:, 2, :], op=ALU.subtract),
                          writes=["pC", ckey])
                    S.add("act", lambda e, csj=csj: e.activation(out=csj[:, 6, :], in_=pS[:, 3 * NH:4 * NH], func=AF.Exp), writes=["pC", ckey])
                    S.add("act", lambda e, csj=csj: e.activation(out=csj[:, 7, :], in_=csj[:, 7, :], func=AF.Exp), reads=[ckey], writes=[ckey])
                    S.add("dve", lambda e, csj=csj: e.tensor_tensor(out=csj[:, 5, :], in0=csj[:, 7, :], in1=csj[:, 0, :], op=ALU.mult), reads=[ckey], writes=[ckey])
                    if full:
                        S.add("dve", lambda e, csj=csj, hlj=hlj: e.tensor_copy(out=hlj[:, 0, :], in_=csj[:, 1, :]), reads=[ckey], writes=[hkey])
                        S.add("dve", lambda e, csj=csj, hlj=hlj: e.tensor_tensor(out=hlj[:, 1, :], in0=csj[:, 1, :], in1=hlj[:, 0, :], op=ALU.subtract),
                              reads=[ckey, hkey], writes=[hkey])
                        S.add("dve", lambda e, csj=csj, hlj=hlj: e.tensor_copy(out=hlj[:, 2, :], in_=csj[:, 3, :]), reads=[ckey, hkey], writes=[hkey])
                        S.add("dve", lambda e, csj=csj, hlj=hlj: e.tensor_tensor(out=hlj[:, 3, :], in0=csj[:, 3, :], in1=hlj[:, 2, :], op=ALU.subtract),
                              reads=[ckey, hkey], writes=[hkey])
                    else:
                        S.add("dve", lambda e, csj=csj: e.tensor_tensor(out=dcum, in0=dcum, in1=csj[:, 6, :], op=ALU.mult), reads=[ckey, "dcum"], writes=["dcum"])

            state["ocnt"] = 0
            if not halo:
                emit_dt_stats()
            if not full:
                hook1()
            stats_after = None if halo else (3 if full else 4)
            hook1_after = 7 if full else None
            cast_ready(1)
            pend = []
            for gi in groups:
                ws = state["wcnt"] % 3
                state["wcnt"] += 1
                wbv = wbuf[ws][:].rearrange("p (kc n) -> p kc n", kc=KD)
                S.add("sp", lambda e, ws=ws, gi=gi: e.dma_start(out=wbuf[ws][:], in_=w1s[gi]), reads=["w1s%d" % gi], writes=["wb%d" % ws], dma="wb%d" % ws)
                for q in range(4):
                    o = gi * 4 + q
                    pb, pk = l1banks[state["ocnt"] % 4]
                    state["ocnt"] += 1
                    for kc in range(KD):
                        S.add("pe", lambda e, wbv=wbv, q=q, kc=kc, pb=pb, Tt=Tt: e.matmul(
                            pb[:, 0:Tt], lhsT=wbv[:, kc, q * 128:(q + 1) * 128], rhs=uT[:, kc, 0:Tt],
                            start=(kc == 0), stop=(kc == KD - 1)), reads=["wb%d" % ws] + ukeys, writes=[pk])
                    if o < 16:
                        S.add("act", lambda e, o=o, pb=pb, Tt=Tt: e.activation(out=yT[:, o, 0:Tt], in_=pb[:, 0:Tt], func=AF.Silu),
                              writes=[pk, "yT%d" % o])
                        continue
                    ci = o - 16
                    ck = "carry%d" % ci
                    if halo:
                        S.add("dve", lambda e, ci=ci, pb=pb, Tt=Tt: e.tensor_copy(out=carry1[:, ci, :], in_=pb[:, Tt - 3:Tt]), writes=[pk, ck])
                        continue
                    sl = ci % 4
                    xpb, ab = xp[sl], acc[sl]
                    S.add("act", lambda e, xpb=xpb, pb=pb, Tt=Tt: e.activation(out=xpb[:, 3:3 + Tt], in_=pb[:, 0:Tt], func=AF.Copy),
                          writes=[pk, "xpb%d" % sl])
                    S.add("pool", lambda e, xpb=xpb, ci=ci: e.tensor_copy(out=xpb[:, 0:3], in_=carry1[:, ci, :]), reads=[ck], writes=["xph%d" % sl])
                    S.add("act", lambda e, pb=pb, ab=ab, ci=ci, Tt=Tt: e.activation(
                        out=ab[:, 0:Tt], in_=pb[:, 0:Tt], func=AF.Copy, scale=cw1[:, ci * 4 + 3:ci * 4 + 4]),
                        reads=["cw1"], writes=[pk, "acc%d" % sl])
                    for tap in (0, 1, 2):
                        S.add("dve", lambda e, xpb=xpb, ab=ab, ci=ci, tap=tap, Tt=Tt: e.scalar_tensor_tensor(
                            out=ab[:, 0:Tt], in0=xpb[:, tap:tap + Tt], scalar=cw1[:, ci * 4 + tap:ci * 4 + tap + 1],
                            in1=ab[:, 0:Tt], op0=ALU.mult, op1=ALU.add),
                            reads=["xph%d" % sl, "xpb%d" % sl, "cw1", "acc%d" % sl], writes=["acc%d" % sl])
                    S.add("pool", lambda e, xpb=xpb, ci=ci, Tt=Tt: e.tensor_copy(out=carry1[:, ci, :], in_=xpb[:, Tt:Tt + 3]),
                          reads=["xpb%d" % sl], writes=[ck])
                    if ci < 16:
                        dst, dk = xT[:, ci, 0:Tt], "xT%d" % ci
                    elif ci < 24:
                        dst, dk = BT[:, ci - 16, 0:Tt], "BT%d" % (ci - 16)
                    else:
                        dst, dk = CT[:, ci - 24, 0:Tt], "CT%d" % (ci - 24)
                    pend.append(lambda dst=dst, ab=ab, ci=ci, Tt=Tt, sl=sl, dk=dk: S.add(
                        "act", lambda e: e.activation(out=dst, in_=ab[:, 0:Tt], func=AF.Silu, bias=cb1[:, ci:ci + 1], scale=1.0),
                        reads=["acc%d" % sl, "cb1"], writes=[dk]))
                    if len(pend) > 2:
                        pend.pop(0)()
                if gi == stats_after:
                    emit_stats()
                if gi == hook1_after:
                    hook1()
            while pend:
                pend.pop(0)()
            if halo:
                hook2()
                return
            assert 2 * T <= 512
            its = [(g, j) for gp in range(0, NG, 2) for j in range(nsub) for g in (gp, gp + 1)]

            def stage_a(n):
                g, j = its[n]
                js = slice(j * 128, (j + 1) * 128)
                csj, hlj = cs[j], hl[j]
                ckey, hkey = "cs%d" % j, "hl%d" % j
                sl = state["gj"] % (2 if full else 4)
                state["gj"] += 1
                xBb, xsb, xdb = xB[sl], xs[sl], xdt[sl % 2]
                for q in range(2):
                    S.add("pe", lambda e, q=q, g=g, js=js: e.transpose(tp[:, q * 128:(q + 1) * 128], xT[:, 2 * g + q, js], identb[:]),
                          reads=["xT%d" % (2 * g + q), "identb"], writes=["tp"])
                S.add("pe", lambda e, g=g, js=js: e.transpose(tp[:, 256:384], BT[:, g, js], identb[:]), reads=["BT%d" % g, "identb"], writes=["tp"])
                S.add("act", lambda e, xBb=xBb: e.activation(out=xBb[:], in_=tp[:, 0:384], func=AF.Copy), writes=["tp", "xB%d" % sl])
                S.add("dve", lambda e, xBb=xBb, xsb=xsb, csj=csj, g=g: e.tensor_tensor(
                    out=xsb[:].rearrange("p (h q) -> p h q", h=4), in0=xBb[:, 0:256].rearrange("p (h q) -> p h q", h=4),
                    in1=csj[:, 5, 4 * g:4 * g + 4].unsqueeze(2).to_broadcast([128, 4, 64]), op=ALU.mult),
                    reads=["xB%d" % sl, ckey], writes=["xs%d" % sl])
                if not full:
                    return sl
                S.add("dve", lambda e, xBb=xBb, xdb=xdb, csj=csj, g=g: e.tensor_tensor(
                    out=xdb[:].rearrange("p (h q) -> p h q", h=4), in0=xBb[:, 0:256].rearrange("p (h q) -> p h q", h=4),
                    in1=csj[:, 0, 4 * g:4 * g + 4].unsqueeze(2).to_broadcast([128, 4, 64]), op=ALU.mult),
                    reads=["xB%d" % sl, ckey], writes=["xdt%d" % sl])
                S.add("pe", lambda e, g=g, js=js: e.matmul(pC[:, 0:128], lhsT=BT[:, g, js], rhs=CT[:, g, js], start=True, stop=True),
                      reads=["BT%d" % g, "CT%d" % g], writes=["pC"])
                for i in range(4):
                    h = 4 * g + i
                    reg = slice(i * 128, (i + 1) * 128)
                    S.add("pe", lambda e, hlj=hlj, h=h, reg=reg: e.matmul(pA[:, reg], lhsT=hlj[:, 0, h:h + 1].to_broadcast([128, 128]), rhs=trib[:],
                                                                        start=True, stop=False), reads=[hkey, "trib"], writes=[pAk])
                    S.add("pe", lambda e, hlj=hlj, h=h, reg=reg: e.matmul(pA[:, reg], lhsT=hlj[:, 1, h:h + 1].to_broadcast([128, 128]), rhs=trib[:],
                                                                        start=False, stop=False), reads=[hkey, "trib"], writes=[pAk])
                    S.add("pe", lambda e, hlj=hlj, h=h, reg=reg: e.matmul(pA[:, reg], lhsT=identb[:], rhs=hlj[:, 2, h:h + 1].to_broadcast([128, 128]),
                                                                        start=False, stop=False), reads=[hkey, "identb"], writes=[pAk])
                    S.add("pe", lambda e, hlj=hlj, h=h, reg=reg: e.matmul(pA[:, reg], lhsT=identb[:], rhs=hlj[:, 3, h:h + 1].to_broadcast([128, 128]),
                                                                        start=False, stop=True), reads=[hkey, "identb"], writes=[pAk])
                    S.add("pe", lambda e, hlj=hlj, h=h, reg=reg: e.matmul(pA2[:, reg], lhsT=hlj[:, 0, h:h + 1].to_broadcast([128, 128]), rhs=trib[:],
                                                                        start=True, stop=False), reads=[hkey, "trib"], writes=[pA2k])
                    S.add("pe", lambda e, hlj=hlj, h=h, reg=reg: e.matmul(pA2[:, reg], lhsT=hlj[:, 1, h:h + 1].to_broadcast([128, 128]), rhs=trib[:],
                                                                        start=False, stop=True), reads=[hkey, "trib"], writes=[pA2k])
                S.add("act", lambda e, sl=sl: e.activation(out=dec4[sl][:], in_=pA[:, 0:512], func=AF.Exp), writes=[pAk, "dec%d" % sl])
                S.add("act", lambda e, sl=sl: e.activation(out=ebc4[sl][:], in_=pA2[:, 0:512], func=AF.Exp), writes=[pA2k, "ebc%d" % sl])
                S.add("dve", lambda e, sl=sl: e.tensor_tensor(out=m14[sl][:].rearrange("p (h l) -> p h l", h=4),
                                                              in0=dec4[sl][:].rearrange("p (h l) -> p h l", h=4),
                                                              in1=pC[:, 0:128].unsqueeze(1).to_broadcast([128, 4, 128]), op=ALU.mult),
                      reads=["dec%d" % sl], writes=["pC", "m1%d" % sl])
                S.add("pool", lambda e, sl=sl: e.affine_select(out=MT4[sl][:].rearrange("p (h l) -> p h l", h=4),
                                                               in_=m14[sl][:].rearrange("p (h l) -> p h l", h=4),
                                                               pattern=[[0, 4], [1, 128]], compare_op=ALU.is_ge, fill=0.0, base=0, channel_multiplier=-1),
                      reads=["m1%d" % sl], writes=["MT%d" % sl])
                S.add("pool", lambda e, sl=sl, g=g, js=js: e.tensor_tensor(out=CsT4[sl][:].rearrange("p (h l) -> p h l", h=4),
                                                                          in0=ebc4[sl][:].rearrange("p (h l) -> p h l", h=4),
                                                                          in1=CT[:, g, js].unsqueeze(1).to_broadcast([128, 4, 128]), op=ALU.mult),
                      reads=["ebc%d" % sl, "CT%d" % g], writes=["CsT%d" % sl])
                return sl

            def stage_b(n, sl):
                g, j = its[n]
                js = slice(j * 128, (j + 1) * 128)
                csj = cs[j]
                ckey = "cs%d" % j
                hk, hbk = "hT%d" % g, "hTb%d" % g
                xBb, xsb, xdb = xB[sl], xs[sl], xdt[sl % 2]
                if full:
                    for i in range(4):
                        h = 4 * g + i
                        cc = i // 2
                        reg = slice(i * 128, (i + 1) * 128)
                        yo = pY[g % 2][(i % 2) * 64:(i % 2 + 1) * 64, cc * T + j * 128:cc * T + (j + 1) * 128]
                        yk = pYk[g % 2]
                        S.add("pe", lambda e, yo=yo, xdb=xdb, sl=sl, i=i, reg=reg: e.matmul(yo, lhsT=xdb[:, i * 64:(i + 1) * 64], rhs=MT4[sl][:, reg], start=True, stop=False),
                              reads=["xdt%d" % sl, "MT%d" % sl], writes=[yk])
                        S.add("pe", lambda e, yo=yo, xBb=xBb, h=h, i=i: e.matmul(yo, lhsT=xBb[:, i * 64:(i + 1) * 64], rhs=DIm[:, h, :], start=False, stop=False),
                              reads=["xB%d" % sl, "DIm"], writes=[yk])
                        S.add("pe", lambda e, yo=yo, h=h, sl=sl, reg=reg: e.matmul(yo, lhsT=hTb[:, h * 64:(h + 1) * 64], rhs=CsT4[sl][:, reg], start=False, stop=True),
                              reads=[hbk, "CsT%d" % sl], writes=[yk])
                S.add("pe", lambda e, xBb=xBb, xsb=xsb: e.matmul(pC[:, 128:384], lhsT=xBb[:, 256:384], rhs=xsb[:], start=True, stop=True),
                      reads=["xB%d" % sl, "xs%d" % sl], writes=["pC"])
                hg = hT[:, g * 256:(g + 1) * 256]
                ueng = "dve" if full else "pool"
                S.add(ueng, lambda e, hg=hg, csj=csj, g=g: e.tensor_tensor(
                    out=hg.rearrange("p (h q) -> p h q", h=4), in0=hg.rearrange("p (h q) -> p h q", h=4),
                    in1=csj[:, 6, 4 * g:4 * g + 4].unsqueeze(2).to_broadcast([128, 4, 64]), op=ALU.mult), reads=[ckey], writes=[hk])
                S.add("dve", lambda e, hg=hg: e.tensor_tensor(out=hg, in0=hg, in1=pC[:, 128:384], op=ALU.add), writes=["pC", hk])
                if full:
                    S.add("act", lambda e, hg=hg, g=g: e.activation(out=hTb[:, g * 256:(g + 1) * 256], in_=hg, func=AF.Copy), reads=[hk], writes=[hbk])
                if full and j == nsub - 1:
                    for q in range(2):
                        cc = 2 * g + q
                        ygb = tmp[(2 * g + q) % 4]
                        ygk = "tmp%d" % ((2 * g + q) % 4)
                        sqb = sq[q]
                        S.add("dve", lambda e, ygb=ygb, cc=cc, q=q, g=g, Tt=Tt: e.tensor_tensor(out=ygb[:, 0:Tt], in0=pY[g % 2][:, q * T:q * T + Tt], in1=yT[:, cc, 0:Tt], op=ALU.mult),
                              reads=["yT%d" % cc], writes=[pYk[g % 2], ygk])
                        S.add("act", lambda e, ygb=ygb, sqb=sqb, Tt=Tt: e.activation(out=sqb[:, 0:Tt], in_=ygb[:, 0:Tt], func=AF.Square),
                              reads=[ygk], writes=["sq%d" % q])
                        S.add("pe", lambda e, sqb=sqb, q=q, Tt=Tt: e.matmul(pin[0][:, 0:Tt], lhsT=onesb[:], rhs=sqb[:, 0:Tt], start=(q == 0), stop=(q == 1)),
                              reads=["sq%d" % q, "onesb"], writes=["pin0"])
                    gl, gr = tmp[4], tmp[5]
                    S.add("act", lambda e, Tt=Tt: e.activation(out=gl[:, 0:Tt], in_=pin[0][:, 0:Tt], func=AF.Ln, scale=1.0 / 256, bias=eps[:, 0:1]),
                          reads=["eps"], writes=["pin0", "tmp4"])
                    S.add("act", lambda e, Tt=Tt: e.activation(out=gr[:, 0:Tt], in_=gl[:, 0:Tt], func=AF.Exp, scale=-0.5), reads=["tmp4"], writes=["tmp5"])
                    for q in range(2):
                        cc = 2 * g + q
                        ygb = tmp[(2 * g + q) % 4]
                        ygk = "tmp%d" % ((2 * g + q) % 4)
                        S.add("dve", lambda e, ygb=ygb, cc=cc, Tt=Tt: e.scalar_tensor_tensor(
                            out=yT[:, cc, 0:Tt], in0=ygb[:, 0:Tt], scalar=nw[:, cc:cc + 1], in1=gr[:, 0:Tt], op0=ALU.mult, op1=ALU.mult),
                            reads=[ygk, "nw", "tmp5"], writes=["yT%d" % cc])

            look = 1 if full else 3
            slots = {}
            for n in range(min(look, len(its))):
                slots[n] = stage_a(n)
            for n in range(len(its)):
                if n + look < len(its):
                    slots[n + look] = stage_a(n + look)
                stage_b(n, slots[n])
            if full:
                zkeys = ["yT%d" % cc for cc in range(NCH)]
                ouids = []
                for j in range(nsub):
                    emit_outproj_post(c, yT, zkeys, wout, gpost, xr[:, j, :], xkeys[j], j, bb, state["uid"], part=1)
                    ouids.append(state["uid"])
                    state["uid"] += 1
                hook2()
                for j in range(nsub):
                    emit_outproj_post(c, yT, zkeys, wout, gpost, xr[:, j, :], xkeys[j], j, bb, ouids[j], part=2)
            else:
                hook2()

        tiles = [(0, 1)] + [(HALO + i * T, NSUB) for i in range(NTOK // T)]
        def load_x(ti):
            tok0, nsub = tiles[ti]
            Tt = nsub * 128
            xr = xres[ti % 2]
            xkeys = ["xres%d_%d" % (ti % 2, j) for j in range(NSUB)]
            S.add("sp", lambda e: e.dma_start(out=xr[:, 0:nsub, :], in_=x[tok0:tok0 + Tt, :].rearrange("(j p) d -> p j d", p=128)),
                  writes=xkeys[0:nsub], dma="xld%d" % (ti % 2))
            return xr, xkeys, nsub

        nxt = load_x(0)
        l0_pre(nxt[0], nxt[1], nxt[2], 0)
        for ti, (tok0, nsub) in enumerate(tiles):
            Tt = nsub * 128
            xr, xkeys = nxt[0], nxt[1]
            l0_tile(xr, xkeys, nsub, 0)
            S.add("pool", lambda e, xr=xr, tok0=tok0, nsub=nsub, Tt=Tt: e.dma_start(
                out=h1s[tok0:tok0 + Tt, :].rearrange("(j p) d -> p j d", p=128), in_=xr[:, 0:nsub, :]),
                reads=xkeys[0:nsub], writes=["h1s%d" % ti], dma="xst%d" % (ti % 2))

            nxt_box = [None]

            def hook_a1(ti=ti, nxt_box=nxt_box):
                if ti + 1 < len(tiles):
                    r = load_x(ti + 1)
                    nxt_box[0] = r + (pre_s(r[0], r[1], r[2], gpre0),)

            def hook_a2(ti=ti, nxt_box=nxt_box):
                if ti + 1 < len(tiles):
                    r = nxt_box[0]
                    pre_t(r[2], 0, r[3])
            l1_tile(xr, xkeys, nsub, False, ti == 0, b, 1, False, hook_a1, hook_a2)
            nxt = nxt_box[0]
        S.add("sp", lambda e: e.dma_start(out=sbn[:, :], in_=hT[:]), reads=hkeys, writes=["sbn"], dma="sbn")
        S.add("sp", lambda e: e.dma_start(out=dbn[:, :], in_=dcumt[:]), reads=["dcum"], writes=["dbn"], dma="dbn")
        RG = [[0, 1, 2, 3], [4, 5, 6, 7]]
        S.add("pool", lambda e: e.collective_compute("AllGather", ALU.bypass, replica_groups=RG, ins=[sbn[:, :]], outs=[sgt[:, :]]),
              reads=["sbn"], writes=["sgt"], dma="cc", inc=1)
        S.add("pool", lambda e: e.collective_compute("AllGather", ALU.bypass, replica_groups=RG, ins=[dbn[:, :]], outs=[dgt[:, :]]),
              reads=["dbn"], writes=["dgt"], dma="cc2", inc=1)
        S.barrier()
        S.add("sp", lambda e: e.dma_start(out=gpost[:], in_=post1.partition_broadcast(128)), writes=["gpost"], dma="gpost")
        load_wout(c, wout, w1_out)
        S.add("pool", lambda e: e.memset(carry1[:], 0.0), writes=["carry%d" % i for i in range(32)])
        S.add("dve", lambda e: e.memset(hT[:], 0.0), writes=hkeys)
        S.add("sp", lambda e: e.dma_start(out=dstage[:], in_=dgt.rearrange("(k p) h -> p k h", p=128)), reads=["dgt"], writes=["dstage"], dma="dstage")
        for k in range(3):
            S.add("sp", lambda e, k=k: e.dma_start(out=sstage[:], in_=sgt[k * 128:(k + 1) * 128, :]), reads=["sgt"], writes=["sstage"], dma="sstage")
            S.add("dve", lambda e, k=k: e.tensor_scalar(out=fac[:], in0=dstage[:, k, 0:NH], scalar1=cm[:, k:k + 1],
                                                        scalar2=cm[:, 4 + k:5 + k], op0=ALU.mult, op1=ALU.add),
                  reads=["dstage", "cm"], writes=["fac"])
            S.add("dve", lambda e: e.tensor_tensor(out=hT[:].rearrange("p (h q) -> p h q", h=NH),
                                                   in0=hT[:].rearrange("p (h q) -> p h q", h=NH),
                                                   in1=fac[:].unsqueeze(2).to_broadcast([128, NH, 64]), op=ALU.mult),
                  reads=["fac"], writes=hkeys)
            S.add("dve", lambda e, k=k: e.scalar_tensor_tensor(out=hT[:], in0=sstage[:, 0:DI], scalar=cm[:, k:k + 1], in1=hT[:],
                                                               op0=ALU.mult, op1=ALU.add),
                  reads=["sstage", "cm"], writes=hkeys)
        S.add("act", lambda e: e.activation(out=hTb[:], in_=hT[:], func=AF.Copy), reads=hkeys, writes=["hTb%d" % g for g in range(NG)])
        def load_h(ti):
            tok0, nsub = tiles[ti]
            Tt = nsub * 128
            xr = xres[ti % 2]
            xkeys = ["xres%d_%d" % (ti % 2, j) for j in range(NSUB)]
            S.add("sp", lambda e: e.dma_start(out=xr[:, 0:nsub, :], in_=h1s[tok0:tok0 + Tt, :].rearrange("(j p) d -> p j d", p=128)),
                  reads=["h1s%d" % ti], writes=xkeys[0:nsub], dma="xld%d" % (ti % 2))
            return xr, xkeys, nsub

        nxt = load_h(0)
        l1_pre(nxt[0], nxt[1], nxt[2], 0)
        for ti, (tok0, nsub) in enumerate(tiles):
            Tt = nsub * 128
            xr, xkeys = nxt[0], nxt[1]
            nxt_box = [None]

            def hook_b1(ti=ti, nxt_box=nxt_box):
                if ti + 1 < len(tiles):
                    r = load_h(ti + 1)
                    nxt_box[0] = r + (pre_s(r[0], r[1], r[2], gpre1),)

            def hook_b2(ti=ti, nxt_box=nxt_box):
                if ti + 1 < len(tiles):
                    r = nxt_box[0]
                    pre_t(r[2], (ti + 1) % 2, r[3])
            l1_tile(xr, xkeys, nsub, True, ti == 0, bB, ti % 2, True, hook_b1, hook_b2)
            nxt = nxt_box[0]
            if ti > 0:
                S.add("pool", lambda e, xr=xr, tok0=tok0, nsub=nsub, Tt=Tt: e.dma_start(
                    out=out[tok0 - HALO:tok0 - HALO + Tt, :].rearrange("(j p) d -> p j d", p=128), in_=xr[:, 0:nsub, :]),
                    reads=xkeys[0:nsub], writes=["outd"], dma="xst%d" % (ti % 2))
        S.add("sp", lambda e: e.nop(), reads=["outd"])
        S.emit(nc, st)
    return nc


TF = 256
_PROGS = {}


def fused_maps(x, pre_norm, post_norm, sc_w_in, sc_conv_w, sc_w_out, ssd_w_in, ssd_conv_w, ssd_conv_b,
               ssd_dt_bias, ssd_a_log, ssd_d_skip, ssd_norm, ssd_w_out, NTOK):
    B, L, _ = x.shape
    cpb = L // NTOK
    cwh = np.ascontiguousarray(sc_conv_w[0].reshape(3, NCH, 128).transpose(2, 1, 0)).reshape(128, NCH * 3)
    pm = l1_param_maps(ssd_w_in, ssd_conv_w, ssd_conv_b, ssd_dt_bias, ssd_a_log)
    ex = l1_full_extra(post_norm, ssd_d_skip, ssd_norm, ssd_w_out)
    shared = {"pre0": np.ascontiguousarray(pre_norm[0]), "post0": np.ascontiguousarray(post_norm[0]),
              "w0_in": np.ascontiguousarray(sc_w_in[0]), "cw0h": cwh, "w0_out": np.ascontiguousarray(sc_w_out[0]),
              "pre1": np.ascontiguousarray(pre_norm[1]), "post1": ex["post"], "w1_in": pm["w_in"], "cw1h": pm["cw1h"],
              "cb1h": pm["cb1h"], "dtbh": pm["dtbh"], "alogh": pm["alogh"], "dskh": ex["dskh"], "nwh": ex["nwh"],
              "w1_out": ex["w_out"]}
    maps = []
    for core in range(B * cpb):
        bi, ci = divmod(core, cpb)
        cm = np.zeros((128, 8), np.float32)
        for k in range(4):
            cm[:, k] = 1.0 if k < ci else 0.0
        cm[:, 4:8] = 1.0 - cm[:, 0:4]
        m = dict(shared)
        m["x"] = core_tokens(x[bi], ci * NTOK, NTOK)
        m["cmask"] = cm
        maps.append(m)
    return maps


def kernel(x, pre_norm, post_norm, sc_w_in, sc_conv_w, sc_w_out, ssd_w_in, ssd_conv_w, ssd_conv_b,
           ssd_dt_bias, ssd_a_log, ssd_d_skip, ssd_norm, ssd_w_out):
    f = lambda a: np.ascontiguousarray(np.asarray(a), dtype=np.float32)
    args = [f(a) for a in (x, pre_norm, post_norm, sc_w_in, sc_conv_w, sc_w_out, ssd_w_in, ssd_conv_w, ssd_conv_b,
                           ssd_dt_bias, ssd_a_log, ssd_d_skip, ssd_norm, ssd_w_out)]
    B, L, _ = args[0].shape
    ncores = 8
    NTOK = B * L // ncores
    if ("fused", NTOK) not in _PROGS:
        _PROGS[("fused", NTOK)] = build_fused(NTOK, TF)
    nc = _PROGS[("fused", NTOK)]
    res = run_bass_kernel_spmd(nc, fused_maps(*args, NTOK), core_ids=list(range(ncores)))
    out = np.concatenate([r["out"] for r in res.results], 0).reshape(B, L, D)
    return out.astype(np.float32)
```

```python
from contextlib import ExitStack
import numpy as np
import concourse.bass as bass
import concourse.mybir as mybir
from concourse.bass_utils import run_bass_kernel_spmd

F32 = mybir.dt.float32
BF16 = mybir.dt.bfloat16
AF = mybir.ActivationFunctionType
ALU = mybir.AluOpType

D = 1024
KD = 8
DI = 2048
NCH = 16
NH = 32
NG = 8
HALO = 128
EPS = 1e-6
L1IN = 6176


class Sched:
    ENGS = ("pe", "act", "dve", "pool", "sp")

    def __init__(self):
        self.ops = {e: [] for e in self.ENGS}
        self.last_w = {}
        self.readers = {}
        self.dma_cnt = {}
        self.dma_inc = {}

    def barrier(self):
        toks = set()
        for e in self.ENGS:
            for i in range(len(self.ops[e]) - 1, -1, -1):
                if self.ops[e][i]["dma"] is None:
                    toks.add(("eng", e, i))
                    break
        for k, cnt in self.dma_cnt.items():
            toks.add(("dma", k, cnt))
        self.pending = {e: set(toks) for e in self.ENGS}

    def add(self, eng, fn, reads=(), writes=(), dma=None, inc=16):
        deps = set()
        if getattr(self, "pending", None) and self.pending.get(eng):
            deps |= self.pending[eng]
            self.pending[eng] = set()
        for r in reads:
            t = self.last_w.get(r)
            if t is not None:
                deps.add(t)
        for w in writes:
            t = self.last_w.get(w)
            if t is not None:
                deps.add(t)
            for t in self.readers.get(w, ()):
                deps.add(t)
        idx = len(self.ops[eng])
        if dma is not None:
            c = self.dma_cnt.get(dma, 0) + 1
            self.dma_cnt[dma] = c
            tok = ("dma", dma, c)
        else:
            tok = ("eng", eng, idx)
        if eng == "pe":
            deps = {d for d in deps if not (d[0] == "eng" and d[1] == "pe")}
        deps.discard(tok)
        if dma is not None:
            self.dma_inc[dma] = inc
        self.ops[eng].append(dict(fn=fn, deps=deps, tok=tok, dma=dma))
        for r in reads:
            lst = self.readers.setdefault(r, [])
            if tok[0] == "eng":
                lst[:] = [t for t in lst if not (t[0] == "eng" and t[1] == eng)]
            lst.append(tok)
        for w in writes:
            self.last_w[w] = tok
            self.readers[w] = []
        return tok

    def emit(self, nc, stack):
        sig = {e: [False] * len(self.ops[e]) for e in self.ENGS}
        for e in self.ENGS:
            for op in self.ops[e]:
                for d in op["deps"]:
                    if d[0] == "eng":
                        sig[d[1]][d[2]] = True
        cum = {}
        for e in self.ENGS:
            c = 0
            arr = []
            for s in sig[e]:
                if s:
                    c += 1
                arr.append(c)
            cum[e] = arr
        esem = {e: stack.enter_context(nc.semaphore("s_" + e)) for e in self.ENGS}
        dsem = {k: stack.enter_context(nc.semaphore("d_" + str(k))) for k in self.dma_cnt}
        block = stack.enter_context(nc.Block())

        def run(e, engobj):
            known = {}
            for i, op in enumerate(self.ops[e]):
                need = {}
                for d in op["deps"]:
                    if d[0] == "eng":
                        s, v = esem[d[1]], cum[d[1]][d[2]]
                    else:
                        s, v = dsem[d[1]], self.dma_inc[d[1]] * d[2]
                    if need.get(s.name, (None, 0))[1] < v:
                        need[s.name] = (s, v)
                for nm, (s, v) in need.items():
                    if known.get(nm, 0) >= v:
                        continue
                    engobj.wait_ge(s, v)
                    known[nm] = v
                ins = op["fn"](engobj)
                if op["dma"] is not None:
                    ins.then_inc(dsem[op["dma"]], self.dma_inc[op["dma"]])
                elif sig[e][i]:
                    ins.then_inc(esem[e], 1)

        block.tensor(lambda eng: run("pe", eng))
        block.scalar(lambda eng: run("act", eng))
        block.vector(lambda eng: run("dve", eng))
        block.gpsimd(lambda eng: run("pool", eng))
        block.sync(lambda eng: run("sp", eng))


class Ctx:
    def __init__(self, nc, st):
        self.nc = nc
        self.st = st
        self.S = Sched()

    def sb(self, name, shape, dt):
        return self.st.enter_context(self.nc.sbuf_tensor(name, shape, dt))

    def ps(self, name, shape, dt):
        return self.st.enter_context(self.nc.psum_tensor(name, shape, dt))

    def din(self, name, shape, dt=F32):
        return self.nc.dram_tensor(name, list(shape), dt, kind="ExternalInput").ap()

    def dout(self, name, shape, dt=F32):
        return self.nc.dram_tensor(name, list(shape), dt, kind="ExternalOutput").ap()


def make_ident(c, t, key):
    S = c.S
    S.add("pool", lambda e: e.memset(t[:], 0.0), writes=[key])
    S.add("pool", lambda e: e.affine_select(out=t[:], in_=t[:], pattern=[[-1, 128]],
                                            compare_op=ALU.not_equal, fill=1.0, base=0,
                                            channel_multiplier=1), reads=[key], writes=[key])


def make_tri(c, t, key):
    S = c.S
    S.add("pool", lambda e: e.memset(t[:], 1.0), writes=[key])
    S.add("pool", lambda e: e.affine_select(out=t[:], in_=t[:], pattern=[[1, 128]],
                                            compare_op=ALU.is_ge, fill=0.0, base=0,
                                            channel_multiplier=-1), reads=[key], writes=[key])


def emit_prenorm_stats(c, xres_j, xkey, gpre, bufs, uid):
    S = c.S
    stat, junk, utok, eps = bufs["stat"], bufs["junk"], bufs["utok"], bufs["eps"]
    sl = uid % 4
    st_ = stat[sl]
    sk = "stat%d" % sl
    ut = utok[uid % 2]
    uk = "utok%d" % (uid % 2)
    S.add("act", lambda e: e.activation(out=junk[:], in_=xres_j, func=AF.Square, accum_out=st_[:, 0:1]),
          reads=[xkey], writes=["junk", sk])
    S.add("act", lambda e: e.activation(out=st_[:, 1:2], in_=st_[:, 0:1], func=AF.Ln, scale=1.0 / D,
                                        bias=eps[:, 0:1]), reads=[sk, "eps"], writes=[sk])
    S.add("act", lambda e: e.activation(out=st_[:, 2:3], in_=st_[:, 1:2], func=AF.Exp, scale=-0.5),
          reads=[sk], writes=[sk])
    S.add("dve", lambda e: e.scalar_tensor_tensor(out=ut[:], in0=xres_j, scalar=st_[:, 2:3], in1=gpre[:],
                                                  op0=ALU.mult, op1=ALU.mult),
          reads=[xkey, sk, "gpre"], writes=[uk])


def emit_prenorm_T(c, uT, j, bufs, uid, ukp="uT"):
    S = c.S
    utok, tp, identb = bufs["utok"], bufs["tp"], bufs["identb"]
    ut = utok[uid % 2]
    uk = "utok%d" % (uid % 2)
    for kc in range(KD):
        S.add("pe", lambda e, kc=kc: e.transpose(tp[:, kc * 128:(kc + 1) * 128], ut[:, kc * 128:(kc + 1) * 128], identb[:]),
              reads=[uk, "identb"], writes=["tp"])
    S.add("act", lambda e: e.activation(out=uT[:, :, j * 128:(j + 1) * 128],
                                        in_=tp[:, 0:1024].rearrange("p (k t) -> p k t", k=KD), func=AF.Copy),
          reads=[uk], writes=["tp", "%s%d" % (ukp, j)])


def emit_prenorm(c, xres_j, xkey, gpre, uT, j, bufs, uid, ukp="uT"):
    emit_prenorm_stats(c, xres_j, xkey, gpre, bufs, uid)
    emit_prenorm_T(c, uT, j, bufs, uid, ukp)


def emit_outproj_post(c, lhs_buf, lhs_keys, wout, gpost, xres_j, xkey, j, bufs, uid, part=3):
    S = c.S
    stat, junk, eps, mres = bufs["stat"], bufs["junk"], bufs["eps"], bufs["mres"]
    po = bufs["po"][2 * (j % 2):2 * (j % 2) + 2] if len(bufs["po"]) >= 4 else bufs["po"]
    pok = bufs["pokeys"][2 * (j % 2):2 * (j % 2) + 2] if len(bufs["po"]) >= 4 else bufs["pokeys"]
    for hf in range(2 if (part & 1) else 0):
        for cc in range(NCH):
            S.add("pe", lambda e, hf=hf, cc=cc: e.matmul(po[hf][:], lhsT=lhs_buf[:, cc, j * 128:(j + 1) * 128],
                                                         rhs=wout[:, cc, hf * 512:(hf + 1) * 512],
                                                         start=(cc == 0), stop=(cc == NCH - 1)),
                  reads=[lhs_keys[cc], "wout"], writes=[pok[hf]])
    if not (part & 2):
        return
    sl = uid % 4
    st_ = stat[sl]
    sk = "stat%d" % sl
    for hf in range(2):
        S.add("act", lambda e, hf=hf: e.activation(out=junk[:, 0:512], in_=po[hf][:], func=AF.Square,
                                                   accum_out=st_[:, 3 + hf:4 + hf]),
              writes=[pok[hf], "junk", sk])
    S.add("dve", lambda e: e.tensor_tensor(out=st_[:, 5:6], in0=st_[:, 3:4], in1=st_[:, 4:5], op=ALU.add),
          reads=[sk], writes=[sk])
    S.add("act", lambda e: e.activation(out=st_[:, 6:7], in_=st_[:, 5:6], func=AF.Ln, scale=1.0 / D,
                                        bias=eps[:, 0:1]), reads=[sk, "eps"], writes=[sk])
    S.add("act", lambda e: e.activation(out=st_[:, 7:8], in_=st_[:, 6:7], func=AF.Exp, scale=-0.5),
          reads=[sk], writes=[sk])
    for hf in range(2):
        S.add("dve", lambda e, hf=hf: e.scalar_tensor_tensor(out=mres[:, hf * 512:(hf + 1) * 512], in0=po[hf][:],
                                                             scalar=st_[:, 7:8], in1=gpost[:, hf * 512:(hf + 1) * 512],
                                                             op0=ALU.mult, op1=ALU.mult),
              reads=[sk, "gpost"], writes=[pok[hf], "mres%d" % hf])
    S.add("pool", lambda e: e.tensor_tensor(out=xres_j, in0=xres_j, in1=mres[:], op=ALU.add),
          reads=["mres0", "mres1", xkey], writes=[xkey])


def common_bufs(c, po=None, pokeys=None):
    b = {}
    b["stat"] = [c.sb("stat%d" % i, [128, 8], F32) for i in range(4)]
    b["junk"] = c.sb("junk", [128, 1024], BF16)
    b["utok"] = [c.sb("utok%d" % i, [128, 1024], BF16) for i in range(2)]
    b["tp"] = c.ps("tp", [128, 1024], BF16)
    b["identb"] = c.sb("identb", [128, 128], BF16)
    b["eps"] = c.sb("eps", [128, 1], F32)
    b["mres"] = c.sb("mres", [128, 1024], F32)
    if po is None:
        po = [c.ps("po%d" % i, [128, 512], F32) for i in range(2)]
        pokeys = ["po0", "po1"]
    b["po"] = po
    b["pokeys"] = pokeys
    make_ident(c, b["identb"], "identb")
    c.S.add("dve", lambda e: e.memset(b["eps"][:], EPS), writes=["eps"])
    return b


def load_wout(c, wout_sb, w_out_dram):
    src = w_out_dram.rearrange("(c p) n -> p c n", p=128)
    for q in range(4):
        c.S.add("pool", lambda e, q=q: e.dma_start(out=wout_sb[:, q * 4:(q + 1) * 4, :], in_=src[:, q * 4:(q + 1) * 4, :]),
                writes=["wout"], dma="wout%d" % q)


def core_tokens(xflat_b, start, NTOK):
    out = np.zeros((HALO + NTOK, xflat_b.shape[1]), np.float32)
    lo = max(0, start - HALO)
    out[HALO - (start - lo):] = xflat_b[lo:start + NTOK]
    return out


def l1_param_maps(ssd_w_in, ssd_conv_w, ssd_conv_b, ssd_dt_bias, ssd_a_log):
    cw1h = np.ascontiguousarray(ssd_conv_w[0].reshape(4, 32, 128).transpose(2, 1, 0)).reshape(128, 128)
    cb1h = np.ascontiguousarray(ssd_conv_b[0].reshape(32, 128).T)
    dtbh = np.zeros((128, 1), np.float32)
    dtbh[:NH, 0] = ssd_dt_bias[0]
    alogh = np.zeros((128, 1), np.float32)
    alogh[:NH, 0] = ssd_a_log[0]
    return {"w_in": np.ascontiguousarray(ssd_w_in[0]), "cw1h": cw1h, "cb1h": cb1h, "dtbh": dtbh, "alogh": alogh}


def l1_full_extra(post_norm, ssd_d_skip, ssd_norm, ssd_w_out):
    dskh = np.ascontiguousarray(np.broadcast_to(ssd_d_skip[0][None, :], (128, NH))).astype(np.float32)
    nwh = np.ascontiguousarray(ssd_norm[0].reshape(NCH, 128).T)
    return {"post": np.ascontiguousarray(post_norm[1]), "dskh": dskh, "nwh": nwh,
            "w_out": np.ascontiguousarray(ssd_w_out[0])}


def build_fused(NTOK, T):
    nc = bass.Bass("TRN2", target_bir_lowering=False)
    with ExitStack() as st:
        c = Ctx(nc, st)
        S = c.S
        NSUB = T // 128
        x = c.din("x", [HALO + NTOK, D])
        pre0 = c.din("pre0", [D])
        post0 = c.din("post0", [D])
        w0_in = c.din("w0_in", [D, 4 * DI])
        cw0h = c.din("cw0h", [128, NCH * 3])
        w0_out = c.din("w0_out", [DI, D])
        pre1 = c.din("pre1", [D])
        post1 = c.din("post1", [D])
        w1_in = c.din("w1_in", [D, L1IN])
        cw1h = c.din("cw1h", [128, 32 * 4])
        cb1h = c.din("cb1h", [128, 32])
        dtbh = c.din("dtbh", [128, 1])
        alogh = c.din("alogh", [128, 1])
        dskh = c.din("dskh", [128, NH])
        nwh = c.din("nwh", [128, NCH])
        w1_out = c.din("w1_out", [DI, D])
        cmask = c.din("cmask", [128, 8])
        out = c.dout("out", [NTOK, D])
        w0s = nc.dram_tensor("w0s", [NCH, 128, KD * 4 * 128], BF16, kind="Internal").ap()
        w1s = nc.dram_tensor("w1s", [12, 128, KD * 512], BF16, kind="Internal").ap()
        h1s = nc.dram_tensor("h1s", [HALO + NTOK, D], F32, kind="Internal").ap()
        sbn = nc.dram_tensor("sbn", [128, DI], F32, kind="Internal").ap()
        sgt = nc.dram_tensor("sgt", [4 * 128, DI], F32, kind="Internal").ap()
        dbn = nc.dram_tensor("dbn", [128, 64], F32, kind="Internal").ap()
        dgt = nc.dram_tensor("dgt", [4 * 128, 64], F32, kind="Internal").ap()

        pin = [c.ps("pin%d" % i, [128, 512], F32) for i in range(4)]
        b = common_bufs(c)
        po = b["po"]
        identb, eps, tp = b["identb"], b["eps"], b["tp"]
        pC = c.ps("pC", [128, 512], F32)
        pS = pC[:, 384:512]
        pA = po[0]
        pAk = "po0"
        pA2 = po[1]
        pA2k = "po1"
        pY = [pin[2], pin[3]]
        pYk = ["pin2", "pin3"]
        bB = dict(b)
        bB["po"] = [pin[2], pin[3], po[0], po[1]]
        bB["pokeys"] = ["pin2", "pin3", "po0", "po1"]
        bA = dict(b)
        bA["po"] = [po[0], po[1], pin[0], pin[1]]
        bA["pokeys"] = ["po0", "po1", "pin0", "pin1"]

        gpre0 = c.sb("gpre0", [128, D], F32)
        gpre1 = c.sb("gpre1", [128, D], F32)
        gpost = c.sb("gpost", [128, D], F32)
        cw0 = c.sb("cw0", [128, NCH * 3], F32)
        carry0 = c.sb("carry0", [128, NCH, 2], F32)
        wout = c.sb("wout", [128, NCH, D], BF16)
        xres = [c.sb("xres%d" % i, [128, NSUB, D], F32) for i in range(2)]
        uTs = [c.sb("uT%d" % i, [128, KD, T], BF16) for i in range(2)]
        wbuf = [c.sb("wbuf%d" % i, [128, KD * 512], BF16) for i in range(3)]
        yT = c.sb("yT", [128, NCH, T], BF16)
        tmp = [c.sb("tmp%d" % i, [128, T], F32) for i in range(8)]
        cv = [c.sb("cv%d" % i, [128, T + 2], F32) for i in range(2)]
        cw1 = c.sb("cw1", [128, 32 * 4], F32)
        cb1 = c.sb("cb1", [128, 32], F32)
        dtb = c.sb("dtb", [128, 1], F32)
        acol = c.sb("acol", [128, 1], F32)
        onec = c.sb("onec", [128, 1], F32)
        identf = c.sb("identf", [128, 128], F32)
        trif = c.sb("trif", [128, 128], F32)
        onesf = c.sb("onesf", [128, 128], F32)
        wdt = c.sb("wdt", [128, KD, NH], BF16)
        carry1 = c.sb("carry1", [128, 32, 3], F32)
        xT = c.sb("xT", [128, NCH, T], BF16)
        BT = c.sb("BT", [128, NG, T], BF16)
        CT = c.sb("CT", [128, NG, T], BF16)
        xp = [c.sb("xp%d" % i, [128, T + 3], F32) for i in range(4)]
        acc = [c.sb("acc%d" % i, [128, T], F32) for i in range(4)]
        l1banks = [(pin[0], "pin0"), (pin[1], "pin1"), (po[0], "po0"), (po[1], "po1")]
        dtT = c.sb("dtT", [128, T], F32)
        adtT = c.sb("adtT", [128, T], F32)
        cs = [c.sb("cs%d" % i, [128, 8, NH], F32) for i in range(NSUB)]
        hl = [c.sb("hl%d" % i, [128, 4, NH], BF16) for i in range(NSUB)]
        xdt = [c.sb("xdt%d" % i, [128, 256], BF16) for i in range(2)]
        xB = [c.sb("xB%d" % i, [128, 384], BF16) for i in range(4)]
        xs = [c.sb("xs%d" % i, [128, 256], BF16) for i in range(4)]
        hT = c.sb("hT", [128, DI], F32)
        dcumt = c.sb("dcumt", [128, 64], F32)
        dcum = dcumt[:, 0:NH]
        dstage = c.sb("dstage", [128, 4, 64], F32)
        dsk = c.sb("dsk", [128, NH], F32)
        nw = c.sb("nw", [128, NCH], F32)
        cm = c.sb("cm", [128, 8], F32)
        trib = c.sb("trib", [128, 128], BF16)
        onesb = c.sb("onesb", [128, 128], BF16)
        DIm = c.sb("DIm", [128, NH, 128], BF16)
        hTb = c.sb("hTb", [128, DI], BF16)
        dec4 = [c.sb("dec%d" % i, [128, 512], F32) for i in range(2)]
        ebc4 = [c.sb("ebc%d" % i, [128, 512], F32) for i in range(2)]
        m14 = [c.sb("m1%d" % i, [128, 512], F32) for i in range(2)]
        MT4 = [c.sb("MT%d" % i, [128, 512], BF16) for i in range(2)]
        CsT4 = [c.sb("CsT%d" % i, [128, 512], BF16) for i in range(2)]
        sstage = c.sb("sstage", [128, DI], F32)
        fac = c.sb("fac", [128, NH], F32)
        sq = [c.sb("sq%d" % i, [128, T], BF16) for i in range(2)]
        hkeys = ["hT%d" % g for g in range(NG)]

        w0src = w0_in.rearrange("(kc p) n -> p kc n", p=128)
        for ch in range(NCH):
            dst = w0s[ch].rearrange("p (kc w n) -> p kc w n", kc=KD, w=4)
            for which in range(4):
                col0 = which * DI + ch * 128
                S.add("pool", lambda e, dst=dst, which=which, col0=col0: e.dma_start(out=dst[:, :, which, :], in_=w0src[:, :, col0:col0 + 128]),
                      writes=["w0sraw%d_%d" % (ch, which)], dma="cast%d" % ((ch * 4 + which) % 4))
        w1src = w1_in.rearrange("(kc p) n -> p kc n", p=128)
        for t_, src_, k_ in ((gpre0, pre0, "gpre0"), (gpre1, pre1, "gpre1"), (gpost, post0, "gpost")):
            S.add("sp", lambda e, t_=t_, src_=src_: e.dma_start(out=t_[:], in_=src_.partition_broadcast(128)), writes=[k_], dma=k_)
        for t_, src_, k_ in ((cw0, cw0h, "cw0"), (cw1, cw1h, "cw1"), (cb1, cb1h, "cb1"), (dtb, dtbh, "dtb"), (acol, alogh, "acol"),
                             (dsk, dskh, "dsk"), (nw, nwh, "nw"), (cm, cmask, "cm")):
            S.add("sp", lambda e, t_=t_, src_=src_: e.dma_start(out=t_[:], in_=src_[:, :]), writes=[k_], dma=k_)
        S.add("pool", lambda e: e.dma_start(out=wdt[:], in_=w1src[:, :, L1IN - NH:L1IN]), writes=["wdt"], dma="wdt")
        load_wout(c, wout, w0_out)
        S.add("act", lambda e: e.activation(out=acol[:], in_=acol[:], func=AF.Exp), reads=["acol"], writes=["acol"])
        S.add("dve", lambda e: e.tensor_scalar(out=acol[:], in0=acol[:], scalar1=-1.0, scalar2=None, op0=ALU.mult),
              reads=["acol"], writes=["acol"])
        S.add("dve", lambda e: e.memset(onec[:], 1.0), writes=["onec"])
        S.add("pool", lambda e: e.memset(onesf[:], 1.0), writes=["onesf"])
        S.add("pool", lambda e: e.memset(onesb[:], 1.0), writes=["onesb"])
        S.add("pool", lambda e: e.memset(carry0[:], 0.0), writes=["carry0_%d" % i for i in range(NCH)])
        S.add("pool", lambda e: e.memset(carry1[:], 0.0), writes=["carry%d" % i for i in range(32)])
        make_ident(c, identf, "identf")
        make_tri(c, trif, "trif")
        make_tri(c, trib, "trib")
        for h in range(NH):
            S.add("dve", lambda e, h=h: e.tensor_scalar(out=DIm[:, h, :], in0=identf[:], scalar1=dsk[:, h:h + 1], scalar2=None, op0=ALU.mult),
                  reads=["identf", "dsk"], writes=["DIm"])
        S.add("dve", lambda e: e.memset(hT[:], 0.0), writes=hkeys)
        S.add("dve", lambda e: e.memset(dcumt[:], 1.0), writes=["dcum"])
        for gi in range(12):
            S.add("pool", lambda e, gi=gi: e.dma_start(out=w1s[gi].rearrange("p (kc n) -> p kc n", kc=KD), in_=w1src[:, :, gi * 512:(gi + 1) * 512]),
                  writes=["w1sraw%d" % gi], dma="cast%d" % (gi % 4))

        state = dict(uid=0, wcnt=0, ocnt=0, gj=0, w0ready=False, w1ready=False)

        def cast_ready(which_layer):
            if which_layer == 0 and not state["w0ready"]:
                state["w0ready"] = True
                S.add("sp", lambda e: e.nop(), reads=["w0sraw%d_%d" % (ch, w) for ch in range(NCH) for w in range(4)],
                      writes=["w0s%d" % ch for ch in range(NCH)])
            if which_layer == 1 and not state["w1ready"]:
                state["w1ready"] = True
                S.add("sp", lambda e: e.nop(), reads=["w1sraw%d" % gi for gi in range(12)] + ["w0sraw%d_%d" % (ch, w) for ch in range(NCH) for w in range(4)],
                      writes=["w1s%d" % gi for gi in range(12)])

        def pre_s(xr, xkeys, nsub, gp):
            uids = []
            for j in range(nsub):
                emit_prenorm_stats(c, xr[:, j, :], xkeys[j], gp, b, state["uid"])
                uids.append(state["uid"])
                state["uid"] += 1
            return uids

        def pre_t(nsub, ub, uids):
            for j in range(nsub):
                emit_prenorm_T(c, uTs[ub], j, b, uids[j], ukp="uT%d_" % ub)

        def l0_pre(xr, xkeys, nsub, ub):
            pre_t(nsub, ub, pre_s(xr, xkeys, nsub, gpre0))

        def l0_tile(xr, xkeys, nsub, ub):
            Tt = nsub * 128
            uT = uTs[ub]
            ukeys = ["uT%d_%d" % (ub, j) for j in range(nsub)]
            cast_ready(0)
            for ch in range(NCH):
                ws = state["wcnt"] % 3
                state["wcnt"] += 1
                wbv = wbuf[ws][:].rearrange("p (kc w n) -> p kc w n", kc=KD, w=4)
                S.add("sp", lambda e, ws=ws, ch=ch: e.dma_start(out=wbuf[ws][:], in_=w0s[ch]), reads=["w0s%d" % ch], writes=["wb%d" % ws], dma="wb%d" % ws)
                pr = ch % 2
                for which, bank in ((2, 0), (3, 1), (0, 2), (1, 3)):
                    for kc in range(KD):
                        S.add("pe", lambda e, wbv=wbv, which=which, bank=bank, kc=kc, Tt=Tt: e.matmul(
                            pin[bank][:, 0:Tt], lhsT=wbv[:, kc, which, :], rhs=uT[:, kc, 0:Tt], start=(kc == 0), stop=(kc == KD - 1)),
                            reads=["wb%d" % ws] + ukeys, writes=["pin%d" % bank])
                a, bq, cq, dq, cvb = tmp[pr], tmp[2 + pr], tmp[4 + pr], tmp[6 + pr], cv[pr]
                ka, kb, kc_, kd = "tmp%d" % pr, "tmp%d" % (2 + pr), "tmp%d" % (4 + pr), "tmp%d" % (6 + pr)
                ck = "carry0_%d" % ch
                S.add("act", lambda e, a=a, Tt=Tt: e.activation(out=a[:, 0:Tt], in_=pin[0][:, 0:Tt], func=AF.Copy), writes=["pin0", ka])
                S.add("pool", lambda e, cvb=cvb, ch=ch: e.tensor_copy(out=cvb[:, 0:2], in_=carry0[:, ch, :]), reads=[ck], writes=["cvh%d" % pr])
                S.add("dve", lambda e, cvb=cvb, a=a, Tt=Tt: e.tensor_tensor(out=cvb[:, 2:2 + Tt], in0=a[:, 0:Tt], in1=pin[1][:, 0:Tt], op=ALU.mult),
                      reads=[ka], writes=["pin1", "cvb%d" % pr])
                S.add("act", lambda e, bq=bq, Tt=Tt: e.activation(out=bq[:, 0:Tt], in_=pin[2][:, 0:Tt], func=AF.Silu), writes=["pin2", kb])
                S.add("dve", lambda e, cq=cq, bq=bq, Tt=Tt: e.tensor_tensor(out=cq[:, 0:Tt], in0=bq[:, 0:Tt], in1=pin[3][:, 0:Tt], op=ALU.mult),
                      reads=[kb], writes=["pin3", kc_])
                S.add("act", lambda e, dq=dq, cvb=cvb, ch=ch, Tt=Tt: e.activation(
                    out=dq[:, 0:Tt], in_=cvb[:, 0:Tt], func=AF.Copy, scale=cw0[:, ch * 3:ch * 3 + 1]),
                    reads=["cvh%d" % pr, "cvb%d" % pr, "cw0"], writes=[kd])
                for tap in (1, 2):
                    S.add("dve", lambda e, dq=dq, cvb=cvb, ch=ch, tap=tap, Tt=Tt: e.scalar_tensor_tensor(
                        out=dq[:, 0:Tt], in0=cvb[:, tap:tap + Tt], scalar=cw0[:, ch * 3 + tap:ch * 3 + tap + 1],
                        in1=dq[:, 0:Tt], op0=ALU.mult, op1=ALU.add),
                        reads=["cvh%d" % pr, "cvb%d" % pr, "cw0", kd], writes=[kd])
                S.add("pool", lambda e, cvb=cvb, ch=ch, Tt=Tt: e.tensor_copy(out=carry0[:, ch, :], in_=cvb[:, Tt:Tt + 2]),
                      reads=["cvb%d" % pr], writes=[ck])
                S.add("pool", lambda e, dq=dq, cq=cq, ch=ch, Tt=Tt: e.tensor_tensor(out=yT[:, ch, 0:Tt], in0=dq[:, 0:Tt], in1=cq[:, 0:Tt], op=ALU.mult),
                      reads=[kd, kc_], writes=["yT%d" % ch])
            ykeys = ["yT%d" % ch for ch in range(NCH)]
            ouids = []
            for j in range(nsub):
                emit_outproj_post(c, yT, ykeys, wout, gpost, xr[:, j, :], xkeys[j], j, bA, state["uid"], part=1)
                ouids.append(state["uid"])
                state["uid"] += 1
            for j in range(nsub):
                emit_outproj_post(c, yT, ykeys, wout, gpost, xr[:, j, :], xkeys[j], j, bA, ouids[j], part=2)

        def l1_pre(xr, xkeys, nsub, ub):
            pre_t(nsub, ub, pre_s(xr, xkeys, nsub, gpre1))

        def l1_tile(xr, xkeys, nsub, full, halo, bb, ub, pre_done, hook1, hook2):
            Tt = nsub * 128
            uT = uTs[ub]
            if not pre_done:
                l1_pre(xr, xkeys, nsub, ub)
            ukeys = ["uT%d_%d" % (ub, j) for j in range(nsub)]
            groups = list(range(12)) if (full and not halo) else ([4, 5, 6, 7, 8, 9, 10, 11] if full else [4, 5, 6, 7, 8, 9])
            def emit_dt_stats():
                pb, pk = l1banks[state["ocnt"] % 4]
                state["ocnt"] += 1
                for kc in range(KD):
                    S.add("pe", lambda e, kc=kc, pb=pb, Tt=Tt: e.matmul(pb[0:NH, 0:Tt], lhsT=wdt[:, kc, :], rhs=uT[:, kc, 0:Tt],
                                                                      start=(kc == 0), stop=(kc == KD - 1)), reads=["wdt"] + ukeys, writes=[pk])
                S.add("act", lambda e, pb=pb, Tt=Tt: e.activation(out=dtT[0:NH, 0:Tt], in_=pb[0:NH, 0:Tt], func=AF.Exp, bias=dtb[0:NH, 0:1], scale=1.0),
                      reads=["dtb"], writes=[pk, "dtT"])
                S.add("act", lambda e, Tt=Tt: e.activation(out=dtT[0:NH, 0:Tt], in_=dtT[0:NH, 0:Tt], func=AF.Ln, bias=onec[0:NH, 0:1], scale=1.0),
                      reads=["dtT", "onec"], writes=["dtT"])
                S.add("dve", lambda e, Tt=Tt: e.tensor_scalar(out=adtT[0:NH, 0:Tt], in0=dtT[0:NH, 0:Tt], scalar1=acol[0:NH, 0:1], scalar2=None, op0=ALU.mult),
                      reads=["dtT", "acol"], writes=["adtT"])

            def emit_stats():
                for j in range(nsub):
                    js = slice(j * 128, (j + 1) * 128)
                    csj, hlj = cs[j], hl[j]
                    ckey, hkey = "cs%d" % j, "hl%d" % j
                    S.add("pe", lambda e, js=js: e.transpose(pS[:, 0:NH], dtT[0:NH, js], identf[0:NH, 0:NH]), reads=["dtT", "identf"], writes=["pC"])
                    S.add("pe", lambda e, js=js: e.transpose(pS[:, NH:2 * NH], adtT[0:NH, js], identf[0:NH, 0:NH]), reads=["adtT", "identf"], writes=["pC"])
                    S.add("act", lambda e, csj=csj: e.activation(out=csj[:, 0:2, :], in_=pS[:, 0:2 * NH].rearrange("p (a h) -> p a h", a=2), func=AF.Copy),
                          writes=["pC", ckey])
                    S.add("pe", lambda e, csj=csj: e.matmul(pS[:, 2 * NH:3 * NH], lhsT=trif[:], rhs=csj[:, 1, :], start=True, stop=True),
                          reads=[ckey, "trif"], writes=["pC"])
                    S.add("pe", lambda e, csj=csj: e.matmul(pS[:, 3 * NH:4 * NH], lhsT=onesf[:], rhs=csj[:, 1, :], start=True, stop=True),
                          reads=[ckey, "onesf"], writes=["pC"])
                    S.add("act", lambda e, csj=csj: e.activation(out=csj[:, 2, :], in_=pS[:, 2 * NH:3 * NH], func=AF.Copy), writes=["pC", ckey])
                    S.add("act", lambda e, csj=csj: e.activation(out=csj[:, 3, :], in_=pS[:, 2 * NH:3 * NH], func=AF.Copy, scale=-1.0), writes=["pC", ckey])
                    S.add("dve", lambda e, csj=csj: e.tensor_tensor(out=csj[:, 7, :], in0=pS[:, 3 * NH:4 * NH], in1=csj[:, 2, :], op=ALU.subtract),
                          writes=["pC", ckey])
                    S.add("act", lambda e, csj=csj: e.activation(out=csj[:, 6, :], in_=pS[:, 3 * NH:4 * NH], func=AF.Exp), writes=["pC", ckey])
                    S.add("act", lambda e, csj=csj: e.activation(out=csj[:, 7, :], in_=csj[:, 7, :], func=AF.Exp), reads=[ckey], writes=[ckey])
                    S.add("dve", lambda e, csj=csj: e.tensor_tensor(out=csj[:, 5, :], in0=csj[:, 7, :], in1=csj[:, 0, :], op=ALU.mult), reads=[ckey], writes=[ckey])
                    if full:
                        S.add("dve", lambda e, csj=csj, hlj=hlj: e.tensor_copy(out=hlj[:, 0, :], in_=csj[:, 1, :]), reads=[ckey], writes=[hkey])
                        S.add("dve", lambda e, csj=csj, hlj=hlj: e.tensor_tensor(out=hlj[:, 1, :], in0=csj[:, 1, :], in1=hlj[:, 0, :], op=ALU.subtract),
                              reads=[ckey, hkey], writes=[hkey])
                        S.add("dve", lambda e, csj=csj, hlj=hlj: e.tensor_copy(out=hlj[:, 2, :], in_=csj[:, 3, :]), reads=[ckey, hkey], writes=[hkey])
                        S.add("dve", lambda e, csj=csj, hlj=hlj: e.tensor_tensor(out=hlj[:, 3, :], in0=csj[:, 3, :], in1=hlj[:, 2, :], op=ALU.subtract),
                              reads=[ckey, hkey], writes=[hkey])
                    else:
                        S.add("dve", lambda e, csj=csj: e.tensor_tensor(out=dcum, in0=dcum, in1=csj[:, 6, :], op=ALU.mult), reads=[ckey, "dcum"], writes=["dcum"])
                if not full:
                    last = cs[nsub - 1]
                    S.add("dve", lambda e, last=last: e.tensor_copy(out=last[:, 4, :], in_=last[:, 6, :]), reads=["cs%d" % (nsub - 1)], writes=["cs%d" % (nsub - 1)])
                    for j in range(nsub - 2, -1, -1):
                        csj = cs[j]
                        S.add("dve", lambda e, csj=csj, last=last: e.tensor_tensor(out=csj[:, 5, :], in0=csj[:, 5, :], in1=last[:, 4, :], op=ALU.mult),
                              reads=["cs%d" % j, "cs%d" % (nsub - 1)], writes=["cs%d" % j])
                        S.add("dve", lambda e, csj=csj, last=last: e.tensor_tensor(out=last[:, 4, :], in0=last[:, 4, :], in1=csj[:, 6, :], op=ALU.mult),
                              reads=["cs%d" % j, "cs%d" % (nsub - 1)], writes=["cs%d" % (nsub - 1)])

            state["ocnt"] = 0
            if not halo:
                emit_dt_stats()
            if not full:
                hook1()
            stats_after = None if halo else (3 if full else 4)
            hook1_after = 7 if full else None
            cast_ready(1)
            pend = []
            for gi in groups:
                ws = state["wcnt"] % 3
                state["wcnt"] += 1
                wbv = wbuf[ws][:].rearrange("p (kc n) -> p kc n", kc=KD)
                S.add("sp", lambda e, ws=ws, gi=gi: e.dma_start(out=wbuf[ws][:], in_=w1s[gi]), reads=["w1s%d" % gi], writes=["wb%d" % ws], dma="wb%d" % ws)
                for q in range(4):
                    o = gi * 4 + q
                    pb, pk = l1banks[state["ocnt"] % 4]
                    state["ocnt"] += 1
                    for kc in range(KD):
                        S.add("pe", lambda e, wbv=wbv, q=q, kc=kc, pb=pb, Tt=Tt: e.matmul(
                            pb[:, 0:Tt], lhsT=wbv[:, kc, q * 128:(q + 1) * 128], rhs=uT[:, kc, 0:Tt],
                            start=(kc == 0), stop=(kc == KD - 1)), reads=["wb%d" % ws] + ukeys, writes=[pk])
                    if o < 16:
                        S.add("act", lambda e, o=o, pb=pb, Tt=Tt: e.activation(out=yT[:, o, 0:Tt], in_=pb[:, 0:Tt], func=AF.Silu),
                              writes=[pk, "yT%d" % o])
                        continue
                    ci = o - 16
                    ck = "carry%d" % ci
                    if halo:
                        S.add("dve", lambda e, ci=ci, pb=pb, Tt=Tt: e.tensor_copy(out=carry1[:, ci, :], in_=pb[:, Tt - 3:Tt]), writes=[pk, ck])
                        continue
                    sl = ci % 4
                    xpb, ab = xp[sl], acc[sl]
                    S.add("act", lambda e, xpb=xpb, pb=pb, Tt=Tt: e.activation(out=xpb[:, 3:3 + Tt], in_=pb[:, 0:Tt], func=AF.Copy),
                          writes=[pk, "xpb%d" % sl])
                    S.add("pool", lambda e, xpb=xpb, ci=ci: e.tensor_copy(out=xpb[:, 0:3], in_=carry1[:, ci, :]), reads=[ck], writes=["xph%d" % sl])
                    S.add("act", lambda e, pb=pb, ab=ab, ci=ci, Tt=Tt: e.activation(
                        out=ab[:, 0:Tt], in_=pb[:, 0:Tt], func=AF.Copy, scale=cw1[:, ci * 4 + 3:ci * 4 + 4]),
                        reads=["cw1"], writes=[pk, "acc%d" % sl])
                    for tap in (0, 1, 2):
                        S.add("dve", lambda e, xpb=xpb, ab=ab, ci=ci, tap=tap, Tt=Tt: e.scalar_tensor_tensor(
                            out=ab[:, 0:Tt], in0=xpb[:, tap:tap + Tt], scalar=cw1[:, ci * 4 + tap:ci * 4 + tap + 1],
                            in1=ab[:, 0:Tt], op0=ALU.mult, op1=ALU.add),
                            reads=["xph%d" % sl, "xpb%d" % sl, "cw1", "acc%d" % sl], writes=["acc%d" % sl])
                    S.add("pool", lambda e, xpb=xpb, ci=ci, Tt=Tt: e.tensor_copy(out=carry1[:, ci, :], in_=xpb[:, Tt:Tt + 3]),
                          reads=["xpb%d" % sl], writes=[ck])
                    if ci < 16:
                        dst, dk = xT[:, ci, 0:Tt], "xT%d" % ci
                    elif ci < 24:
                        dst, dk = BT[:, ci - 16, 0:Tt], "BT%d" % (ci - 16)
                    else:
                        dst, dk = CT[:, ci - 24, 0:Tt], "CT%d" % (ci - 24)
                    pend.append(lambda dst=dst, ab=ab, ci=ci, Tt=Tt, sl=sl, dk=dk: S.add(
                        "act", lambda e: e.activation(out=dst, in_=ab[:, 0:Tt], func=AF.Silu, bias=cb1[:, ci:ci + 1], scale=1.0),
                        reads=["acc%d" % sl, "cb1"], writes=[dk]))
                    if len(pend) > 2:
                        pend.pop(0)()
                if gi == stats_after:
                    emit_stats()
                if gi == hook1_after:
                    hook1()
            while pend:
                pend.pop(0)()
            if halo:
                hook2()
                return
            assert 2 * T <= 512
            if full:
                its = [(g, j) for gp in range(0, NG, 2) for j in range(nsub) for g in (gp, gp + 1)]
            else:
                its = [(g, j) for g in range(NG) for j in range(nsub)]

            def stage_a(n):
                g, j = its[n]
                js = slice(j * 128, (j + 1) * 128)
                csj, hlj = cs[j], hl[j]
                ckey, hkey = "cs%d" % j, "hl%d" % j
                sl = state["gj"] % (2 if full else 4)
                state["gj"] += 1
                xBb, xsb, xdb = xB[sl], xs[sl], xdt[sl % 2]
                for q in range(2):
                    S.add("pe", lambda e, q=q, g=g, js=js: e.transpose(tp[:, q * 128:(q + 1) * 128], xT[:, 2 * g + q, js], identb[:]),
                          reads=["xT%d" % (2 * g + q), "identb"], writes=["tp"])
                S.add("pe", lambda e, g=g, js=js: e.transpose(tp[:, 256:384], BT[:, g, js], identb[:]), reads=["BT%d" % g, "identb"], writes=["tp"])
                S.add("act", lambda e, xBb=xBb: e.activation(out=xBb[:], in_=tp[:, 0:384], func=AF.Copy), writes=["tp", "xB%d" % sl])
                S.add("dve", lambda e, xBb=xBb, xsb=xsb, csj=csj, g=g: e.tensor_tensor(
                    out=xsb[:].rearrange("p (h q) -> p h q", h=4), in0=xBb[:, 0:256].rearrange("p (h q) -> p h q", h=4),
                    in1=csj[:, 5, 4 * g:4 * g + 4].unsqueeze(2).to_broadcast([128, 4, 64]), op=ALU.mult),
                    reads=["xB%d" % sl, ckey], writes=["xs%d" % sl])
                if not full:
                    return sl
                S.add("dve", lambda e, xBb=xBb, xdb=xdb, csj=csj, g=g: e.tensor_tensor(
                    out=xdb[:].rearrange("p (h q) -> p h q", h=4), in0=xBb[:, 0:256].rearrange("p (h q) -> p h q", h=4),
                    in1=csj[:, 0, 4 * g:4 * g + 4].unsqueeze(2).to_broadcast([128, 4, 64]), op=ALU.mult),
                    reads=["xB%d" % sl, ckey], writes=["xdt%d" % sl])
                S.add("pe", lambda e, g=g, js=js: e.matmul(pC[:, 0:128], lhsT=BT[:, g, js], rhs=CT[:, g, js], start=True, stop=True),
                      reads=["BT%d" % g, "CT%d" % g], writes=["pC"])
                for i in range(4):
                    h = 4 * g + i
                    reg = slice(i * 128, (i + 1) * 128)
                    S.add("pe", lambda e, hlj=hlj, h=h, reg=reg: e.matmul(pA[:, reg], lhsT=hlj[:, 0, h:h + 1].to_broadcast([128, 128]), rhs=trib[:],
                                                                        start=True, stop=False), reads=[hkey, "trib"], writes=[pAk])
                    S.add("pe", lambda e, hlj=hlj, h=h, reg=reg: e.matmul(pA[:, reg], lhsT=hlj[:, 1, h:h + 1].to_broadcast([128, 128]), rhs=trib[:],
                                                                        start=False, stop=False), reads=[hkey, "trib"], writes=[pAk])
                    S.add("pe", lambda e, hlj=hlj, h=h, reg=reg: e.matmul(pA[:, reg], lhsT=identb[:], rhs=hlj[:, 2, h:h + 1].to_broadcast([128, 128]),
                                                                        start=False, stop=False), reads=[hkey, "identb"], writes=[pAk])
                    S.add("pe", lambda e, hlj=hlj, h=h, reg=reg: e.matmul(pA[:, reg], lhsT=identb[:], rhs=hlj[:, 3, h:h + 1].to_broadcast([128, 128]),
                                                                        start=False, stop=True), reads=[hkey, "identb"], writes=[pAk])
                    S.add("pe", lambda e, hlj=hlj, h=h, reg=reg: e.matmul(pA2[:, reg], lhsT=hlj[:, 0, h:h + 1].to_broadcast([128, 128]), rhs=trib[:],
                                                                        start=True, stop=False), reads=[hkey, "trib"], writes=[pA2k])
                    S.add("pe", lambda e, hlj=hlj, h=h, reg=reg: e.matmul(pA2[:, reg], lhsT=hlj[:, 1, h:h + 1].to_broadcast([128, 128]), rhs=trib[:],
                                                                        start=False, stop=True), reads=[hkey, "trib"], writes=[pA2k])
                S.add("act", lambda e, sl=sl: e.activation(out=dec4[sl][:], in_=pA[:, 0:512], func=AF.Exp), writes=[pAk, "dec%d" % sl])
                S.add("act", lambda e, sl=sl: e.activation(out=ebc4[sl][:], in_=pA2[:, 0:512], func=AF.Exp), writes=[pA2k, "ebc%d" % sl])
                S.add("dve", lambda e, sl=sl: e.tensor_tensor(out=m14[sl][:].rearrange("p (h l) -> p h l", h=4),
                                                              in0=dec4[sl][:].rearrange("p (h l) -> p h l", h=4),
                                                              in1=pC[:, 0:128].unsqueeze(1).to_broadcast([128, 4, 128]), op=ALU.mult),
                      reads=["dec%d" % sl], writes=["pC", "m1%d" % sl])
                S.add("pool", lambda e, sl=sl: e.affine_select(out=MT4[sl][:].rearrange("p (h l) -> p h l", h=4),
                                                               in_=m14[sl][:].rearrange("p (h l) -> p h l", h=4),
                                                               pattern=[[0, 4], [1, 128]], compare_op=ALU.is_ge, fill=0.0, base=0, channel_multiplier=-1),
                      reads=["m1%d" % sl], writes=["MT%d" % sl])
                S.add("pool", lambda e, sl=sl, g=g, js=js: e.tensor_tensor(out=CsT4[sl][:].rearrange("p (h l) -> p h l", h=4),
                                                                          in0=ebc4[sl][:].rearrange("p (h l) -> p h l", h=4),
                                                                          in1=CT[:, g, js].unsqueeze(1).to_broadcast([128, 4, 128]), op=ALU.mult),
                      reads=["ebc%d" % sl, "CT%d" % g], writes=["CsT%d" % sl])
                return sl

            def stage_b(n, sl):
                g, j = its[n]
                js = slice(j * 128, (j + 1) * 128)
                csj = cs[j]
                ckey = "cs%d" % j
                hk, hbk = "hT%d" % g, "hTb%d" % g
                xBb, xsb, xdb = xB[sl], xs[sl], xdt[sl % 2]
                if full:
                    for i in range(4):
                        h = 4 * g + i
                        cc = i // 2
                        reg = slice(i * 128, (i + 1) * 128)
                        yo = pY[g % 2][(i % 2) * 64:(i % 2 + 1) * 64, cc * T + j * 128:cc * T + (j + 1) * 128]
                        yk = pYk[g % 2]
                        S.add("pe", lambda e, yo=yo, xdb=xdb, sl=sl, i=i, reg=reg: e.matmul(yo, lhsT=xdb[:, i * 64:(i + 1) * 64], rhs=MT4[sl][:, reg], start=True, stop=False),
                              reads=["xdt%d" % sl, "MT%d" % sl], writes=[yk])
                        S.add("pe", lambda e, yo=yo, xBb=xBb, h=h, i=i: e.matmul(yo, lhsT=xBb[:, i * 64:(i + 1) * 64], rhs=DIm[:, h, :], start=False, stop=False),
                              reads=["xB%d" % sl, "DIm"], writes=[yk])
                        S.add("pe", lambda e, yo=yo, h=h, sl=sl, reg=reg: e.matmul(yo, lhsT=hTb[:, h * 64:(h + 1) * 64], rhs=CsT4[sl][:, reg], start=False, stop=True),
                              reads=[hbk, "CsT%d" % sl], writes=[yk])
                hg = hT[:, g * 256:(g + 1) * 256]
                if full:
                    S.add("pe", lambda e, xBb=xBb, xsb=xsb: e.matmul(pC[:, 128:384], lhsT=xBb[:, 256:384], rhs=xsb[:], start=True, stop=True),
                          reads=["xB%d" % sl, "xs%d" % sl], writes=["pC"])
                    S.add("dve", lambda e, hg=hg, csj=csj, g=g: e.tensor_tensor(
                        out=hg.rearrange("p (h q) -> p h q", h=4), in0=hg.rearrange("p (h q) -> p h q", h=4),
                        in1=csj[:, 6, 4 * g:4 * g + 4].unsqueeze(2).to_broadcast([128, 4, 64]), op=ALU.mult), reads=[ckey], writes=[hk])
                    S.add("dve", lambda e, hg=hg: e.tensor_tensor(out=hg, in0=hg, in1=pC[:, 128:384], op=ALU.add), writes=["pC", hk])
                else:
                    S.add("pe", lambda e, xBb=xBb, xsb=xsb, j=j: e.matmul(pC[:, 128:384], lhsT=xBb[:, 256:384], rhs=xsb[:], start=(j == 0), stop=(j == nsub - 1)),
                          reads=["xB%d" % sl, "xs%d" % sl], writes=["pC"])
                    if j == nsub - 1:
                        last = cs[nsub - 1]
                        S.add("pool", lambda e, hg=hg, last=last, g=g: e.tensor_tensor(
                            out=hg.rearrange("p (h q) -> p h q", h=4), in0=hg.rearrange("p (h q) -> p h q", h=4),
                            in1=last[:, 4, 4 * g:4 * g + 4].unsqueeze(2).to_broadcast([128, 4, 64]), op=ALU.mult), reads=["cs%d" % (nsub - 1)], writes=[hk])
                        S.add("dve", lambda e, hg=hg: e.tensor_tensor(out=hg, in0=hg, in1=pC[:, 128:384], op=ALU.add), writes=["pC", hk])
                if full:
                    S.add("act", lambda e, hg=hg, g=g: e.activation(out=hTb[:, g * 256:(g + 1) * 256], in_=hg, func=AF.Copy), reads=[hk], writes=[hbk])
                if full and j == nsub - 1:
                    for q in range(2):
                        cc = 2 * g + q
                        ygb = tmp[(2 * g + q) % 4]
                        ygk = "tmp%d" % ((2 * g + q) % 4)
                        sqb = sq[q]
                        S.add("dve", lambda e, ygb=ygb, cc=cc, q=q, g=g, Tt=Tt: e.tensor_tensor(out=ygb[:, 0:Tt], in0=pY[g % 2][:, q * T:q * T + Tt], in1=yT[:, cc, 0:Tt], op=ALU.mult),
                              reads=["yT%d" % cc], writes=[pYk[g % 2], ygk])
                        S.add("act", lambda e, ygb=ygb, sqb=sqb, Tt=Tt: e.activation(out=sqb[:, 0:Tt], in_=ygb[:, 0:Tt], func=AF.Square),
                              reads=[ygk], writes=["sq%d" % q])
                        S.add("pe", lambda e, sqb=sqb, q=q, Tt=Tt: e.matmul(pin[0][:, 0:Tt], lhsT=onesb[:], rhs=sqb[:, 0:Tt], start=(q == 0), stop=(q == 1)),
                              reads=["sq%d" % q, "onesb"], writes=["pin0"])
                    gl, gr = tmp[4], tmp[5]
                    S.add("act", lambda e, Tt=Tt: e.activation(out=gl[:, 0:Tt], in_=pin[0][:, 0:Tt], func=AF.Ln, scale=1.0 / 256, bias=eps[:, 0:1]),
                          reads=["eps"], writes=["pin0", "tmp4"])
                    S.add("act", lambda e, Tt=Tt: e.activation(out=gr[:, 0:Tt], in_=gl[:, 0:Tt], func=AF.Exp, scale=-0.5), reads=["tmp4"], writes=["tmp5"])
                    for q in range(2):
                        cc = 2 * g + q
                        ygb = tmp[(2 * g + q) % 4]
                        ygk = "tmp%d" % ((2 * g + q) % 4)
                        S.add("dve", lambda e, ygb=ygb, cc=cc, Tt=Tt: e.scalar_tensor_tensor(
                            out=yT[:, cc, 0:Tt], in0=ygb[:, 0:Tt], scalar=nw[:, cc:cc + 1], in1=gr[:, 0:Tt], op0=ALU.mult, op1=ALU.mult),
                            reads=[ygk, "nw", "tmp5"], writes=["yT%d" % cc])

            look = 1 if full else 3
            slots = {}
            for n in range(min(look, len(its))):
                slots[n] = stage_a(n)
            for n in range(len(its)):
                if n + look < len(its):
                    slots[n + look] = stage_a(n + look)
                stage_b(n, slots[n])
            if full:
                zkeys = ["yT%d" % cc for cc in range(NCH)]
                ouids = []
                for j in range(nsub):
                    emit_outproj_post(c, yT, zkeys, wout, gpost, xr[:, j, :], xkeys[j], j, bb, state["uid"], part=1)
                    ouids.append(state["uid"])
                    state["uid"] += 1
                hook2()
                for j in range(nsub):
                    emit_outproj_post(c, yT, zkeys, wout, gpost, xr[:, j, :], xkeys[j], j, bb, ouids[j], part=2)
            else:
                hook2()

        tiles = [(0, 1)] + [(HALO + i * T, NSUB) for i in range(NTOK // T)]
        def load_x(ti):
            tok0, nsub = tiles[ti]
            Tt = nsub * 128
            xr = xres[ti % 2]
            xkeys = ["xres%d_%d" % (ti % 2, j) for j in range(NSUB)]
            S.add("sp", lambda e: e.dma_start(out=xr[:, 0:nsub, :], in_=x[tok0:tok0 + Tt, :].rearrange("(j p) d -> p j d", p=128)),
                  writes=xkeys[0:nsub], dma="xld%d" % (ti % 2))
            return xr, xkeys, nsub

        nxt = load_x(0)
        l0_pre(nxt[0], nxt[1], nxt[2], 0)
        for ti, (tok0, nsub) in enumerate(tiles):
            Tt = nsub * 128
            xr, xkeys = nxt[0], nxt[1]
            l0_tile(xr, xkeys, nsub, 0)
            S.add("pool", lambda e, xr=xr, tok0=tok0, nsub=nsub, Tt=Tt: e.dma_start(
                out=h1s[tok0:tok0 + Tt, :].rearrange("(j p) d -> p j d", p=128), in_=xr[:, 0:nsub, :]),
                reads=xkeys[0:nsub], writes=["h1s%d" % ti], dma="xst%d" % (ti % 2))

            nxt_box = [None]

            def hook_a1(ti=ti, nxt_box=nxt_box):
                if ti + 1 < len(tiles):
                    r = load_x(ti + 1)
                    nxt_box[0] = r + (pre_s(r[0], r[1], r[2], gpre0),)

            def hook_a2(ti=ti, nxt_box=nxt_box):
                if ti + 1 < len(tiles):
                    r = nxt_box[0]
                    pre_t(r[2], 0, r[3])
            l1_tile(xr, xkeys, nsub, False, ti == 0, b, 1, False, hook_a1, hook_a2)
            nxt = nxt_box[0]
        S.add("sp", lambda e: e.dma_start(out=sbn[:, :], in_=hT[:]), reads=hkeys, writes=["sbn"], dma="sbn")
        S.add("sp", lambda e: e.dma_start(out=dbn[:, :], in_=dcumt[:]), reads=["dcum"], writes=["dbn"], dma="dbn")
        RG = [[0, 1, 2, 3], [4, 5, 6, 7]]
        S.add("pool", lambda e: e.collective_compute("AllGather", ALU.bypass, replica_groups=RG, ins=[sbn[:, :]], outs=[sgt[:, :]]),
              reads=["sbn"], writes=["sgt"], dma="cc", inc=1)
        S.add("pool", lambda e: e.collective_compute("AllGather", ALU.bypass, replica_groups=RG, ins=[dbn[:, :]], outs=[dgt[:, :]]),
              reads=["dbn"], writes=["dgt"], dma="cc2", inc=1)
        S.barrier()
        S.add("sp", lambda e: e.dma_start(out=gpost[:], in_=post1.partition_broadcast(128)), writes=["gpost"], dma="gpost")
        load_wout(c, wout, w1_out)
        S.add("pool", lambda e: e.memset(carry1[:], 0.0), writes=["carry%d" % i for i in range(32)])
        S.add("dve", lambda e: e.memset(hT[:], 0.0), writes=hkeys)
        S.add("sp", lambda e: e.dma_start(out=dstage[:], in_=dgt.rearrange("(k p) h -> p k h", p=128)), reads=["dgt"], writes=["dstage"], dma="dstage")
        for k in range(3):
            S.add("sp", lambda e, k=k: e.dma_start(out=sstage[:], in_=sgt[k * 128:(k + 1) * 128, :]), reads=["sgt"], writes=["sstage"], dma="sstage")
            S.add("dve", lambda e, k=k: e.tensor_scalar(out=fac[:], in0=dstage[:, k, 0:NH], scalar1=cm[:, k:k + 1],
                                                        scalar2=cm[:, 4 + k:5 + k], op0=ALU.mult, op1=ALU.add),
                  reads=["dstage", "cm"], writes=["fac"])
            S.add("dve", lambda e: e.tensor_tensor(out=hT[:].rearrange("p (h q) -> p h q", h=NH),
                                                   in0=hT[:].rearrange("p (h q) -> p h q", h=NH),
                                                   in1=fac[:].unsqueeze(2).to_broadcast([128, NH, 64]), op=ALU.mult),
                  reads=["fac"], writes=hkeys)
            S.add("dve", lambda e, k=k: e.scalar_tensor_tensor(out=hT[:], in0=sstage[:, 0:DI], scalar=cm[:, k:k + 1], in1=hT[:],
                                                               op0=ALU.mult, op1=ALU.add),
                  reads=["sstage", "cm"], writes=hkeys)
        S.add("act", lambda e: e.activation(out=hTb[:], in_=hT[:], func=AF.Copy), reads=hkeys, writes=["hTb%d" % g for g in range(NG)])
        def load_h(ti):
            tok0, nsub = tiles[ti]
            Tt = nsub * 128
            xr = xres[ti % 2]
            xkeys = ["xres%d_%d" % (ti % 2, j) for j in range(NSUB)]
            S.add("sp", lambda e: e.dma_start(out=xr[:, 0:nsub, :], in_=h1s[tok0:tok0 + Tt, :].rearrange("(j p) d -> p j d", p=128)),
                  reads=["h1s%d" % ti], writes=xkeys[0:nsub], dma="xld%d" % (ti % 2))
            return xr, xkeys, nsub

        nxt = load_h(0)
        l1_pre(nxt[0], nxt[1], nxt[2], 0)
        for ti, (tok0, nsub) in enumerate(tiles):
            Tt = nsub * 128
            xr, xkeys = nxt[0], nxt[1]
            nxt_box = [None]

            def hook_b1(ti=ti, nxt_box=nxt_box):
                if ti + 1 < len(tiles):
                    r = load_h(ti + 1)
                    nxt_box[0] = r + (pre_s(r[0], r[1], r[2], gpre1),)

            def hook_b2(ti=ti, nxt_box=nxt_box):
                if ti + 1 < len(tiles):
                    r = nxt_box[0]
                    pre_t(r[2], (ti + 1) % 2, r[3])
            l1_tile(xr, xkeys, nsub, True, ti == 0, bB, ti % 2, True, hook_b1, hook_b2)
            nxt = nxt_box[0]
            if ti > 0:
                S.add("pool", lambda e, xr=xr, tok0=tok0, nsub=nsub, Tt=Tt: e.dma_start(
                    out=out[tok0 - HALO:tok0 - HALO + Tt, :].rearrange("(j p) d -> p j d", p=128), in_=xr[:, 0:nsub, :]),
                    reads=xkeys[0:nsub], writes=["outd"], dma="xst%d" % (ti % 2))
        S.add("sp", lambda e: e.nop(), reads=["outd"])
        S.emit(nc, st)
    return nc


TF = 256
_PROGS = {}


def fused_maps(x, pre_norm, post_norm, sc_w_in, sc_conv_w, sc_w_out, ssd_w_in, ssd_conv_w, ssd_conv_b,
               ssd_dt_bias, ssd_a_log, ssd_d_skip, ssd_norm, ssd_w_out, NTOK):
    B, L, _ = x.shape
    cpb = L // NTOK
    cwh = np.ascontiguousarray(sc_conv_w[0].reshape(3, NCH, 128).transpose(2, 1, 0)).reshape(128, NCH * 3)
    pm = l1_param_maps(ssd_w_in, ssd_conv_w, ssd_conv_b, ssd_dt_bias, ssd_a_log)
    ex = l1_full_extra(post_norm, ssd_d_skip, ssd_norm, ssd_w_out)
    shared = {"pre0": np.ascontiguousarray(pre_norm[0]), "post0": np.ascontiguousarray(post_norm[0]),
              "w0_in": np.ascontiguousarray(sc_w_in[0]), "cw0h": cwh, "w0_out": np.ascontiguousarray(sc_w_out[0]),
              "pre1": np.ascontiguousarray(pre_norm[1]), "post1": ex["post"], "w1_in": pm["w_in"], "cw1h": pm["cw1h"],
              "cb1h": pm["cb1h"], "dtbh": pm["dtbh"], "alogh": pm["alogh"], "dskh": ex["dskh"], "nwh": ex["nwh"],
              "w1_out": ex["w_out"]}
    maps = []
    for core in range(B * cpb):
        bi, ci = divmod(core, cpb)
        cm = np.zeros((128, 8), np.float32)
        for k in range(4):
            cm[:, k] = 1.0 if k < ci else 0.0
        cm[:, 4:8] = 1.0 - cm[:, 0:4]
        m = dict(shared)
        m["x"] = core_tokens(x[bi], ci * NTOK, NTOK)
        m["cmask"] = cm
        maps.append(m)
    return maps


def kernel(x, pre_norm, post_norm, sc_w_in, sc_conv_w, sc_w_out, ssd_w_in, ssd_conv_w, ssd_conv_b,
           ssd_dt_bias, ssd_a_log, ssd_d_skip, ssd_norm, ssd_w_out):
    f = lambda a: np.ascontiguousarray(np.asarray(a), dtype=np.float32)
    args = [f(a) for a in (x, pre_norm, post_norm, sc_w_in, sc_conv_w, sc_w_out, ssd_w_in, ssd_conv_w, ssd_conv_b,
                           ssd_dt_bias, ssd_a_log, ssd_d_skip, ssd_norm, ssd_w_out)]
    B, L, _ = args[0].shape
    ncores = 8
    NTOK = B * L // ncores
    if ("fused", NTOK) not in _PROGS:
        _PROGS[("fused", NTOK)] = build_fused(NTOK, TF)
    nc = _PROGS[("fused", NTOK)]
    res = run_bass_kernel_spmd(nc, fused_maps(*args, NTOK), core_ids=list(range(ncores)))
    out = np.concatenate([r["out"] for r in res.results], 0).reshape(B, L, D)
    return out.astype(np.float32)
```

```python
from contextlib import ExitStack
import numpy as np
import concourse.bass as bass
import concourse.mybir as mybir
from concourse.bass_utils import run_bass_kernel_spmd

F32 = mybir.dt.float32
BF16 = mybir.dt.bfloat16
AF = mybir.ActivationFunctionType
ALU = mybir.AluOpType

D = 1024
KD = 8
DI = 2048
NCH = 16
NH = 32
NG = 8
HALO = 128
EPS = 1e-6
L1IN = 6176


class Sched:
    ENGS = ("pe", "act", "dve", "pool", "sp")

    def __init__(self):
        self.ops = {e: [] for e in self.ENGS}
        self.last_w = {}
        self.readers = {}
        self.dma_cnt = {}
        self.dma_inc = {}

    def barrier(self):
        toks = set()
        for e in self.ENGS:
            for i in range(len(self.ops[e]) - 1, -1, -1):
                if self.ops[e][i]["dma"] is None:
                    toks.add(("eng", e, i))
                    break
        for k, cnt in self.dma_cnt.items():
            toks.add(("dma", k, cnt))
        self.pending = {e: set(toks) for e in self.ENGS}

    def add(self, eng, fn, reads=(), writes=(), dma=None, inc=16):
        deps = set()
        if getattr(self, "pending", None) and self.pending.get(eng):
            deps |= self.pending[eng]
            self.pending[eng] = set()
        for r in reads:
            t = self.last_w.get(r)
            if t is not None:
                deps.add(t)
        for w in writes:
            t = self.last_w.get(w)
            if t is not None:
                deps.add(t)
            for t in self.readers.get(w, ()):
                deps.add(t)
        idx = len(self.ops[eng])
        if dma is not None:
            c = self.dma_cnt.get(dma, 0) + 1
            self.dma_cnt[dma] = c
            tok = ("dma", dma, c)
        else:
            tok = ("eng", eng, idx)
        if eng == "pe":
            deps = {d for d in deps if not (d[0] == "eng" and d[1] == "pe")}
        deps.discard(tok)
        if dma is not None:
            self.dma_inc[dma] = inc
        self.ops[eng].append(dict(fn=fn, deps=deps, tok=tok, dma=dma))
        for r in reads:
            lst = self.readers.setdefault(r, [])
            if tok[0] == "eng":
                lst[:] = [t for t in lst if not (t[0] == "eng" and t[1] == eng)]
            lst.append(tok)
        for w in writes:
            self.last_w[w] = tok
            self.readers[w] = []
        return tok

    def emit(self, nc, stack):
        sig = {e: [False] * len(self.ops[e]) for e in self.ENGS}
        for e in self.ENGS:
            for op in self.ops[e]:
                for d in op["deps"]:
                    if d[0] == "eng":
                        sig[d[1]][d[2]] = True
        cum = {}
        for e in self.ENGS:
            c = 0
            arr = []
            for s in sig[e]:
                if s:
                    c += 1
                arr.append(c)
            cum[e] = arr
        esem = {e: stack.enter_context(nc.semaphore("s_" + e)) for e in self.ENGS}
        dsem = {k: stack.enter_context(nc.semaphore("d_" + str(k))) for k in self.dma_cnt}
        block = stack.enter_context(nc.Block())

        def run(e, engobj):
            known = {}
            for i, op in enumerate(self.ops[e]):
                need = {}
                for d in op["deps"]:
                    if d[0] == "eng":
                        s, v = esem[d[1]], cum[d[1]][d[2]]
                    else:
                        s, v = dsem[d[1]], self.dma_inc[d[1]] * d[2]
                    if need.get(s.name, (None, 0))[1] < v:
                        need[s.name] = (s, v)
                for nm, (s, v) in need.items():
                    if known.get(nm, 0) >= v:
                        continue
                    engobj.wait_ge(s, v)
                    known[nm] = v
                ins = op["fn"](engobj)
                if op["dma"] is not None:
                    ins.then_inc(dsem[op["dma"]], self.dma_inc[op["dma"]])
                elif sig[e][i]:
                    ins.then_inc(esem[e], 1)

        block.tensor(lambda eng: run("pe", eng))
        block.scalar(lambda eng: run("act", eng))
        block.vector(lambda eng: run("dve", eng))
        block.gpsimd(lambda eng: run("pool", eng))
        block.sync(lambda eng: run("sp", eng))


class Ctx:
    def __init__(self, nc, st):
        self.nc = nc
        self.st = st
        self.S = Sched()

    def sb(self, name, shape, dt):
        return self.st.enter_context(self.nc.sbuf_tensor(name, shape, dt))

    def ps(self, name, shape, dt):
        return self.st.enter_context(self.nc.psum_tensor(name, shape, dt))

    def din(self, name, shape, dt=F32):
        return self.nc.dram_tensor(name, list(shape), dt, kind="ExternalInput").ap()

    def dout(self, name, shape, dt=F32):
        return self.nc.dram_tensor(name, list(shape), dt, kind="ExternalOutput").ap()


def make_ident(c, t, key):
    S = c.S
    S.add("pool", lambda e: e.memset(t[:], 0.0), writes=[key])
    S.add("pool", lambda e: e.affine_select(out=t[:], in_=t[:], pattern=[[-1, 128]],
                                            compare_op=ALU.not_equal, fill=1.0, base=0,
                                            channel_multiplier=1), reads=[key], writes=[key])


def make_tri(c, t, key):
    S = c.S
    S.add("pool", lambda e: e.memset(t[:], 1.0), writes=[key])
    S.add("pool", lambda e: e.affine_select(out=t[:], in_=t[:], pattern=[[1, 128]],
                                            compare_op=ALU.is_ge, fill=0.0, base=0,
                                            channel_multiplier=-1), reads=[key], writes=[key])


def emit_prenorm_stats(c, xres_j, xkey, gpre, bufs, uid):
    S = c.S
    stat, junk, utok, eps = bufs["stat"], bufs["junk"], bufs["utok"], bufs["eps"]
    sl = uid % 4
    st_ = stat[sl]
    sk = "stat%d" % sl
    ut = utok[uid % 2]
    uk = "utok%d" % (uid % 2)
    S.add("act", lambda e: e.activation(out=junk[:], in_=xres_j, func=AF.Square, accum_out=st_[:, 0:1]),
          reads=[xkey], writes=["junk", sk])
    S.add("act", lambda e: e.activation(out=st_[:, 1:2], in_=st_[:, 0:1], func=AF.Ln, scale=1.0 / D,
                                        bias=eps[:, 0:1]), reads=[sk, "eps"], writes=[sk])
    S.add("act", lambda e: e.activation(out=st_[:, 2:3], in_=st_[:, 1:2], func=AF.Exp, scale=-0.5),
          reads=[sk], writes=[sk])
    S.add("dve", lambda e: e.scalar_tensor_tensor(out=ut[:], in0=xres_j, scalar=st_[:, 2:3], in1=gpre[:],
                                                  op0=ALU.mult, op1=ALU.mult),
          reads=[xkey, sk, "gpre"], writes=[uk])


def emit_prenorm_T(c, uT, j, bufs, uid, ukp="uT"):
    S = c.S
    utok, tp, identb = bufs["utok"], bufs["tp"], bufs["identb"]
    ut = utok[uid % 2]
    uk = "utok%d" % (uid % 2)
    for kc in range(KD):
        S.add("pe", lambda e, kc=kc: e.transpose(tp[:, kc * 128:(kc + 1) * 128], ut[:, kc * 128:(kc + 1) * 128], identb[:]),
              reads=[uk, "identb"], writes=["tp"])
    S.add("act", lambda e: e.activation(out=uT[:, :, j * 128:(j + 1) * 128],
                                        in_=tp[:, 0:1024].rearrange("p (k t) -> p k t", k=KD), func=AF.Copy),
          reads=[uk], writes=["tp", "%s%d" % (ukp, j)])


def emit_prenorm(c, xres_j, xkey, gpre, uT, j, bufs, uid, ukp="uT"):
    emit_prenorm_stats(c, xres_j, xkey, gpre, bufs, uid)
    emit_prenorm_T(c, uT, j, bufs, uid, ukp)


def emit_outproj_post(c, lhs_buf, lhs_keys, wout, gpost, xres_j, xkey, j, bufs, uid, part=3):
    S = c.S
    stat, junk, eps, mres = bufs["stat"], bufs["junk"], bufs["eps"], bufs["mres"]
    po = bufs["po"][2 * (j % 2):2 * (j % 2) + 2] if len(bufs["po"]) >= 4 else bufs["po"]
    pok = bufs["pokeys"][2 * (j % 2):2 * (j % 2) + 2] if len(bufs["po"]) >= 4 else bufs["pokeys"]
    for hf in range(2 if (part & 1) else 0):
        for cc in range(NCH):
            S.add("pe", lambda e, hf=hf, cc=cc: e.matmul(po[hf][:], lhsT=lhs_buf[:, cc, j * 128:(j + 1) * 128],
                                                         rhs=wout[:, cc, hf * 512:(hf + 1) * 512],
                                                         start=(cc == 0), stop=(cc == NCH - 1)),
                  reads=[lhs_keys[cc], "wout"], writes=[pok[hf]])
    if not (part & 2):
        return
    sl = uid % 4
    st_ = stat[sl]
    sk = "stat%d" % sl
    for hf in range(2):
        S.add("act", lambda e, hf=hf: e.activation(out=junk[:, 0:512], in_=po[hf][:], func=AF.Square,
                                                   accum_out=st_[:, 3 + hf:4 + hf]),
              writes=[pok[hf], "junk", sk])
    S.add("dve", lambda e: e.tensor_tensor(out=st_[:, 5:6], in0=st_[:, 3:4], in1=st_[:, 4:5], op=ALU.add),
          reads=[sk], writes=[sk])
    S.add("act", lambda e: e.activation(out=st_[:, 6:7], in_=st_[:, 5:6], func=AF.Ln, scale=1.0 / D,
                                        bias=eps[:, 0:1]), reads=[sk, "eps"], writes=[sk])
    S.add("act", lambda e: e.activation(out=st_[:, 7:8], in_=st_[:, 6:7], func=AF.Exp, scale=-0.5),
          reads=[sk], writes=[sk])
    for hf in range(2):
        S.add("dve", lambda e, hf=hf: e.scalar_tensor_tensor(out=mres[:, hf * 512:(hf + 1) * 512], in0=po[hf][:],
                                                             scalar=st_[:, 7:8], in1=gpost[:, hf * 512:(hf + 1) * 512],
                                                             op0=ALU.mult, op1=ALU.mult),
              reads=[sk, "gpost"], writes=[pok[hf], "mres%d" % hf])
    S.add("pool", lambda e: e.tensor_tensor(out=xres_j, in0=xres_j, in1=mres[:], op=ALU.add),
          reads=["mres0", "mres1", xkey], writes=[xkey])


def common_bufs(c, po=None, pokeys=None):
    b = {}
    b["stat"] = [c.sb("stat%d" % i, [128, 8], F32) for i in range(4)]
    b["junk"] = c.sb("junk", [128, 1024], BF16)
    b["utok"] = [c.sb("utok%d" % i, [128, 1024], BF16) for i in range(2)]
    b["tp"] = c.ps("tp", [128, 1024], BF16)
    b["identb"] = c.sb("identb", [128, 128], BF16)
    b["eps"] = c.sb("eps", [128, 1], F32)
    b["mres"] = c.sb("mres", [128, 1024], F32)
    if po is None:
        po = [c.ps("po%d" % i, [128, 512], F32) for i in range(2)]
        pokeys = ["po0", "po1"]
    b["po"] = po
    b["pokeys"] = pokeys
    make_ident(c, b["identb"], "identb")
    c.S.add("dve", lambda e: e.memset(b["eps"][:], EPS), writes=["eps"])
    return b


def load_wout(c, wout_sb, w_out_dram):
    src = w_out_dram.rearrange("(c p) n -> p c n", p=128)
    for q in range(4):
        c.S.add("pool", lambda e, q=q: e.dma_start(out=wout_sb[:, q * 4:(q + 1) * 4, :], in_=src[:, q * 4:(q + 1) * 4, :]),
                writes=["wout"], dma="wout%d" % q)


def core_tokens(xflat_b, start, NTOK):
    out = np.zeros((HALO + NTOK, xflat_b.shape[1]), np.float32)
    lo = max(0, start - HALO)
    out[HALO - (start - lo):] = xflat_b[lo:start + NTOK]
    return out


def l1_param_maps(ssd_w_in, ssd_conv_w, ssd_conv_b, ssd_dt_bias, ssd_a_log):
    cw1h = np.ascontiguousarray(ssd_conv_w[0].reshape(4, 32, 128).transpose(2, 1, 0)).reshape(128, 128)
    cb1h = np.ascontiguousarray(ssd_conv_b[0].reshape(32, 128).T)
    dtbh = np.zeros((128, 1), np.float32)
    dtbh[:NH, 0] = ssd_dt_bias[0]
    alogh = np.zeros((128, 1), np.float32)
    alogh[:NH, 0] = ssd_a_log[0]
    return {"w_in": np.ascontiguousarray(ssd_w_in[0]), "cw1h": cw1h, "cb1h": cb1h, "dtbh": dtbh, "alogh": alogh}


def l1_full_extra(post_norm, ssd_d_skip, ssd_norm, ssd_w_out):
    dskh = np.ascontiguousarray(np.broadcast_to(ssd_d_skip[0][None, :], (128, NH))).astype(np.float32)
    nwh = np.ascontiguousarray(ssd_norm[0].reshape(NCH, 128).T)
    return {"post": np.ascontiguousarray(post_norm[1]), "dskh": dskh, "nwh": nwh,
            "w_out": np.ascontiguousarray(ssd_w_out[0])}


def build_fused(NTOK, T):
    nc = bass.Bass("TRN2", target_bir_lowering=False)
    with ExitStack() as st:
        c = Ctx(nc, st)
        S = c.S
        NSUB = T // 128
        x = c.din("x", [HALO + NTOK, D])
        pre0 = c.din("pre0", [D])
        post0 = c.din("post0", [D])
        w0_in = c.din("w0_in", [D, 4 * DI])
        cw0h = c.din("cw0h", [128, NCH * 3])
        w0_out = c.din("w0_out", [DI, D])
        pre1 = c.din("pre1", [D])
        post1 = c.din("post1", [D])
        w1_in = c.din("w1_in", [D, L1IN])
        cw1h = c.din("cw1h", [128, 32 * 4])
        cb1h = c.din("cb1h", [128, 32])
        dtbh = c.din("dtbh", [128, 1])
        alogh = c.din("alogh", [128, 1])
        dskh = c.din("dskh", [128, NH])
        nwh = c.din("nwh", [128, NCH])
        w1_out = c.din("w1_out", [DI, D])
        cmask = c.din("cmask", [128, 8])
        out = c.dout("out", [NTOK, D])
        w0s = nc.dram_tensor("w0s", [NCH, 128, KD * 4 * 128], BF16, kind="Internal").ap()
        w1s = nc.dram_tensor("w1s", [12, 128, KD * 512], BF16, kind="Internal").ap()
        h1s = nc.dram_tensor("h1s", [HALO + NTOK, D], F32, kind="Internal").ap()
        sbn = nc.dram_tensor("sbn", [128, DI], F32, kind="Internal").ap()
        sgt = nc.dram_tensor("sgt", [4 * 128, DI], F32, kind="Internal").ap()
        dbn = nc.dram_tensor("dbn", [128, 64], F32, kind="Internal").ap()
        dgt = nc.dram_tensor("dgt", [4 * 128, 64], F32, kind="Internal").ap()

        pin = [c.ps("pin%d" % i, [128, 512], F32) for i in range(4)]
        b = common_bufs(c)
        po = b["po"]
        identb, eps, tp = b["identb"], b["eps"], b["tp"]
        pC = c.ps("pC", [128, 512], F32)
        pS = pC[:, 384:512]
        pA = po[0]
        pAk = "po0"
        pA2 = po[1]
        pA2k = "po1"
        pY = [pin[2], pin[3]]
        pYk = ["pin2", "pin3"]
        bB = dict(b)
        bB["po"] = [pin[2], pin[3], po[0], po[1]]
        bB["pokeys"] = ["pin2", "pin3", "po0", "po1"]
        bA = dict(b)
        bA["po"] = [po[0], po[1], pin[0], pin[1]]
        bA["pokeys"] = ["po0", "po1", "pin0", "pin1"]

        gpre0 = c.sb("gpre0", [128, D], F32)
        gpre1 = c.sb("gpre1", [128, D], F32)
        gpost = c.sb("gpost", [128, D], F32)
        cw0 = c.sb("cw0", [128, NCH * 3], F32)
        carry0 = c.sb("carry0", [128, NCH, 2], F32)
        wout = c.sb("wout", [128, NCH, D], BF16)
        xres = [c.sb("xres%d" % i, [128, NSUB, D], F32) for i in range(2)]
        uTs = [c.sb("uT%d" % i, [128, KD, T], BF16) for i in range(2)]
        wbuf = [c.sb("wbuf%d" % i, [128, KD * 512], BF16) for i in range(3)]
        yT = c.sb("yT", [128, NCH, T], BF16)
        tmp = [c.sb("tmp%d" % i, [128, T], F32) for i in range(8)]
        cv = [c.sb("cv%d" % i, [128, T + 2], F32) for i in range(2)]
        cw1 = c.sb("cw1", [128, 32 * 4], F32)
        cb1 = c.sb("cb1", [128, 32], F32)
        dtb = c.sb("dtb", [128, 1], F32)
        acol = c.sb("acol", [128, 1], F32)
        onec = c.sb("onec", [128, 1], F32)
        identf = c.sb("identf", [128, 128], F32)
        trif = c.sb("trif", [128, 128], F32)
        onesf = c.sb("onesf", [128, 128], F32)
        wdt = c.sb("wdt", [128, KD, NH], BF16)
        carry1 = c.sb("carry1", [128, 32, 3], F32)
        xT = c.sb("xT", [128, NCH, T], BF16)
        BT = c.sb("BT", [128, NG, T], BF16)
        CT = c.sb("CT", [128, NG, T], BF16)
        xp = [c.sb("xp%d" % i, [128, T + 3], F32) for i in range(4)]
        acc = [c.sb("acc%d" % i, [128, T], F32) for i in range(4)]
        l1banks = [(pin[0], "pin0"), (pin[1], "pin1"), (po[0], "po0"), (po[1], "po1")]
        dtT = c.sb("dtT", [128, T], F32)
        adtT = c.sb("adtT", [128, T], F32)
        cs = [c.sb("cs%d" % i, [128, 8, NH], F32) for i in range(NSUB)]
        hl = [c.sb("hl%d" % i, [128, 4, NH], BF16) for i in range(NSUB)]
        xdt = [c.sb("xdt%d" % i, [128, 256], BF16) for i in range(2)]
        xB = [c.sb("xB%d" % i, [128, 384], BF16) for i in range(2)]
        xs = [c.sb("xs%d" % i, [128, 256], BF16) for i in range(2)]
        hT = c.sb("hT", [128, DI], F32)
        dcumt = c.sb("dcumt", [128, 64], F32)
        dcum = dcumt[:, 0:NH]
        dstage = c.sb("dstage", [128, 4, 64], F32)
        dsk = c.sb("dsk", [128, NH], F32)
        nw = c.sb("nw", [128, NCH], F32)
        cm = c.sb("cm", [128, 8], F32)
        trib = c.sb("trib", [128, 128], BF16)
        onesb = c.sb("onesb", [128, 128], BF16)
        DIm = c.sb("DIm", [128, NH, 128], BF16)
        hTb = c.sb("hTb", [128, DI], BF16)
        dec4 = [c.sb("dec%d" % i, [128, 512], F32) for i in range(2)]
        ebc4 = [c.sb("ebc%d" % i, [128, 512], F32) for i in range(2)]
        m14 = [c.sb("m1%d" % i, [128, 512], F32) for i in range(2)]
        MT4 = [c.sb("MT%d" % i, [128, 512], BF16) for i in range(2)]
        CsT4 = [c.sb("CsT%d" % i, [128, 512], BF16) for i in range(2)]
        sstage = c.sb("sstage", [128, DI], F32)
        fac = c.sb("fac", [128, NH], F32)
        sq = [c.sb("sq%d" % i, [128, T], BF16) for i in range(2)]
        hkeys = ["hT%d" % g for g in range(NG)]

        w0src = w0_in.rearrange("(kc p) n -> p kc n", p=128)
        for ch in range(NCH):
            dst = w0s[ch].rearrange("p (kc w n) -> p kc w n", kc=KD, w=4)
            for which in range(4):
                col0 = which * DI + ch * 128
                S.add("pool", lambda e, dst=dst, which=which, col0=col0: e.dma_start(out=dst[:, :, which, :], in_=w0src[:, :, col0:col0 + 128]),
                      writes=["w0sraw%d_%d" % (ch, which)], dma="cast%d" % ((ch * 4 + which) % 4))
        w1src = w1_in.rearrange("(kc p) n -> p kc n", p=128)
        for t_, src_, k_ in ((gpre0, pre0, "gpre0"), (gpre1, pre1, "gpre1"), (gpost, post0, "gpost")):
            S.add("sp", lambda e, t_=t_, src_=src_: e.dma_start(out=t_[:], in_=src_.partition_broadcast(128)), writes=[k_], dma=k_)
        for t_, src_, k_ in ((cw0, cw0h, "cw0"), (cw1, cw1h, "cw1"), (cb1, cb1h, "cb1"), (dtb, dtbh, "dtb"), (acol, alogh, "acol"),
                             (dsk, dskh, "dsk"), (nw, nwh, "nw"), (cm, cmask, "cm")):
            S.add("sp", lambda e, t_=t_, src_=src_: e.dma_start(out=t_[:], in_=src_[:, :]), writes=[k_], dma=k_)
        S.add("pool", lambda e: e.dma_start(out=wdt[:], in_=w1src[:, :, L1IN - NH:L1IN]), writes=["wdt"], dma="wdt")
        load_wout(c, wout, w0_out)
        S.add("act", lambda e: e.activation(out=acol[:], in_=acol[:], func=AF.Exp), reads=["acol"], writes=["acol"])
        S.add("dve", lambda e: e.tensor_scalar(out=acol[:], in0=acol[:], scalar1=-1.0, scalar2=None, op0=ALU.mult),
              reads=["acol"], writes=["acol"])
        S.add("dve", lambda e: e.memset(onec[:], 1.0), writes=["onec"])
        S.add("pool", lambda e: e.memset(onesf[:], 1.0), writes=["onesf"])
        S.add("pool", lambda e: e.memset(onesb[:], 1.0), writes=["onesb"])
        S.add("pool", lambda e: e.memset(carry0[:], 0.0), writes=["carry0_%d" % i for i in range(NCH)])
        S.add("pool", lambda e: e.memset(carry1[:], 0.0), writes=["carry%d" % i for i in range(32)])
        make_ident(c, identf, "identf")
        make_tri(c, trif, "trif")
        make_tri(c, trib, "trib")
        for h in range(NH):
            S.add("dve", lambda e, h=h: e.tensor_scalar(out=DIm[:, h, :], in0=identf[:], scalar1=dsk[:, h:h + 1], scalar2=None, op0=ALU.mult),
                  reads=["identf", "dsk"], writes=["DIm"])
        S.add("dve", lambda e: e.memset(hT[:], 0.0), writes=hkeys)
        S.add("dve", lambda e: e.memset(dcumt[:], 1.0), writes=["dcum"])
        for gi in range(12):
            S.add("pool", lambda e, gi=gi: e.dma_start(out=w1s[gi].rearrange("p (kc n) -> p kc n", kc=KD), in_=w1src[:, :, gi * 512:(gi + 1) * 512]),
                  writes=["w1sraw%d" % gi], dma="castb%d" % (gi % 4))

        state = dict(uid=0, wcnt=0, ocnt=0, gj=0, w0ready=False, w1ready=False)

        def cast_ready(which_layer):
            if which_layer == 0 and not state["w0ready"]:
                state["w0ready"] = True
                S.add("sp", lambda e: e.nop(), reads=["w0sraw%d_%d" % (ch, w) for ch in range(NCH) for w in range(4)],
                      writes=["w0s%d" % ch for ch in range(NCH)])
            if which_layer == 1 and not state["w1ready"]:
                state["w1ready"] = True
                S.add("sp", lambda e: e.nop(), reads=["w1sraw%d" % gi for gi in range(12)] + ["w0sraw%d_%d" % (ch, w) for ch in range(NCH) for w in range(4)],
                      writes=["w1s%d" % gi for gi in range(12)])

        def pre_s(xr, xkeys, nsub, gp):
            uids = []
            for j in range(nsub):
                emit_prenorm_stats(c, xr[:, j, :], xkeys[j], gp, b, state["uid"])
                uids.append(state["uid"])
                state["uid"] += 1
            return uids

        def pre_t(nsub, ub, uids):
            for j in range(nsub):
                emit_prenorm_T(c, uTs[ub], j, b, uids[j], ukp="uT%d_" % ub)

        def l0_pre(xr, xkeys, nsub, ub):
            pre_t(nsub, ub, pre_s(xr, xkeys, nsub, gpre0))

        def l0_tile(xr, xkeys, nsub, ub):
            Tt = nsub * 128
            uT = uTs[ub]
            ukeys = ["uT%d_%d" % (ub, j) for j in range(nsub)]
            cast_ready(0)
            for ch in range(NCH):
                ws = state["wcnt"] % 3
                state["wcnt"] += 1
                wbv = wbuf[ws][:].rearrange("p (kc w n) -> p kc w n", kc=KD, w=4)
                S.add("sp", lambda e, ws=ws, ch=ch: e.dma_start(out=wbuf[ws][:], in_=w0s[ch]), reads=["w0s%d" % ch], writes=["wb%d" % ws], dma="wb%d" % ws)
                pr = ch % 2
                for which, bank in ((2, 0), (3, 1), (0, 2), (1, 3)):
                    for kc in range(KD):
                        S.add("pe", lambda e, wbv=wbv, which=which, bank=bank, kc=kc, Tt=Tt: e.matmul(
                            pin[bank][:, 0:Tt], lhsT=wbv[:, kc, which, :], rhs=uT[:, kc, 0:Tt], start=(kc == 0), stop=(kc == KD - 1)),
                            reads=["wb%d" % ws] + ukeys, writes=["pin%d" % bank])
                a, bq, cq, dq, cvb = tmp[pr], tmp[2 + pr], tmp[4 + pr], tmp[6 + pr], cv[pr]
                ka, kb, kc_, kd = "tmp%d" % pr, "tmp%d" % (2 + pr), "tmp%d" % (4 + pr), "tmp%d" % (6 + pr)
                ck = "carry0_%d" % ch
                S.add("act", lambda e, a=a, Tt=Tt: e.activation(out=a[:, 0:Tt], in_=pin[0][:, 0:Tt], func=AF.Copy), writes=["pin0", ka])
                S.add("pool", lambda e, cvb=cvb, ch=ch: e.tensor_copy(out=cvb[:, 0:2], in_=carry0[:, ch, :]), reads=[ck], writes=["cvh%d" % pr])
                S.add("dve", lambda e, cvb=cvb, a=a, Tt=Tt: e.tensor_tensor(out=cvb[:, 2:2 + Tt], in0=a[:, 0:Tt], in1=pin[1][:, 0:Tt], op=ALU.mult),
                      reads=[ka], writes=["pin1", "cvb%d" % pr])
                S.add("act", lambda e, bq=bq, Tt=Tt: e.activation(out=bq[:, 0:Tt], in_=pin[2][:, 0:Tt], func=AF.Silu), writes=["pin2", kb])
                S.add("dve", lambda e, cq=cq, bq=bq, Tt=Tt: e.tensor_tensor(out=cq[:, 0:Tt], in0=bq[:, 0:Tt], in1=pin[3][:, 0:Tt], op=ALU.mult),
                      reads=[kb], writes=["pin3", kc_])
                S.add("act", lambda e, dq=dq, cvb=cvb, ch=ch, Tt=Tt: e.activation(
                    out=dq[:, 0:Tt], in_=cvb[:, 0:Tt], func=AF.Copy, scale=cw0[:, ch * 3:ch * 3 + 1]),
                    reads=["cvh%d" % pr, "cvb%d" % pr, "cw0"], writes=[kd])
                for tap in (1, 2):
                    S.add("dve", lambda e, dq=dq, cvb=cvb, ch=ch, tap=tap, Tt=Tt: e.scalar_tensor_tensor(
                        out=dq[:, 0:Tt], in0=cvb[:, tap:tap + Tt], scalar=cw0[:, ch * 3 + tap:ch * 3 + tap + 1],
                        in1=dq[:, 0:Tt], op0=ALU.mult, op1=ALU.add),
                        reads=["cvh%d" % pr, "cvb%d" % pr, "cw0", kd], writes=[kd])
                S.add("pool", lambda e, cvb=cvb, ch=ch, Tt=Tt: e.tensor_copy(out=carry0[:, ch, :], in_=cvb[:, Tt:Tt + 2]),
                      reads=["cvb%d" % pr], writes=[ck])
                S.add("pool", lambda e, dq=dq, cq=cq, ch=ch, Tt=Tt: e.tensor_tensor(out=yT[:, ch, 0:Tt], in0=dq[:, 0:Tt], in1=cq[:, 0:Tt], op=ALU.mult),
                      reads=[kd, kc_], writes=["yT%d" % ch])
            ykeys = ["yT%d" % ch for ch in range(NCH)]
            ouids = []
            for j in range(nsub):
                emit_outproj_post(c, yT, ykeys, wout, gpost, xr[:, j, :], xkeys[j], j, bA, state["uid"], part=1)
                ouids.append(state["uid"])
                state["uid"] += 1
            for j in range(nsub):
                emit_outproj_post(c, yT, ykeys, wout, gpost, xr[:, j, :], xkeys[j], j, bA, ouids[j], part=2)

        def l1_pre(xr, xkeys, nsub, ub):
            pre_t(nsub, ub, pre_s(xr, xkeys, nsub, gpre1))

        def l1_tile(xr, xkeys, nsub, full, halo, bb, ub, pre_done, hook1, hook2):
            Tt = nsub * 128
            uT = uTs[ub]
            if not pre_done:
                l1_pre(xr, xkeys, nsub, ub)
            ukeys = ["uT%d_%d" % (ub, j) for j in range(nsub)]
            groups = list(range(12)) if (full and not halo) else ([4, 5, 6, 7, 8, 9, 10, 11] if full else [4, 5, 6, 7, 8, 9])
            def emit_dt_stats():
                pb, pk = l1banks[state["ocnt"] % 4]
                state["ocnt"] += 1
                for kc in range(KD):
                    S.add("pe", lambda e, kc=kc, pb=pb, Tt=Tt: e.matmul(pb[0:NH, 0:Tt], lhsT=wdt[:, kc, :], rhs=uT[:, kc, 0:Tt],
                                                                      start=(kc == 0), stop=(kc == KD - 1)), reads=["wdt"] + ukeys, writes=[pk])
                S.add("act", lambda e, pb=pb, Tt=Tt: e.activation(out=dtT[0:NH, 0:Tt], in_=pb[0:NH, 0:Tt], func=AF.Exp, bias=dtb[0:NH, 0:1], scale=1.0),
                      reads=["dtb"], writes=[pk, "dtT"])
                S.add("act", lambda e, Tt=Tt: e.activation(out=dtT[0:NH, 0:Tt], in_=dtT[0:NH, 0:Tt], func=AF.Ln, bias=onec[0:NH, 0:1], scale=1.0),
                      reads=["dtT", "onec"], writes=["dtT"])
                S.add("dve", lambda e, Tt=Tt: e.tensor_scalar(out=adtT[0:NH, 0:Tt], in0=dtT[0:NH, 0:Tt], scalar1=acol[0:NH, 0:1], scalar2=None, op0=ALU.mult),
                      reads=["dtT", "acol"], writes=["adtT"])

            def emit_stats():
                for j in range(nsub):
                    js = slice(j * 128, (j + 1) * 128)
                    csj, hlj = cs[j], hl[j]
                    ckey, hkey = "cs%d" % j, "hl%d" % j
                    S.add("pe", lambda e, js=js: e.transpose(pS[:, 0:NH], dtT[0:NH, js], identf[0:NH, 0:NH]), reads=["dtT", "identf"], writes=["pC"])
                    S.add("pe", lambda e, js=js: e.transpose(pS[:, NH:2 * NH], adtT[0:NH, js], identf[0:NH, 0:NH]), reads=["adtT", "identf"], writes=["pC"])
                    S.add("act", lambda e, csj=csj: e.activation(out=csj[:, 0:2, :], in_=pS[:, 0:2 * NH].rearrange("p (a h) -> p a h", a=2), func=AF.Copy),
                          writes=["pC", ckey])
                    S.add("pe", lambda e, csj=csj: e.matmul(pS[:, 2 * NH:3 * NH], lhsT=trif[:], rhs=csj[:, 1, :], start=True, stop=True),
                          reads=[ckey, "trif"], writes=["pC"])
                    S.add("pe", lambda e, csj=csj: e.matmul(pS[:, 3 * NH:4 * NH], lhsT=onesf[:], rhs=csj[:, 1, :], start=True, stop=True),
                          reads=[ckey, "onesf"], writes=["pC"])
                    S.add("act", lambda e, csj=csj: e.activation(out=csj[:, 2, :], in_=pS[:, 2 * NH:3 * NH], func=AF.Copy), writes=["pC", ckey])
                    S.add("act", lambda e, csj=csj: e.activation(out=csj[:, 3, :], in_=pS[:, 2 * NH:3 * NH], func=AF.Copy, scale=-1.0), writes=["pC", ckey])
                    S.add("dve", lambda e, csj=csj: e.tensor_tensor(out=csj[:, 7, :], in0=pS[:, 3 * NH:4 * NH], in1=csj[:, 2, :], op=ALU.subtract),
                          writes=["pC", ckey])
                    S.add("act", lambda e, csj=csj: e.activation(out=csj[:, 6, :], in_=pS[:, 3 * NH:4 * NH], func=AF.Exp), writes=["pC", ckey])
                    S.add("act", lambda e, csj=csj: e.activation(out=csj[:, 7, :], in_=csj[:, 7, :], func=AF.Exp), reads=[ckey], writes=[ckey])
                    S.add("dve", lambda e, csj=csj: e.tensor_tensor(out=csj[:, 5, :], in0=csj[:, 7, :], in1=csj[:, 0, :], op=ALU.mult), reads=[ckey], writes=[ckey])
                    if full:
                        S.add("dve", lambda e, csj=csj, hlj=hlj: e.tensor_copy(out=hlj[:, 0, :], in_=csj[:, 1, :]), reads=[ckey], writes=[hkey])
                        S.add("dve", lambda e, csj=csj, hlj=hlj: e.tensor_tensor(out=hlj[:, 1, :], in0=csj[:, 1, :], in1=hlj[:, 0, :], op=ALU.subtract),
                              reads=[ckey, hkey], writes=[hkey])
                        S.add("dve", lambda e, csj=csj, hlj=hlj: e.tensor_copy(out=hlj[:, 2, :], in_=csj[:, 3, :]), reads=[ckey, hkey], writes=[hkey])
                        S.add("dve", lambda e, csj=csj, hlj=hlj: e.tensor_tensor(out=hlj[:, 3, :], in0=csj[:, 3, :], in1=hlj[:, 2, :], op=ALU.subtract),
                              reads=[ckey, hkey], writes=[hkey])
                    else:
                        S.add("dve", lambda e, csj=csj: e.tensor_tensor(out=dcum, in0=dcum, in1=csj[:, 6, :], op=ALU.mult), reads=[ckey, "dcum"], writes=["dcum"])

            state["ocnt"] = 0
            if not halo:
                emit_dt_stats()
            if not full:
                hook1()
            stats_after = None if halo else (3 if full else 4)
            hook1_after = 7 if full else None
            cast_ready(1)
            pend = []
            for gi in groups:
                ws = state["wcnt"] % 3
                state["wcnt"] += 1
                wbv = wbuf[ws][:].rearrange("p (kc n) -> p kc n", kc=KD)
                S.add("sp", lambda e, ws=ws, gi=gi: e.dma_start(out=wbuf[ws][:], in_=w1s[gi]), reads=["w1s%d" % gi], writes=["wb%d" % ws], dma="wb%d" % ws)
                for q in range(4):
                    o = gi * 4 + q
                    pb, pk = l1banks[state["ocnt"] % 4]
                    state["ocnt"] += 1
                    for kc in range(KD):
                        S.add("pe", lambda e, wbv=wbv, q=q, kc=kc, pb=pb, Tt=Tt: e.matmul(
                            pb[:, 0:Tt], lhsT=wbv[:, kc, q * 128:(q + 1) * 128], rhs=uT[:, kc, 0:Tt],
                            start=(kc == 0), stop=(kc == KD - 1)), reads=["wb%d" % ws] + ukeys, writes=[pk])
                    if o < 16:
                        S.add("act", lambda e, o=o, pb=pb, Tt=Tt: e.activation(out=yT[:, o, 0:Tt], in_=pb[:, 0:Tt], func=AF.Silu),
                              writes=[pk, "yT%d" % o])
                        continue
                    ci = o - 16
                    ck = "carry%d" % ci
                    if halo:
                        S.add("dve", lambda e, ci=ci, pb=pb, Tt=Tt: e.tensor_copy(out=carry1[:, ci, :], in_=pb[:, Tt - 3:Tt]), writes=[pk, ck])
                        continue
                    sl = ci % 4
                    xpb, ab = xp[sl], acc[sl]
                    S.add("act", lambda e, xpb=xpb, pb=pb, Tt=Tt: e.activation(out=xpb[:, 3:3 + Tt], in_=pb[:, 0:Tt], func=AF.Copy),
                          writes=[pk, "xpb%d" % sl])
                    S.add("pool", lambda e, xpb=xpb, ci=ci: e.tensor_copy(out=xpb[:, 0:3], in_=carry1[:, ci, :]), reads=[ck], writes=["xph%d" % sl])
                    S.add("act", lambda e, pb=pb, ab=ab, ci=ci, Tt=Tt: e.activation(
                        out=ab[:, 0:Tt], in_=pb[:, 0:Tt], func=AF.Copy, scale=cw1[:, ci * 4 + 3:ci * 4 + 4]),
                        reads=["cw1"], writes=[pk, "acc%d" % sl])
                    for tap in (0, 1, 2):
                        S.add("dve", lambda e, xpb=xpb, ab=ab, ci=ci, tap=tap, Tt=Tt: e.scalar_tensor_tensor(
                            out=ab[:, 0:Tt], in0=xpb[:, tap:tap + Tt], scalar=cw1[:, ci * 4 + tap:ci * 4 + tap + 1],
                            in1=ab[:, 0:Tt], op0=ALU.mult, op1=ALU.add),
                            reads=["xph%d" % sl, "xpb%d" % sl, "cw1", "acc%d" % sl], writes=["acc%d" % sl])
                    S.add("pool", lambda e, xpb=xpb, ci=ci, Tt=Tt: e.tensor_copy(out=carry1[:, ci, :], in_=xpb[:, Tt:Tt + 3]),
                          reads=["xpb%d" % sl], writes=[ck])
                    if ci < 16:
                        dst, dk = xT[:, ci, 0:Tt], "xT%d" % ci
                    elif ci < 24:
                        dst, dk = BT[:, ci - 16, 0:Tt], "BT%d" % (ci - 16)
                    else:
                        dst, dk = CT[:, ci - 24, 0:Tt], "CT%d" % (ci - 24)
                    pend.append(lambda dst=dst, ab=ab, ci=ci, Tt=Tt, sl=sl, dk=dk: S.add(
                        "act", lambda e: e.activation(out=dst, in_=ab[:, 0:Tt], func=AF.Silu, bias=cb1[:, ci:ci + 1], scale=1.0),
                        reads=["acc%d" % sl, "cb1"], writes=[dk]))
                    if len(pend) > 2:
                        pend.pop(0)()
                if gi == stats_after:
                    emit_stats()
                if gi == hook1_after:
                    hook1()
            while pend:
                pend.pop(0)()
            if halo:
                hook2()
                return
            assert 2 * T <= 512
            its = [(g, j) for gp in range(0, NG, 2) for j in range(nsub) for g in (gp, gp + 1)]

            def stage_a(n):
                g, j = its[n]
                js = slice(j * 128, (j + 1) * 128)
                csj, hlj = cs[j], hl[j]
                ckey, hkey = "cs%d" % j, "hl%d" % j
                sl = state["gj"] % 2
                state["gj"] += 1
                xBb, xsb, xdb = xB[sl], xs[sl], xdt[sl]
                for q in range(2):
                    S.add("pe", lambda e, q=q, g=g, js=js: e.transpose(tp[:, q * 128:(q + 1) * 128], xT[:, 2 * g + q, js], identb[:]),
                          reads=["xT%d" % (2 * g + q), "identb"], writes=["tp"])
                S.add("pe", lambda e, g=g, js=js: e.transpose(tp[:, 256:384], BT[:, g, js], identb[:]), reads=["BT%d" % g, "identb"], writes=["tp"])
                S.add("act", lambda e, xBb=xBb: e.activation(out=xBb[:], in_=tp[:, 0:384], func=AF.Copy), writes=["tp", "xB%d" % sl])
                S.add("dve", lambda e, xBb=xBb, xsb=xsb, csj=csj, g=g: e.tensor_tensor(
                    out=xsb[:].rearrange("p (h q) -> p h q", h=4), in0=xBb[:, 0:256].rearrange("p (h q) -> p h q", h=4),
                    in1=csj[:, 5, 4 * g:4 * g + 4].unsqueeze(2).to_broadcast([128, 4, 64]), op=ALU.mult),
                    reads=["xB%d" % sl, ckey], writes=["xs%d" % sl])
                if not full:
                    return sl
                S.add("dve", lambda e, xBb=xBb, xdb=xdb, csj=csj, g=g: e.tensor_tensor(
                    out=xdb[:].rearrange("p (h q) -> p h q", h=4), in0=xBb[:, 0:256].rearrange("p (h q) -> p h q", h=4),
                    in1=csj[:, 0, 4 * g:4 * g + 4].unsqueeze(2).to_broadcast([128, 4, 64]), op=ALU.mult),
                    reads=["xB%d" % sl, ckey], writes=["xdt%d" % sl])
                S.add("pe", lambda e, g=g, js=js: e.matmul(pC[:, 0:128], lhsT=BT[:, g, js], rhs=CT[:, g, js], start=True, stop=True),
                      reads=["BT%d" % g, "CT%d" % g], writes=["pC"])
                for i in range(4):
                    h = 4 * g + i
                    reg = slice(i * 128, (i + 1) * 128)
                    S.add("pe", lambda e, hlj=hlj, h=h, reg=reg: e.matmul(pA[:, reg], lhsT=hlj[:, 0, h:h + 1].to_broadcast([128, 128]), rhs=trib[:],
                                                                        start=True, stop=False), reads=[hkey, "trib"], writes=[pAk])
                    S.add("pe", lambda e, hlj=hlj, h=h, reg=reg: e.matmul(pA[:, reg], lhsT=hlj[:, 1, h:h + 1].to_broadcast([128, 128]), rhs=trib[:],
                                                                        start=False, stop=False), reads=[hkey, "trib"], writes=[pAk])
                    S.add("pe", lambda e, hlj=hlj, h=h, reg=reg: e.matmul(pA[:, reg], lhsT=identb[:], rhs=hlj[:, 2, h:h + 1].to_broadcast([128, 128]),
                                                                        start=False, stop=False), reads=[hkey, "identb"], writes=[pAk])
                    S.add("pe", lambda e, hlj=hlj, h=h, reg=reg: e.matmul(pA[:, reg], lhsT=identb[:], rhs=hlj[:, 3, h:h + 1].to_broadcast([128, 128]),
                                                                        start=False, stop=True), reads=[hkey, "identb"], writes=[pAk])
                    S.add("pe", lambda e, hlj=hlj, h=h, reg=reg: e.matmul(pA2[:, reg], lhsT=hlj[:, 0, h:h + 1].to_broadcast([128, 128]), rhs=trib[:],
                                                                        start=True, stop=False), reads=[hkey, "trib"], writes=[pA2k])
                    S.add("pe", lambda e, hlj=hlj, h=h, reg=reg: e.matmul(pA2[:, reg], lhsT=hlj[:, 1, h:h + 1].to_broadcast([128, 128]), rhs=trib[:],
                                                                        start=False, stop=True), reads=[hkey, "trib"], writes=[pA2k])
                S.add("act", lambda e, sl=sl: e.activation(out=dec4[sl][:], in_=pA[:, 0:512], func=AF.Exp), writes=[pAk, "dec%d" % sl])
                S.add("act", lambda e, sl=sl: e.activation(out=ebc4[sl][:], in_=pA2[:, 0:512], func=AF.Exp), writes=[pA2k, "ebc%d" % sl])
                S.add("dve", lambda e, sl=sl: e.tensor_tensor(out=m14[sl][:].rearrange("p (h l) -> p h l", h=4),
                                                              in0=dec4[sl][:].rearrange("p (h l) -> p h l", h=4),
                                                              in1=pC[:, 0:128].unsqueeze(1).to_broadcast([128, 4, 128]), op=ALU.mult),
                      reads=["dec%d" % sl], writes=["pC", "m1%d" % sl])
                S.add("pool", lambda e, sl=sl: e.affine_select(out=MT4[sl][:].rearrange("p (h l) -> p h l", h=4),
                                                               in_=m14[sl][:].rearrange("p (h l) -> p h l", h=4),
                                                               pattern=[[0, 4], [1, 128]], compare_op=ALU.is_ge, fill=0.0, base=0, channel_multiplier=-1),
                      reads=["m1%d" % sl], writes=["MT%d" % sl])
                S.add("pool", lambda e, sl=sl, g=g, js=js: e.tensor_tensor(out=CsT4[sl][:].rearrange("p (h l) -> p h l", h=4),
                                                                          in0=ebc4[sl][:].rearrange("p (h l) -> p h l", h=4),
                                                                          in1=CT[:, g, js].unsqueeze(1).to_broadcast([128, 4, 128]), op=ALU.mult),
                      reads=["ebc%d" % sl, "CT%d" % g], writes=["CsT%d" % sl])
                return sl

            def stage_b(n, sl):
                g, j = its[n]
                js = slice(j * 128, (j + 1) * 128)
                csj = cs[j]
                ckey = "cs%d" % j
                hk, hbk = "hT%d" % g, "hTb%d" % g
                xBb, xsb, xdb = xB[sl], xs[sl], xdt[sl]
                if full:
                    for i in range(4):
                        h = 4 * g + i
                        cc = i // 2
                        reg = slice(i * 128, (i + 1) * 128)
                        yo = pY[g % 2][(i % 2) * 64:(i % 2 + 1) * 64, cc * T + j * 128:cc * T + (j + 1) * 128]
                        yk = pYk[g % 2]
                        S.add("pe", lambda e, yo=yo, xdb=xdb, sl=sl, i=i, reg=reg: e.matmul(yo, lhsT=xdb[:, i * 64:(i + 1) * 64], rhs=MT4[sl][:, reg], start=True, stop=False),
                              reads=["xdt%d" % sl, "MT%d" % sl], writes=[yk])
                        S.add("pe", lambda e, yo=yo, xBb=xBb, h=h, i=i: e.matmul(yo, lhsT=xBb[:, i * 64:(i + 1) * 64], rhs=DIm[:, h, :], start=False, stop=False),
                              reads=["xB%d" % sl, "DIm"], writes=[yk])
                        S.add("pe", lambda e, yo=yo, h=h, sl=sl, reg=reg: e.matmul(yo, lhsT=hTb[:, h * 64:(h + 1) * 64], rhs=CsT4[sl][:, reg], start=False, stop=True),
                              reads=[hbk, "CsT%d" % sl], writes=[yk])
                S.add("pe", lambda e, xBb=xBb, xsb=xsb: e.matmul(pC[:, 128:384], lhsT=xBb[:, 256:384], rhs=xsb[:], start=True, stop=True),
                      reads=["xB%d" % sl, "xs%d" % sl], writes=["pC"])
                hg = hT[:, g * 256:(g + 1) * 256]
                ueng = "dve" if full else "pool"
                S.add(ueng, lambda e, hg=hg, csj=csj, g=g: e.tensor_tensor(
                    out=hg.rearrange("p (h q) -> p h q", h=4), in0=hg.rearrange("p (h q) -> p h q", h=4),
                    in1=csj[:, 6, 4 * g:4 * g + 4].unsqueeze(2).to_broadcast([128, 4, 64]), op=ALU.mult), reads=[ckey], writes=[hk])
                S.add("dve", lambda e, hg=hg: e.tensor_tensor(out=hg, in0=hg, in1=pC[:, 128:384], op=ALU.add), writes=["pC", hk])
                if full:
                    S.add("act", lambda e, hg=hg, g=g: e.activation(out=hTb[:, g * 256:(g + 1) * 256], in_=hg, func=AF.Copy), reads=[hk], writes=[hbk])
                if full and j == nsub - 1:
                    for q in range(2):
                        cc = 2 * g + q
                        ygb = tmp[(2 * g + q) % 4]
                        ygk = "tmp%d" % ((2 * g + q) % 4)
                        sqb = sq[q]
                        S.add("dve", lambda e, ygb=ygb, cc=cc, q=q, g=g, Tt=Tt: e.tensor_tensor(out=ygb[:, 0:Tt], in0=pY[g % 2][:, q * T:q * T + Tt], in1=yT[:, cc, 0:Tt], op=ALU.mult),
                              reads=["yT%d" % cc], writes=[pYk[g % 2], ygk])
                        S.add("act", lambda e, ygb=ygb, sqb=sqb, Tt=Tt: e.activation(out=sqb[:, 0:Tt], in_=ygb[:, 0:Tt], func=AF.Square),
                              reads=[ygk], writes=["sq%d" % q])
                        S.add("pe", lambda e, sqb=sqb, q=q, Tt=Tt: e.matmul(pin[0][:, 0:Tt], lhsT=onesb[:], rhs=sqb[:, 0:Tt], start=(q == 0), stop=(q == 1)),
                              reads=["sq%d" % q, "onesb"], writes=["pin0"])
                    gl, gr = tmp[4], tmp[5]
                    S.add("act", lambda e, Tt=Tt: e.activation(out=gl[:, 0:Tt], in_=pin[0][:, 0:Tt], func=AF.Ln, scale=1.0 / 256, bias=eps[:, 0:1]),
                          reads=["eps"], writes=["pin0", "tmp4"])
                    S.add("act", lambda e, Tt=Tt: e.activation(out=gr[:, 0:Tt], in_=gl[:, 0:Tt], func=AF.Exp, scale=-0.5), reads=["tmp4"], writes=["tmp5"])
                    for q in range(2):
                        cc = 2 * g + q
                        ygb = tmp[(2 * g + q) % 4]
                        ygk = "tmp%d" % ((2 * g + q) % 4)
                        S.add("dve", lambda e, ygb=ygb, cc=cc, Tt=Tt: e.scalar_tensor_tensor(
                            out=yT[:, cc, 0:Tt], in0=ygb[:, 0:Tt], scalar=nw[:, cc:cc + 1], in1=gr[:, 0:Tt], op0=ALU.mult, op1=ALU.mult),
                            reads=[ygk, "nw", "tmp5"], writes=["yT%d" % cc])

            slots = {0: stage_a(0)}
            for n in range(len(its)):
                if n + 1 < len(its):
                    slots[n + 1] = stage_a(n + 1)
                stage_b(n, slots[n])
            if full:
                zkeys = ["yT%d" % cc for cc in range(NCH)]
                ouids = []
                for j in range(nsub):
                    emit_outproj_post(c, yT, zkeys, wout, gpost, xr[:, j, :], xkeys[j], j, bb, state["uid"], part=1)
                    ouids.append(state["uid"])
                    state["uid"] += 1
                hook2()
                for j in range(nsub):
                    emit_outproj_post(c, yT, zkeys, wout, gpost, xr[:, j, :], xkeys[j], j, bb, ouids[j], part=2)
            else:
                hook2()

        tiles = [(0, 1)] + [(HALO + i * T, NSUB) for i in range(NTOK // T)]
        def load_x(ti):
            tok0, nsub = tiles[ti]
            Tt = nsub * 128
            xr = xres[ti % 2]
            xkeys = ["xres%d_%d" % (ti % 2, j) for j in range(NSUB)]
            S.add("sp", lambda e: e.dma_start(out=xr[:, 0:nsub, :], in_=x[tok0:tok0 + Tt, :].rearrange("(j p) d -> p j d", p=128)),
                  writes=xkeys[0:nsub], dma="xld%d" % (ti % 2))
            return xr, xkeys, nsub

        nxt = load_x(0)
        l0_pre(nxt[0], nxt[1], nxt[2], 0)
        for ti, (tok0, nsub) in enumerate(tiles):
            Tt = nsub * 128
            xr, xkeys = nxt[0], nxt[1]
            l0_tile(xr, xkeys, nsub, 0)
            S.add("pool", lambda e, xr=xr, tok0=tok0, nsub=nsub, Tt=Tt: e.dma_start(
                out=h1s[tok0:tok0 + Tt, :].rearrange("(j p) d -> p j d", p=128), in_=xr[:, 0:nsub, :]),
                reads=xkeys[0:nsub], writes=["h1s%d" % ti], dma="xst%d" % (ti % 2))

            nxt_box = [None]

            def hook_a1(ti=ti, nxt_box=nxt_box):
                if ti + 1 < len(tiles):
                    r = load_x(ti + 1)
                    nxt_box[0] = r + (pre_s(r[0], r[1], r[2], gpre0),)

            def hook_a2(ti=ti, nxt_box=nxt_box):
                if ti + 1 < len(tiles):
                    r = nxt_box[0]
                    pre_t(r[2], 0, r[3])
            l1_tile(xr, xkeys, nsub, False, ti == 0, b, 1, False, hook_a1, hook_a2)
            nxt = nxt_box[0]
        S.add("sp", lambda e: e.dma_start(out=sbn[:, :], in_=hT[:]), reads=hkeys, writes=["sbn"], dma="sbn")
        S.add("sp", lambda e: e.dma_start(out=dbn[:, :], in_=dcumt[:]), reads=["dcum"], writes=["dbn"], dma="dbn")
        RG = [[0, 1, 2, 3], [4, 5, 6, 7]]
        S.add("pool", lambda e: e.collective_compute("AllGather", ALU.bypass, replica_groups=RG, ins=[sbn[:, :]], outs=[sgt[:, :]]),
              reads=["sbn"], writes=["sgt"], dma="cc", inc=1)
        S.add("pool", lambda e: e.collective_compute("AllGather", ALU.bypass, replica_groups=RG, ins=[dbn[:, :]], outs=[dgt[:, :]]),
              reads=["dbn"], writes=["dgt"], dma="cc2", inc=1)
        S.barrier()
        S.add("sp", lambda e: e.dma_start(out=gpost[:], in_=post1.partition_broadcast(128)), writes=["gpost"], dma="gpost")
        load_wout(c, wout, w1_out)
        S.add("pool", lambda e: e.memset(carry1[:], 0.0), writes=["carry%d" % i for i in range(32)])
        S.add("dve", lambda e: e.memset(hT[:], 0.0), writes=hkeys)
        S.add("sp", lambda e: e.dma_start(out=dstage[:], in_=dgt.rearrange("(k p) h -> p k h", p=128)), reads=["dgt"], writes=["dstage"], dma="dstage")
        for k in range(3):
            S.add("sp", lambda e, k=k: e.dma_start(out=sstage[:], in_=sgt[k * 128:(k + 1) * 128, :]), reads=["sgt"], writes=["sstage"], dma="sstage")
            S.add("dve", lambda e, k=k: e.tensor_scalar(out=fac[:], in0=dstage[:, k, 0:NH], scalar1=cm[:, k:k + 1],
                                                        scalar2=cm[:, 4 + k:5 + k], op0=ALU.mult, op1=ALU.add),
                  reads=["dstage", "cm"], writes=["fac"])
            S.add("dve", lambda e: e.tensor_tensor(out=hT[:].rearrange("p (h q) -> p h q", h=NH),
                                                   in0=hT[:].rearrange("p (h q) -> p h q", h=NH),
                                                   in1=fac[:].unsqueeze(2).to_broadcast([128, NH, 64]), op=ALU.mult),
                  reads=["fac"], writes=hkeys)
            S.add("dve", lambda e, k=k: e.scalar_tensor_tensor(out=hT[:], in0=sstage[:, 0:DI], scalar=cm[:, k:k + 1], in1=hT[:],
                                                               op0=ALU.mult, op1=ALU.add),
                  reads=["sstage", "cm"], writes=hkeys)
        S.add("act", lambda e: e.activation(out=hTb[:], in_=hT[:], func=AF.Copy), reads=hkeys, writes=["hTb%d" % g for g in range(NG)])
        def load_h(ti):
            tok0, nsub = tiles[ti]
            Tt = nsub * 128
            xr = xres[ti % 2]
            xkeys = ["xres%d_%d" % (ti % 2, j) for j in range(NSUB)]
            S.add("sp", lambda e: e.dma_start(out=xr[:, 0:nsub, :], in_=h1s[tok0:tok0 + Tt, :].rearrange("(j p) d -> p j d", p=128)),
                  reads=["h1s%d" % ti], writes=xkeys[0:nsub], dma="xld%d" % (ti % 2))
            return xr, xkeys, nsub

        nxt = load_h(0)
        l1_pre(nxt[0], nxt[1], nxt[2], 0)
        for ti, (tok0, nsub) in enumerate(tiles):
            Tt = nsub * 128
            xr, xkeys = nxt[0], nxt[1]
            nxt_box = [None]

            def hook_b1(ti=ti, nxt_box=nxt_box):
                if ti + 1 < len(tiles):
                    r = load_h(ti + 1)
                    nxt_box[0] = r + (pre_s(r[0], r[1], r[2], gpre1),)

            def hook_b2(ti=ti, nxt_box=nxt_box):
                if ti + 1 < len(tiles):
                    r = nxt_box[0]
                    pre_t(r[2], (ti + 1) % 2, r[3])
            l1_tile(xr, xkeys, nsub, True, ti == 0, bB, ti % 2, True, hook_b1, hook_b2)
            nxt = nxt_box[0]
            if ti > 0:
                S.add("pool", lambda e, xr=xr, tok0=tok0, nsub=nsub, Tt=Tt: e.dma_start(
                    out=out[tok0 - HALO:tok0 - HALO + Tt, :].rearrange("(j p) d -> p j d", p=128), in_=xr[:, 0:nsub, :]),
                    reads=xkeys[0:nsub], writes=["outd"], dma="xst%d" % (ti % 2))
        S.add("sp", lambda e: e.nop(), reads=["outd"])
        S.emit(nc, st)
    return nc


TF = 256
_PROGS = {}


def fused_maps(x, pre_norm, post_norm, sc_w_in, sc_conv_w, sc_w_out, ssd_w_in, ssd_conv_w, ssd_conv_b,
               ssd_dt_bias, ssd_a_log, ssd_d_skip, ssd_norm, ssd_w_out, NTOK):
    B, L, _ = x.shape
    cpb = L // NTOK
    cwh = np.ascontiguousarray(sc_conv_w[0].reshape(3, NCH, 128).transpose(2, 1, 0)).reshape(128, NCH * 3)
    pm = l1_param_maps(ssd_w_in, ssd_conv_w, ssd_conv_b, ssd_dt_bias, ssd_a_log)
    ex = l1_full_extra(post_norm, ssd_d_skip, ssd_norm, ssd_w_out)
    shared = {"pre0": np.ascontiguousarray(pre_norm[0]), "post0": np.ascontiguousarray(post_norm[0]),
              "w0_in": np.ascontiguousarray(sc_w_in[0]), "cw0h": cwh, "w0_out": np.ascontiguousarray(sc_w_out[0]),
              "pre1": np.ascontiguousarray(pre_norm[1]), "post1": ex["post"], "w1_in": pm["w_in"], "cw1h": pm["cw1h"],
              "cb1h": pm["cb1h"], "dtbh": pm["dtbh"], "alogh": pm["alogh"], "dskh": ex["dskh"], "nwh": ex["nwh"],
              "w1_out": ex["w_out"]}
    maps = []
    for core in range(B * cpb):
        bi, ci = divmod(core, cpb)
        cm = np.zeros((128, 8), np.float32)
        for k in range(4):
            cm[:, k] = 1.0 if k < ci else 0.0
        cm[:, 4:8] = 1.0 - cm[:, 0:4]
        m = dict(shared)
        m["x"] = core_tokens(x[bi], ci * NTOK, NTOK)
        m["cmask"] = cm
        maps.append(m)
    return maps


def kernel(x, pre_norm, post_norm, sc_w_in, sc_conv_w, sc_w_out, ssd_w_in, ssd_conv_w, ssd_conv_b,
           ssd_dt_bias, ssd_a_log, ssd_d_skip, ssd_norm, ssd_w_out):
    f = lambda a: np.ascontiguousarray(np.asarray(a), dtype=np.float32)
    args = [f(a) for a in (x, pre_norm, post_norm, sc_w_in, sc_conv_w, sc_w_out, ssd_w_in, ssd_conv_w, ssd_conv_b,
                           ssd_dt_bias, ssd_a_log, ssd_d_skip, ssd_norm, ssd_w_out)]
    B, L, _ = args[0].shape
    ncores = 8
    NTOK = B * L // ncores
    if ("fused", NTOK) not in _PROGS:
        _PROGS[("fused", NTOK)] = build_fused(NTOK, TF)
    nc = _PROGS[("fused", NTOK)]
    res = run_bass_kernel_spmd(nc, fused_maps(*args, NTOK), core_ids=list(range(ncores)))
    out = np.concatenate([r["out"] for r in res.results], 0).reshape(B, L, D)
    return out.astype(np.float32)
```
